# Optimizing a Trainium2 kernel written in Bass

```python
import math
import jax, jax.numpy as jnp
from jax import lax
import numpy as np

D_MODEL = 1024
BATCH = 8
SEQ = 4096
DEPTH = 4

N_MIXERS = 3
D_FF = 2816
EPS = 1e-6
D_RNN = D_MODEL
RG_BLOCK = 128
RG_NBLOCKS = D_RNN // RG_BLOCK
RG_CONV = 4
RG_C = 8.0
FOX_HEADS = 16
FOX_HEAD_DIM = D_MODEL // FOX_HEADS
Q_BLOCK = 128
CV_KERNEL = 31

kernel_name = 'hybrid_rglru_fox_conformer_macaron'


def _rms_norm(x, g):
    xf = x.astype(jnp.float32)
    y = xf * lax.rsqrt(jnp.mean(xf * xf, axis=-1, keepdims=True) + EPS)
    return (y * g.astype(jnp.float32)).astype(x.dtype)


def _layer_norm(x, g, b):
    xf = x.astype(jnp.float32)
    mu = jnp.mean(xf, axis=-1, keepdims=True)
    var = jnp.mean(jnp.square(xf - mu), axis=-1, keepdims=True)
    y = (xf - mu) * lax.rsqrt(var + EPS)
    return (y * g.astype(jnp.float32) + b.astype(jnp.float32)).astype(x.dtype)


def _swiglu(x, w_in, w_out):
    gate, up = jnp.split(x @ w_in, 2, axis=-1)
    return (jax.nn.silu(gate) * up) @ w_out


def _causal_dw_conv(x, w, b):
    k_width, channels = w.shape
    y = lax.conv_general_dilated(
        x, w[:, None, :].astype(x.dtype), window_strides=(1,),
        padding=[(k_width - 1, 0)], dimension_numbers=('NWC', 'WIO', 'NWC'),
        feature_group_count=channels)
    return y + b


def _rglru_mixer(x, w_in, conv_w, conv_b, w_a, b_a, w_x, b_x, lam, w_out):
    bsz, seq, _ = x.shape
    gate, u = jnp.split(x @ w_in, 2, axis=-1)
    u = _causal_dw_conv(u, conv_w, conv_b)
    ub = u.reshape(bsz, seq, RG_NBLOCKS, RG_BLOCK)
    r = jax.nn.sigmoid(jnp.einsum('bsnc,ncd->bsnd', ub, w_a).reshape(bsz, seq, D_RNN) + b_a)
    i = jax.nn.sigmoid(jnp.einsum('bsnc,ncd->bsnd', ub, w_x).reshape(bsz, seq, D_RNN) + b_x)
    log_a = -RG_C * r.astype(jnp.float32) * jax.nn.softplus(-lam.astype(jnp.float32))
    a = jnp.exp(log_a)
    b = jnp.sqrt(-jnp.expm1(2.0 * log_a)) * (i * u).astype(jnp.float32)

    def combine(left, right):
        a_l, b_l = left
        a_r, b_r = right
        return a_l * a_r, a_r * b_l + b_r

    _, h = lax.associative_scan(combine, (a, b), axis=1)
    y = h.astype(x.dtype) * jax.nn.gelu(gate)
    return y @ w_out


def _fox_mixer(x, w_in, b_f, q_g, k_g, w_out):
    bsz, seq, _ = x.shape
    q, k, v, f_logit = jnp.split(x @ w_in, [D_MODEL, 2 * D_MODEL, 3 * D_MODEL], axis=-1)
    q = _rms_norm(q.reshape(bsz, seq, FOX_HEADS, FOX_HEAD_DIM), q_g).transpose(0, 2, 1, 3)
    k = _rms_norm(k.reshape(bsz, seq, FOX_HEADS, FOX_HEAD_DIM), k_g).transpose(0, 2, 1, 3)
    v = v.reshape(bsz, seq, FOX_HEADS, FOX_HEAD_DIM).transpose(0, 2, 1, 3)
    log_f = jax.nn.log_sigmoid(f_logit.astype(jnp.float32) + b_f.astype(jnp.float32))
    cum = jnp.cumsum(log_f, axis=1).transpose(0, 2, 1)
    scale = FOX_HEAD_DIM ** -0.5
    outs = []
    for blk in range(seq // Q_BLOCK):
        q0 = blk * Q_BLOCK
        q1 = q0 + Q_BLOCK
        s = jnp.einsum('bhqd,bhkd->bhqk', q[:, :, q0:q1], k[:, :, :q1]).astype(jnp.float32) * scale
        s = s + cum[:, :, q0:q1, None] - cum[:, :, None, :q1]
        mask = (q0 + jnp.arange(Q_BLOCK))[:, None] >= jnp.arange(q1)[None, :]
        s = jnp.where(mask, s, -jnp.inf)
        p = jax.nn.softmax(s, axis=-1).astype(v.dtype)
        outs.append(jnp.einsum('bhqk,bhkd->bhqd', p, v[:, :, :q1]))
    o = jnp.concatenate(outs, axis=2).transpose(0, 2, 1, 3).reshape(bsz, seq, D_MODEL)
    return o @ w_out


def _conv_module(x, w_in, b_in, dw_w, dw_b, ln_g, ln_b, w_out, b_out):
    val, gate = jnp.split(x @ w_in + b_in, 2, axis=-1)
    h = val * jax.nn.sigmoid(gate)
    h = _causal_dw_conv(h, dw_w, dw_b)
    h = jax.nn.silu(_layer_norm(h, ln_g, ln_b))
    return h @ w_out + b_out


def setup_inputs(seed: int = 0) -> dict:
    key = jax.random.key(seed)
    ks = iter(jax.random.split(key, 48))

    def normal(shape, scale):
        return scale * jax.random.normal(next(ks), shape, jnp.float32)

    n_a = (DEPTH + 2) // 3
    n_b = (DEPTH + 1) // 3
    n_c = DEPTH // 3
    out_scale = (2.0 * DEPTH) ** -0.5
    d = D_MODEL

    x = normal((BATCH, SEQ, d), 1.0)
    ffn_norm = 1.0 + normal((DEPTH, 2, d), 0.02)
    ffn_w_in = normal((DEPTH, 2, d, 2 * D_FF), d ** -0.5)
    ffn_w_out = normal((DEPTH, 2, D_FF, d), D_FF ** -0.5 * out_scale)
    mix_norm = 1.0 + normal((DEPTH, d), 0.02)

    rg_w_in = normal((n_a, d, 2 * D_RNN), d ** -0.5)
    rg_conv_w = normal((n_a, RG_CONV, D_RNN), RG_CONV ** -0.5)
    rg_conv_b = normal((n_a, D_RNN), 0.02)
    rg_w_a = normal((n_a, RG_NBLOCKS, RG_BLOCK, RG_BLOCK), RG_BLOCK ** -0.5)
    rg_b_a = normal((n_a, D_RNN), 0.02)
    rg_w_x = normal((n_a, RG_NBLOCKS, RG_BLOCK, RG_BLOCK), RG_BLOCK ** -0.5)
    rg_b_x = normal((n_a, D_RNN), 0.02)
    a_pow_c = jax.random.uniform(next(ks), (n_a, D_RNN), jnp.float32, 0.9, 0.999)
    sig_l = a_pow_c ** (1.0 / RG_C)
    rg_lambda = jnp.log(sig_l) - jnp.log1p(-sig_l)
    rg_w_out = normal((n_a, D_RNN, d), D_RNN ** -0.5 * out_scale)

    fox_w_in = normal((n_b, d, 3 * d + FOX_HEADS), d ** -0.5)
    fox_b_f = jax.random.uniform(next(ks), (n_b, FOX_HEADS), jnp.float32, 1.0, 6.0)
    fox_q_norm = 1.0 + normal((n_b, FOX_HEAD_DIM), 0.02)
    fox_k_norm = 1.0 + normal((n_b, FOX_HEAD_DIM), 0.02)
    fox_w_out = normal((n_b, d, d), d ** -0.5 * out_scale)

    cv_w_in = normal((n_c, d, 2 * d), d ** -0.5)
    cv_b_in = normal((n_c, 2 * d), 0.02)
    cv_dw_w = normal((n_c, CV_KERNEL, d), CV_KERNEL ** -0.5)
    cv_dw_b = normal((n_c, d), 0.02)
    cv_ln_g = 1.0 + normal((n_c, d), 0.02)
    cv_ln_b = normal((n_c, d), 0.02)
    cv_w_out = normal((n_c, d, d), d ** -0.5 * out_scale)
    cv_b_out = normal((n_c, d), 0.02)

    return {
        'x': x, 'ffn_norm': ffn_norm, 'ffn_w_in': ffn_w_in, 'ffn_w_out': ffn_w_out,
        'mix_norm': mix_norm,
        'rg_w_in': rg_w_in, 'rg_conv_w': rg_conv_w, 'rg_conv_b': rg_conv_b,
        'rg_w_a': rg_w_a, 'rg_b_a': rg_b_a, 'rg_w_x': rg_w_x, 'rg_b_x': rg_b_x,
        'rg_lambda': rg_lambda, 'rg_w_out': rg_w_out,
        'fox_w_in': fox_w_in, 'fox_b_f': fox_b_f, 'fox_q_norm': fox_q_norm,
        'fox_k_norm': fox_k_norm, 'fox_w_out': fox_w_out,
        'cv_w_in': cv_w_in, 'cv_b_in': cv_b_in, 'cv_dw_w': cv_dw_w, 'cv_dw_b': cv_dw_b,
        'cv_ln_g': cv_ln_g, 'cv_ln_b': cv_ln_b, 'cv_w_out': cv_w_out, 'cv_b_out': cv_b_out,
    }


def reference(x, ffn_norm, ffn_w_in, ffn_w_out, mix_norm,
              rg_w_in, rg_conv_w, rg_conv_b, rg_w_a, rg_b_a, rg_w_x, rg_b_x, rg_lambda, rg_w_out,
              fox_w_in, fox_b_f, fox_q_norm, fox_k_norm, fox_w_out,
              cv_w_in, cv_b_in, cv_dw_w, cv_dw_b, cv_ln_g, cv_ln_b, cv_w_out, cv_b_out):
    for i in range(DEPTH):
        kind = i % N_MIXERS
        j = i // N_MIXERS
        x = x + 0.5 * _swiglu(_rms_norm(x, ffn_norm[i, 0]), ffn_w_in[i, 0], ffn_w_out[i, 0])
        h = _rms_norm(x, mix_norm[i])
        if kind == 0:
            h = _rglru_mixer(h, rg_w_in[j], rg_conv_w[j], rg_conv_b[j], rg_w_a[j], rg_b_a[j],
                             rg_w_x[j], rg_b_x[j], rg_lambda[j], rg_w_out[j])
        elif kind == 1:
            h = _fox_mixer(h, fox_w_in[j], fox_b_f[j], fox_q_norm[j], fox_k_norm[j], fox_w_out[j])
        else:
            h = _conv_module(h, cv_w_in[j], cv_b_in[j], cv_dw_w[j], cv_dw_b[j],
                             cv_ln_g[j], cv_ln_b[j], cv_w_out[j], cv_b_out[j])
        x = x + h
        x = x + 0.5 * _swiglu(_rms_norm(x, ffn_norm[i, 1]), ffn_w_in[i, 1], ffn_w_out[i, 1])
    return x
```

```python
import contextlib
import numpy as np
import concourse.bass as bass
import concourse.mybir as mybir
from concourse.bass_utils import run_bass_kernel_spmd

F32 = mybir.dt.float32
BF16 = mybir.dt.bfloat16
AF = mybir.ActivationFunctionType
ALU = mybir.AluOpType

D = 1024
KC = 8
DFF = 2816
FC = 22
TT = 512
EPS = 1e-6
DEPTH = 4
SEQ = 4096
NH = 16
DH = 64
CVK = 31
GELU_C = 1.5957691216057308


class _Op:
    __slots__ = ("eng", "fn", "deps", "signals", "sem", "val", "is_dma", "chan", "idx")

    def __init__(self, eng, fn, is_dma=False, chan=None):
        self.eng = eng
        self.fn = fn
        self.deps = []
        self.signals = False
        self.sem = None
        self.val = 0
        self.is_dma = is_dma
        self.chan = chan


class Prog:
    ENGS = ("pe", "act", "dve", "pool", "sp")

    def __init__(self, nc):
        self.nc = nc
        self.ops = []
        self.last_writer = {}
        self.readers = {}
        self.chans = {}
        self.last_on_eng = {}
        self.dmas_since_barrier = []
        self.pending_barrier = {}
        self.exempt = False

    def op(self, eng, fn, reads=(), writes=(), dma_chan=None):
        o = _Op(eng, fn, is_dma=dma_chan is not None, chan=dma_chan)
        deps = []
        for r in reads:
            w = self.last_writer.get(r)
            if w is not None:
                deps.append(w)
        for w_ in writes:
            w = self.last_writer.get(w_)
            if w is not None:
                deps.append(w)
            deps.extend(self.readers.get(w_, ()))
        if not (self.exempt or eng == "pe"):
            pb = self.pending_barrier.pop(eng, None)
            if pb:
                deps.extend(pb)
        seen = set()
        for d in deps:
            if id(d) in seen or d is o:
                continue
            seen.add(id(d))
            if d.eng == "pe" and eng == "pe" and not d.is_dma and not o.is_dma:
                continue
            o.deps.append(d)
            d.signals = True
        for r in reads:
            lst = self.readers.get(r)
            if lst is None:
                self.readers[r] = [o]
            else:
                if not o.is_dma:
                    lst[:] = [x for x in lst if x.is_dma or x.eng != eng]
                lst.append(o)
        for w_ in writes:
            self.last_writer[w_] = o
            self.readers[w_] = []
        o.idx = len(self.ops)
        self.ops.append(o)
        if o.is_dma:
            self.dmas_since_barrier.append(o)
        else:
            self.last_on_eng[eng] = o
        return o

    def barrier(self):
        deps = [o for o in self.last_on_eng.values()] + list(self.dmas_since_barrier)
        self.dmas_since_barrier = []
        for e in self.ENGS:
            self.pending_barrier[e] = list(deps) + self.pending_barrier.get(e, [])

    def pe(self, fn, reads=(), writes=()):
        return self.op("pe", fn, reads, writes)

    def act(self, fn, reads=(), writes=()):
        return self.op("act", fn, reads, writes)

    def dve(self, fn, reads=(), writes=()):
        return self.op("dve", fn, reads, writes)

    def pool(self, fn, reads=(), writes=()):
        return self.op("pool", fn, reads, writes)

    def dma(self, queue, chan, fn, reads=(), writes=()):
        return self.op(queue, fn, reads, writes, dma_chan=chan)

    def emit(self, final_wait_ops=()):
        nc = self.nc
        with contextlib.ExitStack() as es:
            eng_sem = {e: es.enter_context(nc.semaphore("sem_" + e)) for e in self.ENGS}
            chan_names = []
            for o in self.ops:
                if o.is_dma and o.chan not in self.chans:
                    self.chans[o.chan] = None
                    chan_names.append(o.chan)
            for i, c in enumerate(chan_names):
                self.chans[c] = es.enter_context(nc.semaphore("dch%d" % i))
            cnt = {e: 0 for e in self.ENGS}
            ccnt = {c: 0 for c in self.chans}
            for o in self.ops:
                if o.is_dma:
                    ccnt[o.chan] += 16
                    o.sem = self.chans[o.chan]
                    o.val = ccnt[o.chan]
                elif o.signals:
                    cnt[o.eng] += 1
                    o.sem = eng_sem[o.eng]
                    o.val = cnt[o.eng]
            run_c = {c: 0 for c in self.chans}
            dma_need = {}
            for o in self.ops:
                for d in o.deps:
                    if d.is_dma:
                        dma_need[(o.idx, d.chan)] = run_c[d.chan]
                if o.is_dma:
                    run_c[o.chan] += 16
            per_eng = {e: [] for e in self.ENGS}
            for o in self.ops:
                per_eng[o.eng].append(o)
            self.stats = {e: len(per_eng[e]) for e in self.ENGS}
            self.stats["sig"] = dict(cnt)
            block = es.enter_context(nc.Block())

            def run(engname, eng):
                waited = {}
                for o in per_eng[engname]:
                    need = {}
                    for d in o.deps:
                        k = id(d.sem)
                        v = dma_need[(o.idx, d.chan)] if d.is_dma else d.val
                        if k not in need or need[k][1] < v:
                            need[k] = (d.sem, v)
                    for k, (s, v) in need.items():
                        if waited.get(k, 0) >= v:
                            continue
                        eng.wait_ge(s, v)
                        waited[k] = v
                    inst = o.fn(eng)
                    if o.is_dma:
                        inst.then_inc(o.sem, 16)
                    elif o.signals:
                        inst.then_inc(o.sem, 1)
                if engname == "sp":
                    for o in final_wait_ops:
                        eng.wait_ge(o.sem, o.val)

            @block.tensor
            def _(e):
                run("pe", e)

            @block.scalar
            def _(e):
                run("act", e)

            @block.vector
            def _(e):
                run("dve", e)

            @block.gpsimd
            def _(e):
                run("pool", e)

            @block.sync
            def _(e):
                run("sp", e)


class Stream:
    def __init__(self, P, name, slots):
        self.P = P
        self.name = name
        self.slots = slots
        self.n = len(slots)
        self.issued = 0
        self.taken = 0

    def prefetch(self, src_ap, ncols, src_token):
        i = self.issued % self.n
        slot = self.slots[i]
        ex = self.P.exempt
        self.P.exempt = True
        self.P.dma("sp", "%s%d" % (self.name, i),
                   lambda e, slot=slot, src_ap=src_ap, ncols=ncols: e.dma_start(out=slot[:, 0:ncols], in_=src_ap),
                   reads=[src_token], writes=[(self.name, i)])
        self.P.exempt = ex
        self.issued += 1

    def get(self):
        assert self.taken < self.issued
        i = self.taken % self.n
        self.taken += 1
        return self.slots[i], (self.name, i)


def run_units(stream, units, body, depth, after=None):
    n = len(units)
    for i in range(min(depth, n)):
        stream.prefetch(*units[i][:3])
    for i in range(n):
        slot, tok = stream.get()
        body(slot, tok, units[i][3])
        if i + depth < n:
            stream.prefetch(*units[i + depth][:3])
        if after and i in after:
            after[i]()


class PV:
    def __init__(self):
        self.cols = []
        self.idx = {}
        self.n = 0

    def add(self, name, arr2d):
        arr2d = np.ascontiguousarray(arr2d, dtype=np.float32)
        assert arr2d.shape[0] == 128
        self.idx[name] = self.n
        self.n += arr2d.shape[1]
        self.cols.append(arr2d)

    def pack(self):
        return np.ascontiguousarray(np.concatenate(self.cols, axis=1))


def fm(v):
    return np.asarray(v, np.float32).reshape(KC, 128).T


def pv_layout(inp):
    pv = PV()
    for l in range(DEPTH):
        pv.add("fn%d0" % l, fm(inp["ffn_norm"][l, 0]))
        pv.add("fn%d1" % l, fm(inp["ffn_norm"][l, 1]))
        pv.add("mn%d" % l, fm(inp["mix_norm"][l]))
    for j in range(inp["rg_w_in"].shape[0]):
        for k in range(4):
            pv.add("rgcw%d_%d" % (j, k), fm(inp["rg_conv_w"][j, k]))
        pv.add("rgcb%d" % j, fm(inp["rg_conv_b"][j]))
        pv.add("rgba%d" % j, fm(inp["rg_b_a"][j]))
        pv.add("rgbx%d" % j, fm(inp["rg_b_x"][j]))
        pv.add("rglam%d" % j, fm(inp["rg_lambda"][j]))
    g2 = lambda v: np.concatenate([v, v]).reshape(128, 1)
    pv.add("fqg", g2(inp["fox_q_norm"][0]))
    pv.add("fkg", g2(inp["fox_k_norm"][0]))
    pv.add("fbf", np.broadcast_to(inp["fox_b_f"][0][None, :], (128, NH)))
    pv.add("fbfc", np.pad(inp["fox_b_f"][0], (0, 128 - NH)).reshape(128, 1))
    pv.add("cvbin", fm(inp["cv_b_in"][0][:D]))
    pv.add("cvbing", fm(inp["cv_b_in"][0][D:]))
    for k in range(CVK):
        pv.add("cvdw%d" % k, fm(inp["cv_dw_w"][0, k]))
    pv.add("cvdwb", fm(inp["cv_dw_b"][0]))
    pv.add("cvlng", fm(inp["cv_ln_g"][0]))
    pv.add("cvlnb", fm(inp["cv_ln_b"][0]))
    pv.add("cvbout", fm(inp["cv_b_out"][0]))
    return pv


def weight_layouts(inp):
    w = {}
    a = inp["ffn_w_in"].reshape(DEPTH * 2, KC, 128, 2, FC, 128)
    w["ffn_win"] = np.ascontiguousarray(a.transpose(0, 4, 2, 1, 3, 5)).reshape(DEPTH * 2 * FC, 128, 2048)
    a = inp["ffn_w_out"].reshape(DEPTH * 2, 2, FC // 2, 128, KC, 128)
    w["ffn_wout"] = np.ascontiguousarray(a.transpose(0, 4, 1, 3, 2, 5)).reshape(DEPTH * 2 * KC * 2, 128, (FC // 2) * 128)

    def in2(x):
        n = x.shape[0]
        a = x.reshape(n, KC, 128, 2, KC, 128)
        return np.ascontiguousarray(a.transpose(0, 4, 2, 1, 3, 5)).reshape(n * KC, 128, 2048)

    def out1(x):
        n = x.shape[0]
        a = x.reshape(n, KC, 128, KC, 128)
        return np.ascontiguousarray(a.transpose(0, 3, 2, 1, 4)).reshape(n * KC, 128, 1024)

    w["rg_win"] = in2(inp["rg_w_in"])
    w["rg_wout"] = out1(inp["rg_w_out"])
    w["cv_win"] = in2(inp["cv_w_in"])
    w["cv_wout"] = out1(inp["cv_w_out"])
    w["fox_wout"] = out1(inp["fox_w_out"])
    fw = inp["fox_w_in"][0]
    a = fw[:, 0:2048].reshape(KC, 128, 16, 128)
    w["fox_wqk"] = np.ascontiguousarray(a.transpose(2, 1, 0, 3)).reshape(16, 128, 1024)
    a = fw[:, 2048:3072].reshape(KC, 128, 4, 256)
    w["fox_wv"] = np.ascontiguousarray(a.transpose(2, 1, 0, 3)).reshape(4, 128, 2048)
    a = fw[:, 3072:3088].reshape(KC, 128, NH)
    w["fox_wf"] = np.ascontiguousarray(a.transpose(1, 0, 2)).reshape(1, 128, KC * NH)
    for nm, key in (("rg_wa", "rg_w_a"), ("rg_wx", "rg_w_x")):
        x = inp[key]
        n = x.shape[0]
        w[nm] = np.ascontiguousarray(x.transpose(2, 0, 1, 3)).reshape(1, 128, n * KC * 128)
    return w


def const_inputs():
    c = {}
    tri = (np.arange(128)[:, None] <= np.arange(128)[None, :]).astype(np.float32)
    bd = np.zeros((128, 128), np.float32)
    bd[:64, :64] = 1.0
    bd[64:, 64:] = 1.0
    c["c_f32"] = np.ascontiguousarray(np.concatenate([np.ones((128, 128), np.float32), np.eye(128, dtype=np.float32), tri, bd,
                                                     (1.0 - tri) * np.float32(-30000.0)], axis=1))
    return c


DEBUG = []
FOXDBG = False
USE_FOX2 = True


def build_program(S, plan, wshapes, npv, pvidx):
    nc = bass.Bass("TRN2", target_bir_lowering=False)
    NT = S // TT
    NB = S // 128
    xT_d = nc.dram_tensor("xT", [D, S], F32, kind="ExternalInput").ap()
    out_d = nc.dram_tensor("outT", [D, S], F32, kind="ExternalOutput").ap()
    pv_d = nc.dram_tensor("pv", [128, npv], F32, kind="ExternalInput").ap()
    cf_d = nc.dram_tensor("c_f32", [128, 640], F32, kind="ExternalInput").ap()
    wd = {}
    ws = {}
    for nm, shp in wshapes.items():
        wd[nm] = nc.dram_tensor(nm, list(shp), F32, kind="ExternalInput").ap()
        ws[nm] = nc.dram_tensor(nm + "_bf", list(shp), BF16).ap()
    need_fox = any(sub == "mix" and l % 3 == 1 for l, sub in plan)
    if need_fox:
        kt_d = nc.dram_tensor("kt_s", [KC, 128, S], BF16).ap()
        kt2_d = nc.dram_tensor("kt2_s", [NH, 70, S], BF16).ap()
        v_d = nc.dram_tensor("v_s", [128, NB, NH * 128], BF16).ap()

    with contextlib.ExitStack() as es:
        sb_ctr = [0]

        def sb(name, shape, dt, stack=es):
            sb_ctr[0] += 1
            return stack.enter_context(nc.sbuf_tensor("%s_%d" % (name, sb_ctr[0]), shape, dt))

        X = sb("X", [128, KC, S], F32)
        xn = sb("xn", [128, KC, TT], BF16)
        sq = sb("sq", [128, 2, TT], BF16)
        rs0 = sb("rs0", [128, TT], F32)
        rs1 = sb("rs1", [128, TT], F32)
        pvt = sb("pvt", [128, npv], F32)
        cf = sb("cf", [128, 256], F32)
        cb = sb("cb", [128, 640], BF16)
        win_slots = [sb("win%d" % i, [128, 2048], BF16) for i in range(3)]
        wout_slots = [sb("wout%d" % i, [128, (FC // 2) * 128], BF16) for i in range(3)]
        psb = [es.enter_context(nc.psum_tensor("ps%d" % i, [128, 512], F32)) for i in range(8)]
        pss, pg, pu, po, px = psb[0], psb[1:3], psb[3:5], psb[5:7], psb[7]
        ones_bf = cb[:, 0:128]
        tri_bf = cb[:, 256:384]
        bd_bf = cb[:, 384:512]
        negm_bf = cb[:, 512:640]
        ident_bf = cb[:, 128:256]
        ones_f = cf[:, 0:128]
        ident_f = cf[:, 128:256]
        tri_f = None

        P = Prog(nc)
        WIN = Stream(P, "win", win_slots)
        WOUT = Stream(P, "wout", wout_slots)

        epsn = sb("epsn", [128, 1], F32)

        def pvc(name, k=0, w=1):
            if name == "eps":
                return epsn[:, 0:1]
            c0 = pvidx[name] + k
            return pvt[:, c0:c0 + w]

        P.pool(lambda e: e.memset(epsn[:], EPS), writes=["pvt2"])
        P.dma("sp", "ld_pv", lambda e: e.dma_start(out=pvt[:], in_=pv_d), writes=["pvt"])
        P.dma("sp", "ld_cf", lambda e: e.dma_start(out=cf[:], in_=cf_d[:, 0:256]), writes=["cf"])
        P.dma("pool", "ld_cb", lambda e: e.dma_start(out=cb[:], in_=cf_d), writes=["cb"])
        for t in range(NT):
            for k in range(KC):
                P.dma("sp", "ld_xt%d" % t,
                      lambda e, k=k, t=t: e.dma_start(out=X[:, k, t * TT:(t + 1) * TT],
                                                       in_=xT_d[k * 128:(k + 1) * 128, t * TT:(t + 1) * TT]),
                      writes=[("X", k, t)])

        cast_done = set()
        cast_ctr = [0]

        cast_after = [()]

        def cast(nm, lo, hi):
            key = (nm, lo, hi)
            if key in cast_done:
                return
            cast_done.add(key)
            shp = wshapes[nm]
            ncol = shp[2]
            src = wd[nm][lo:hi]
            dst = ws[nm][lo:hi]
            ch = "cast%d" % cast_ctr[0]
            cast_ctr[0] += 1
            kw = dict(max_dma_last_dim=2048) if ncol % 512 == 0 else dict(max_dma_last_dim=1408)
            ex = P.exempt
            P.exempt = True
            P.dma("pool", ch, lambda e, src=src, dst=dst, kw=kw: e.dma_start(out=dst, in_=src, **kw),
                  reads=list(cast_after[0]), writes=[("ws", nm, u) for u in range(lo, hi)])
            P.exempt = ex

        tile_hook = [None]

        def casts_for(i):
            if i >= len(plan):
                return
            l, sub = plan[i]
            if sub in ("f0", "f1"):
                ls = l * 2 + (0 if sub == "f0" else 1)
                cast("ffn_win", ls * FC, (ls + 1) * FC)
                cast("ffn_wout", ls * KC * 2, (ls + 1) * KC * 2)
            else:
                kind = l % 3
                j = l // 3
                if kind == 0:
                    cast("rg_win", j * KC, (j + 1) * KC)
                    cast("rg_wa", 0, 1)
                    cast("rg_wx", 0, 1)
                    cast("rg_wout", j * KC, (j + 1) * KC)
                elif kind == 1:
                    cast("fox_wqk", 0, 16)
                    cast("fox_wv", 0, 4)
                    cast("fox_wf", 0, 1)
                    cast("fox_wout", 0, KC)
                else:
                    cast("cv_win", 0, KC)
                    cast("cv_wout", 0, KC)

        def xtoks(t):
            return [("X", k, t) for k in range(KC)]

        def rmsnorm(t, gname, xo=None, xtok="xn"):
            if xo is None:
                xo = xn
            ex = P.exempt
            P.exempt = (xo is xn)
            c0, c1 = t * TT, (t + 1) * TT
            for k in range(KC):
                P.act(lambda e, k=k: e.activation(out=sq[:, k % 2, :], in_=X[:, k, c0:c1], func=AF.Square),
                      reads=[("X", k, t)], writes=[("sq", k % 2)])
                P.pe(lambda e, k=k: e.matmul(pss[:], lhsT=ones_bf, rhs=sq[:, k % 2, :], start=(k == 0), stop=(k == KC - 1)),
                     reads=[("sq", k % 2), "cb"], writes=["pss"])
            P.act(lambda e: e.activation(out=rs0[:], in_=pss[:], func=AF.Sqrt, scale=1.0 / D, bias=pvc("eps")),
                  reads=["pss", "pvt2"], writes=["rs0"])
            P.dve(lambda e: e.reciprocal(out=rs1[:], in_=rs0[:]), reads=["rs0"], writes=["rs1"])
            for k in range(KC):
                P.dve(lambda e, k=k: e.scalar_tensor_tensor(out=xo[:, k, :], in0=X[:, k, c0:c1], scalar=pvc(gname, k),
                                                            in1=rs1[:], op0=ALU.mult, op1=ALU.mult),
                      reads=[("X", k, t), "rs1", "pvt"], writes=[(xtok, k)])
            P.exempt = ex

        def resid_add(m, t, pbank, ptok, scale):
            c0, c1 = t * TT, (t + 1) * TT
            P.dve(lambda e: e.scalar_tensor_tensor(out=X[:, m, c0:c1], in0=pbank[:], scalar=float(scale),
                                                   in1=X[:, m, c0:c1], op0=ALU.mult, op1=ALU.add),
                  reads=[ptok, ("X", m, t)], writes=[("X", m, t)])

        def out_proj(t, wname, ubase, rhs_tile, rhs_tok, scale, nk=KC, bias_name=None):
            units = [(ws[wname][ubase + m], nk * 128, ("ws", wname, ubase + m), m) for m in range(KC)]

            def body(slot, tok, m):
                pb = po[m % 2]
                ptok = ("po", m % 2)
                for k in range(nk):
                    P.pe(lambda e, k=k, slot=slot, pb=pb: e.matmul(pb[:], lhsT=slot[:, k * 128:(k + 1) * 128], rhs=rhs_tile[:, k, :],
                                                                  start=(k == 0), stop=(k == nk - 1)),
                         reads=[tok, (rhs_tok, k)], writes=[ptok])
                if bias_name is not None:
                    c0, c1 = t * TT, (t + 1) * TT
                    P.dve(lambda e, m=m, pb=pb: e.scalar_tensor_tensor(out=X[:, m, c0:c1], in0=pb[:], scalar=pvc(bias_name, m),
                                                                       in1=X[:, m, c0:c1], op0=ALU.add, op1=ALU.add),
                          reads=[ptok, ("X", m, t), "pvt"], writes=[("X", m, t)])
                else:
                    resid_add(m, t, pb, ptok, scale)
            return units, body

        def ffn(l, s, stack):
            ls = l * 2 + s
            h = sb("h", [128, FC, TT], BF16, stack)
            sg = [sb("sg%d" % i, [128, TT], F32, stack) for i in range(2)]
            xn2 = sb("xn2", [128, KC, TT], BF16, stack)
            xnb = [(xn, "xn"), (xn2, "xn2")]
            gname = "fn%d%d" % (l, s)
            rmsnorm(0, gname, *xnb[0])
            for t in range(NT):
                xcur, xtk = xnb[t % 2]
                units = [(ws["ffn_win"][ls * FC + c], 2048, ("ws", "ffn_win", ls * FC + c), c) for c in range(FC)]

                def body(slot, tok, c, xcur=xcur, xtk=xtk):
                    b = c % 2
                    for g, pb, pn in ((0, pg[b], "pg"), (1, pu[b], "pu")):
                        for k in range(KC):
                            off = (k * 2 + g) * 128
                            P.pe(lambda e, k=k, off=off, pb=pb, slot=slot: e.matmul(pb[:], lhsT=slot[:, off:off + 128], rhs=xcur[:, k, :],
                                                                                    start=(k == 0), stop=(k == KC - 1)),
                                 reads=[tok, (xtk, k)], writes=[(pn, b)])
                    P.act(lambda e, b=b: e.activation(out=sg[b][:], in_=pg[b][:], func=AF.Silu),
                          reads=[("pg", b)], writes=[("sg", b)])
                    P.dve(lambda e, b=b, c=c: e.tensor_tensor(out=h[:, c, :], in0=sg[b][:], in1=pu[b][:], op=ALU.mult),
                          reads=[("sg", b), ("pu", b)], writes=[("h", c)])
                run_units(WIN, units, body, depth=2)
                HF = FC // 2
                units2 = [(ws["ffn_wout"][(ls * KC + m) * 2 + hf], HF * 128, ("ws", "ffn_wout", (ls * KC + m) * 2 + hf), (m, hf))
                          for m in range(KC) for hf in range(2)]

                def body2(slot, tok, mh, t=t):
                    m, hf = mh
                    pb = po[m % 2]
                    ptok = ("po", m % 2)
                    for ci in range(HF):
                        c = hf * HF + ci
                        P.pe(lambda e, c=c, ci=ci, slot=slot, pb=pb: e.matmul(pb[:], lhsT=slot[:, ci * 128:(ci + 1) * 128], rhs=h[:, c, :],
                                                                             start=(c == 0), stop=(c == FC - 1)),
                             reads=[tok, ("h", c)], writes=[ptok])
                    if hf == 1:
                        resid_add(m, t, pb, ptok, 0.5)
                hook = None
                if t + 1 < NT:
                    nxt = xnb[(t + 1) % 2]
                    hook = {3: (lambda t=t, nxt=nxt: rmsnorm(t + 1, gname, *nxt))}
                run_units(WOUT, units2, body2, depth=2, after=hook)
                if tile_hook[0] is not None:
                    tile_hook[0](t)

        def rglru(l, stack):
            j = l // 3
            f32t = lambda nm, w=TT: [sb(nm + str(i), [128, w], F32, stack) for i in range(2)]
            uext = f32t("uext", TT + 3)
            acc = f32t("acc")
            ra = f32t("ra")
            ib = f32t("ib")
            hh = f32t("hh")
            gt = f32t("gt")
            ub = [sb("ub%d" % i, [128, TT], BF16, stack) for i in range(2)]
            yb = sb("yb", [128, KC, TT], BF16, stack)
            ucar = sb("ucar", [128, KC, 3], F32, stack)
            hst = sb("hst", [128, KC], F32, stack)
            cl = sb("cl", [128, KC], F32, stack)
            wg = sb("wg", [128, 2, KC * 128], BF16, stack)
            Rb = [(px, "px"), (po[0], ("po", 0))]
            Ib = [(pss, "pss"), (po[1], ("po", 1))]
            P.dma("sp", "ld_wg0", lambda e: e.dma_start(out=wg[:, 0, :], in_=ws["rg_wa"][0][:, j * KC * 128:(j + 1) * KC * 128]),
                  reads=[("ws", "rg_wa", 0)], writes=["wg0"])
            P.dma("sp", "ld_wg1", lambda e: e.dma_start(out=wg[:, 1, :], in_=ws["rg_wx"][0][:, j * KC * 128:(j + 1) * KC * 128]),
                  reads=[("ws", "rg_wx", 0)], writes=["wg1"])
            P.act(lambda e: e.activation(out=cl[:], in_=pvc("rglam%d" % j, 0, KC), func=AF.Exp, scale=-1.0), reads=["pvt"], writes=["cl"])
            P.act(lambda e: e.activation(out=cl[:], in_=cl[:], func=AF.Ln, bias=1.0), reads=["cl"], writes=["cl"])
            P.dve(lambda e: e.tensor_scalar(out=cl[:], in0=cl[:], scalar1=-8.0, scalar2=None, op0=ALU.mult), reads=["cl"], writes=["cl"])
            P.pool(lambda e: e.memset(ucar[:], 0.0), writes=["ucar"])
            P.pool(lambda e: e.memset(hst[:], 0.0), writes=["hst"])

            def st_proj(c, b, slot, tok):
                for g, pb, pn in ((0, pg[b], "pg"), (1, pu[b], "pu")):
                    for k in range(KC):
                        off = (k * 2 + g) * 128
                        P.pe(lambda e, k=k, off=off, pb=pb, slot=slot: e.matmul(pb[:], lhsT=slot[:, off:off + 128], rhs=xn[:, k, :],
                                                                                start=(k == 0), stop=(k == KC - 1)),
                             reads=[tok, ("xn", k)], writes=[(pn, b)])

            def st_u(c, b):
                P.pool(lambda e: e.tensor_copy(out=uext[b][:, 0:3], in_=ucar[:, c, :]), reads=[("ucar", c)], writes=[("uext_c", b)])
                P.act(lambda e: e.activation(out=uext[b][:, 3:TT + 3], in_=pu[b][:], func=AF.Copy), reads=[("pu", b)], writes=[("uext", b)])
                P.act(lambda e: e.activation(out=acc[b][:], in_=pu[b][:], func=AF.Identity, scale=pvc("rgcw%d_3" % j, c), bias=pvc("rgcb%d" % j, c)),
                      reads=[("pu", b), "pvt"], writes=[("acc", b)])
                P.pool(lambda e: e.tensor_copy(out=ucar[:, c, :], in_=uext[b][:, TT:TT + 3]), reads=[("uext", b), ("uext_c", b)], writes=[("ucar", c)])

            def st_tap(c, b, kk):
                P.dve(lambda e: e.scalar_tensor_tensor(out=acc[b][:], in0=uext[b][:, kk:kk + TT], scalar=pvc("rgcw%d_%d" % (j, kk), c),
                                                       in1=acc[b][:], op0=ALU.mult, op1=ALU.add),
                      reads=[("uext", b), ("uext_c", b), ("acc", b), "pvt"], writes=[("acc", b)])

            def st_ub(c, b):
                P.act(lambda e: e.activation(out=ub[b][:], in_=acc[b][:], func=AF.Copy), reads=[("acc", b)], writes=[("ub", b)])

            def st_gates(c, b):
                P.pe(lambda e: e.matmul(Rb[b][0][:], lhsT=wg[:, 0, c * 128:(c + 1) * 128], rhs=ub[b][:], start=True, stop=True),
                     reads=["wg0", ("ub", b)], writes=[Rb[b][1]])
                P.pe(lambda e: e.matmul(Ib[b][0][:], lhsT=wg[:, 1, c * 128:(c + 1) * 128], rhs=ub[b][:], start=True, stop=True),
                     reads=["wg1", ("ub", b)], writes=[Ib[b][1]])

            def st_sig_r(c, b):
                P.act(lambda e: e.activation(out=ra[b][:], in_=Rb[b][0][:], func=AF.Sigmoid, bias=pvc("rgba%d" % j, c)),
                      reads=[Rb[b][1], "pvt"], writes=[("ra", b)])

            def st_sig_i(c, b):
                P.act(lambda e: e.activation(out=ib[b][:], in_=Ib[b][0][:], func=AF.Sigmoid, bias=pvc("rgbx%d" % j, c)),
                      reads=[Ib[b][1], "pvt"], writes=[("ib", b)])

            def st_a(c, b):
                P.act(lambda e: e.activation(out=ra[b][:], in_=ra[b][:], func=AF.Exp, scale=cl[:, c:c + 1]), reads=[("ra", b), "cl"], writes=[("ra", b)])

            def st_a2(c, b):
                P.act(lambda e: e.activation(out=uext[b][:, 0:TT], in_=ra[b][:], func=AF.Square), reads=[("ra", b), ("acc", b), ("ucar", c)],
                      writes=[("uext", b), ("uext_c", b)])

            def st_m(c, b):
                P.act(lambda e: e.activation(out=uext[b][:, 0:TT], in_=uext[b][:, 0:TT], func=AF.Sqrt, scale=-1.0, bias=1.0),
                      reads=[("uext", b), ("uext_c", b)], writes=[("uext", b), ("uext_c", b)])

            def st_b1(c, b):
                P.dve(lambda e: e.tensor_tensor(out=ib[b][:], in0=ib[b][:], in1=acc[b][:], op=ALU.mult), reads=[("ib", b), ("acc", b)], writes=[("ib", b)])

            def st_b2(c, b):
                P.dve(lambda e: e.tensor_tensor(out=ib[b][:], in0=ib[b][:], in1=uext[b][:, 0:TT], op=ALU.mult),
                      reads=[("ib", b), ("uext", b), ("uext_c", b)], writes=[("ib", b)])

            def st_scan(c, b):
                P.dve(lambda e: e.tensor_tensor_scan(out=hh[b][:], data0=ra[b][:], data1=ib[b][:], initial=hst[:, c:c + 1], op0=ALU.mult, op1=ALU.add),
                      reads=[("ra", b), ("ib", b), ("hst", c)], writes=[("hh", b)])
                P.pool(lambda e: e.tensor_copy(out=hst[:, c:c + 1], in_=hh[b][:, TT - 1:TT]), reads=[("hh", b)], writes=[("hst", c)])

            def st_g1(c, b):
                P.act(lambda e: e.activation(out=gt[b][:], in_=pg[b][:], func=AF.Square), reads=[("pg", b)], writes=[("gt", b)])

            def st_g2(c, b):
                P.act(lambda e: e.activation(out=gt[b][:], in_=gt[b][:], func=AF.Identity, scale=0.044715, bias=1.0), reads=[("gt", b)], writes=[("gt", b)])

            def st_g3(c, b):
                P.dve(lambda e: e.tensor_tensor(out=gt[b][:], in0=gt[b][:], in1=pg[b][:], op=ALU.mult), reads=[("gt", b), ("pg", b)], writes=[("gt", b)])

            def st_g4(c, b):
                P.act(lambda e: e.activation(out=gt[b][:], in_=gt[b][:], func=AF.Sigmoid, scale=GELU_C), reads=[("gt", b)], writes=[("gt", b)])

            def st_g5(c, b):
                P.dve(lambda e: e.tensor_tensor(out=gt[b][:], in0=gt[b][:], in1=pg[b][:], op=ALU.mult), reads=[("gt", b), ("pg", b)], writes=[("gt", b)])

            def st_y(c, b):
                P.dve(lambda e: e.tensor_tensor(out=yb[:, c, :], in0=gt[b][:], in1=hh[b][:], op=ALU.mult), reads=[("gt", b), ("hh", b)], writes=[("yb", c)])

            stages = [st_u, st_g1, lambda c, b: st_tap(c, b, 0), st_g2, lambda c, b: st_tap(c, b, 1), lambda c, b: st_tap(c, b, 2),
                      st_ub, st_g3, st_gates, st_g4, st_sig_r, st_sig_i, st_a, st_g5, st_a2, st_b1, st_m, st_b2, st_scan, st_y]

            for t in range(NT):
                rmsnorm(t, "mn%d" % l)
                for c2 in range(0, KC, 2):
                    cs_ = (c2, c2 + 1)
                    got = []
                    if t == 0 and c2 == 0:
                        for c in cs_:
                            WIN.prefetch(ws["rg_win"][j * KC + c], 2048, ("ws", "rg_win", j * KC + c))
                    for c in cs_:
                        got.append(WIN.get())
                    for c, (slot, tok) in zip(cs_, got):
                        st_proj(c, c % 2, slot, tok)
                    nxt = None
                    if c2 + 2 < KC:
                        nxt = (c2 + 2, c2 + 3)
                    elif t + 1 < NT:
                        nxt = (0, 1)
                    if nxt:
                        for c in nxt:
                            WIN.prefetch(ws["rg_win"][j * KC + c], 2048, ("ws", "rg_win", j * KC + c))
                    for stg in stages:
                        for c in cs_:
                            stg(c, c % 2)
                u2, b2 = out_proj(t, "rg_wout", j * KC, yb, "yb", 1.0)
                run_units(WOUT, u2, b2, depth=2)

        def convmod(l, stack):
            f32t = lambda nm, w=TT: sb(nm, [128, w], F32, stack)
            PAD = CVK - 1
            gext = [sb("gext%d" % i, [128, TT + PAD], BF16, stack) for i in range(2)]
            sgm = f32t("sgm")
            cvo = sb("cvo", [128, KC, TT], F32, stack)
            mu = f32t("mu")
            rstd = f32t("rstd")
            ybc = xn
            gcar = sb("gcar", [128, KC, PAD], BF16, stack)
            dg = sb("dg", [128, CVK, 128], BF16, stack)
            P.pool(lambda e: e.memset(gcar[:], 0.0), writes=["gcar"])
            for t in range(NT):
                rmsnorm(t, "mn%d" % l)
                units = [(ws["cv_win"][c], 2048, ("ws", "cv_win", c), c) for c in range(KC)]

                def body(slot, tok, c):
                    b = c % 2
                    T_ = lambda nm: (nm, b)
                    for g, pb, pn in ((0, pg[b], "pg"), (1, pu[b], "pu")):
                        for k in range(KC):
                            off = (k * 2 + g) * 128
                            P.pe(lambda e, k=k, off=off, pb=pb, slot=slot: e.matmul(pb[:], lhsT=slot[:, off:off + 128], rhs=xn[:, k, :],
                                                                                    start=(k == 0), stop=(k == KC - 1)),
                                 reads=[tok, ("xn", k)], writes=[(pn, b)])
                    P.act(lambda e, c=c, b=b: e.activation(out=sgm[:], in_=pu[b][:], func=AF.Sigmoid, bias=pvc("cvbing", c)),
                          reads=[("pu", b), "pvt"], writes=["sgm"])
                    P.pool(lambda e, c=c, b=b: e.tensor_copy(out=gext[b][:, 0:PAD], in_=gcar[:, c, :]), reads=["gcar"], writes=[T_("gext_c")])
                    P.dve(lambda e, c=c, b=b: e.scalar_tensor_tensor(out=gext[b][:, PAD:PAD + TT], in0=pg[b][:], scalar=pvc("cvbin", c), in1=sgm[:],
                                                                     op0=ALU.add, op1=ALU.mult),
                          reads=[("pg", b), "sgm", "pvt"], writes=[T_("gext")])
                    P.pool(lambda e, c=c, b=b: e.tensor_copy(out=gcar[:, c, :], in_=gext[b][:, TT:TT + PAD]), reads=[T_("gext"), T_("gext_c")], writes=["gcar"])
                    for kk in range(CVK):
                        if kk % 2 == 0:
                            P.act(lambda e, c=c, kk=kk: e.activation(out=dg[:, kk, :], in_=ident_f, func=AF.Copy, scale=pvc("cvdw%d" % kk, c)),
                                  reads=["cf", "pvt"], writes=[("dg", kk)])
                        else:
                            P.dve(lambda e, c=c, kk=kk: e.tensor_scalar(out=dg[:, kk, :], in0=ident_f, scalar1=pvc("cvdw%d" % kk, c), scalar2=None, op0=ALU.mult),
                                  reads=["cf", "pvt"], writes=[("dg", kk)])
                    for kk in range(CVK):
                        P.pe(lambda e, kk=kk, b=b: e.matmul(po[b][:], lhsT=dg[:, kk, :], rhs=gext[b][:, kk:kk + TT], start=(kk == 0), stop=(kk == CVK - 1)),
                             reads=[("dg", kk), T_("gext"), T_("gext_c")], writes=[("po", b)])
                    P.dve(lambda e, c=c, b=b: e.tensor_scalar(out=cvo[:, c, :], in0=po[b][:], scalar1=pvc("cvdwb", c), scalar2=None, op0=ALU.add),
                          reads=[("po", b), "pvt"], writes=[("cvo", c)])
                run_units(WIN, units, body, depth=2)
                for c in range(KC):
                    P.pe(lambda e, c=c: e.matmul(pss[:], lhsT=ones_f, rhs=cvo[:, c, :], start=(c == 0), stop=(c == KC - 1)),
                         reads=[("cvo", c), "cf"], writes=["pss"])
                for c in range(KC):
                    P.act(lambda e, c=c: e.activation(out=sgm[:], in_=cvo[:, c, :], func=AF.Square), reads=[("cvo", c)], writes=["sgm"])
                    P.pe(lambda e, c=c: e.matmul(px[:], lhsT=ones_f, rhs=sgm[:], start=(c == 0), stop=(c == KC - 1)),
                         reads=["sgm", "cf"], writes=["px"])
                P.dve(lambda e: e.tensor_scalar(out=mu[:], in0=pss[:], scalar1=1.0 / D, scalar2=None, op0=ALU.mult), reads=["pss"], writes=["mu"])
                P.dve(lambda e: e.tensor_tensor(out=rstd[:], in0=mu[:], in1=mu[:], op=ALU.mult), reads=["mu"], writes=["rstd"])
                P.dve(lambda e: e.scalar_tensor_tensor(out=rstd[:], in0=px[:], scalar=1.0 / D, in1=rstd[:], op0=ALU.mult, op1=ALU.subtract),
                      reads=["px", "rstd"], writes=["rstd"])
                P.act(lambda e: e.activation(out=rstd[:], in_=rstd[:], func=AF.Sqrt, bias=pvc("eps")), reads=["rstd", "pvt2"], writes=["rstd"])
                P.dve(lambda e: e.reciprocal(out=rstd[:], in_=rstd[:]), reads=["rstd"], writes=["rstd"])
                for c in range(KC):
                    P.dve(lambda e, c=c: e.tensor_tensor(out=sgm[:], in0=cvo[:, c, :], in1=mu[:], op=ALU.subtract),
                          reads=[("cvo", c), "mu"], writes=["sgm"])
                    P.dve(lambda e, c=c: e.scalar_tensor_tensor(out=sgm[:], in0=sgm[:], scalar=pvc("cvlng", c), in1=rstd[:], op0=ALU.mult, op1=ALU.mult),
                          reads=["sgm", "rstd", "pvt"], writes=["sgm"])
                    P.act(lambda e, c=c: e.activation(out=ybc[:, c, :], in_=sgm[:], func=AF.Silu, bias=pvc("cvlnb", c)),
                          reads=["sgm", "pvt"], writes=[("xn", c)])
                u2, b2 = out_proj(t, "cv_wout", 0, ybc, "xn", 1.0, bias_name="cvbout")
                run_units(WOUT, u2, b2, depth=2)

        def fox(l, stack):
            qn = [sb("qn%d" % i, [128, TT], BF16, stack) for i in range(2)]
            kst = [sb("kst%d" % i, [128, TT], BF16, stack) for i in range(2)]
            vst = sb("vst", [128, NH * 128], BF16, stack)
            ot = sb("ot", [128, KC, TT], BF16, stack)
            PB = 4
            kbuf = [sb("kbuf%d" % i, [128, PB * 128], BF16, stack) for i in range(2)]
            vbuf = [sb("vbuf%d" % i, [128, PB, 256], BF16, stack) for i in range(2)]
            pT = [sb("pT%d" % i, [128, 128], BF16, stack) for i in range(4)]
            call = sb("call", [128, NB, NH], F32, stack)
            runtot = sb("runtot", [128, NB + 1, NH], F32, stack)
            bias4 = [sb("bias%d" % i, [128, NB, NH], F32, stack) for i in range(4)]
            rc = sb("rc", [128, TT], F32, stack)
            lnv = sb("lnv", [128, NH], F32, stack)
            wf = sb("wf", [128, KC * NH], BF16, stack)
            qg8 = sb("qg8", [128, 1], F32, stack)
            sqq = sb("sqq", [128, TT], BF16, stack)
            P.dma("sp", "ld_wf", lambda e: e.dma_start(out=wf[:], in_=ws["fox_wf"][0]), reads=[("ws", "fox_wf", 0)], writes=["wf"])
            P.dve(lambda e: e.tensor_scalar(out=qg8[:], in0=pvc("fqg"), scalar1=DH ** -0.5, scalar2=None, op0=ALU.mult), reads=["pvt"], writes=["qg8"])
            P.pool(lambda e: e.memset(runtot[:, 0, :], 0.0), writes=[("runtot", 0)])
            P.pool(lambda e: e.memset(vst[:], 1.0), writes=["vst"])
            pt_ctr = [0]
            kv_ctr = [0]

            def qk_proj(unit_idx, dst, dst_tok, gain_ap):
                WIN.prefetch(ws["fox_wqk"][unit_idx], 1024, ("ws", "fox_wqk", unit_idx))
                slot, tok = WIN.get()
                for k in range(KC):
                    P.pe(lambda e, k=k, slot=slot: e.matmul(pg[0][:], lhsT=slot[:, k * 128:(k + 1) * 128], rhs=xn[:, k, :],
                                                            start=(k == 0), stop=(k == KC - 1)),
                         reads=[tok, ("xn", k)], writes=[("pg", 0)])
                P.act(lambda e: e.activation(out=sqq[:], in_=pg[0][:], func=AF.Square), reads=[("pg", 0)], writes=["sqq"])
                P.pe(lambda e: e.matmul(pu[0][:], lhsT=bd_bf, rhs=sqq[:], start=True, stop=True), reads=["sqq", "cb"], writes=[("pu", 0)])
                P.act(lambda e: e.activation(out=rs0[:], in_=pu[0][:], func=AF.Sqrt, scale=1.0 / DH, bias=pvc("eps")),
                      reads=[("pu", 0), "pvt2"], writes=["rs0"])
                P.dve(lambda e: e.reciprocal(out=rs1[:], in_=rs0[:]), reads=["rs0"], writes=["rs1"])
                P.dve(lambda e: e.scalar_tensor_tensor(out=dst[:], in0=pg[0][:], scalar=gain_ap, in1=rs1[:], op0=ALU.mult, op1=ALU.mult),
                      reads=[("pg", 0), "rs1", "qg8", "pvt"], writes=[dst_tok])

            for t in range(NT):
                c0 = t * TT
                rmsnorm(t, "mn%d" % l)
                for b in range(4):
                    blk = t * 4 + b
                    for k in range(KC):
                        P.pe(lambda e, k=k, b=b: e.matmul(px[:, 0:NH], lhsT=xn[:, k, b * 128:(b + 1) * 128], rhs=wf[:, k * NH:(k + 1) * NH],
                                                          start=(k == 0), stop=(k == KC - 1)),
                             reads=["wf", ("xn", k)], writes=["px"])
                    P.dve(lambda e: e.tensor_tensor(out=lnv[:], in0=px[:, 0:NH], in1=pvc("fbf", 0, NH), op=ALU.add), reads=["px", "pvt"], writes=["lnv"])
                    P.act(lambda e: e.activation(out=lnv[:], in_=lnv[:], func=AF.Exp, scale=-1.0), reads=["lnv"], writes=["lnv"])
                    P.act(lambda e: e.activation(out=lnv[:], in_=lnv[:], func=AF.Ln, bias=1.0), reads=["lnv"], writes=["lnv"])
                    P.pe(lambda e: e.matmul(pss[:, 0:NH], lhsT=tri_f, rhs=lnv[:], start=True, stop=True, skip_group_check=True), reads=["lnv", "cf"], writes=["pss"])
                    P.pe(lambda e: e.matmul(pss[:, NH:2 * NH], lhsT=ones_f, rhs=lnv[:], start=False, stop=True, skip_group_check=True),
                         reads=["lnv", "cf"], writes=["pss"])
                    P.dve(lambda e, blk=blk: e.tensor_tensor(out=call[:, blk, :], in0=pss[:, 0:NH], in1=runtot[:, blk, :], op=ALU.add),
                          reads=["pss", ("runtot", blk)], writes=[("call", blk)])
                    P.dve(lambda e, blk=blk: e.tensor_tensor(out=runtot[:, blk + 1, :], in0=pss[:, NH:2 * NH], in1=runtot[:, blk, :], op=ALU.add),
                          reads=["pss", ("runtot", blk)], writes=[("runtot", blk + 1)])
                for c in range(KC):
                    ks = kst[c % 2]
                    qk_proj(8 + c, ks, ("kst", c % 2), pvc("fkg"))
                    P.dma("sp", "st_k%d" % (c % 2), lambda e, c=c, ks=ks, c0=c0: e.dma_start(out=kt_d[c][:, c0:c0 + TT], in_=ks[:]),
                          reads=[("kst", c % 2)], writes=[("kt_d", c, t)])
                for b in range(4):
                    blk = t * 4 + b
                    units = [(ws["fox_wv"][g], 2048, ("ws", "fox_wv", g), g) for g in range(4)]

                    def vbody(slot, tok, g, b=b):
                        pb = pu[0] if g % 2 == 0 else pg[0]
                        ptok = ("pu", 0) if g % 2 == 0 else ("pg", 0)
                        for k in range(KC):
                            P.pe(lambda e, k=k, slot=slot, pb=pb: e.matmul(pb[:, 0:256], lhsT=xn[:, k, b * 128:(b + 1) * 128],
                                                                          rhs=slot[:, k * 256:(k + 1) * 256], start=(k == 0), stop=(k == KC - 1)),
                                 reads=[tok, ("xn", k)], writes=[ptok])
                        for i in range(4):
                            hd = 4 * g + i
                            d0 = hd * 128 + (hd % 2) * 64
                            P.act(lambda e, pb=pb, i=i, d0=d0: e.activation(out=vst[:, d0:d0 + 64], in_=pb[:, i * 64:(i + 1) * 64], func=AF.Copy),
                                  reads=[ptok], writes=["vst"])
                    run_units(WIN, units, vbody, depth=2)
                    P.dma("sp", "st_v", lambda e, blk=blk: e.dma_start(out=v_d[:, blk, :], in_=vst[:]), reads=["vst"], writes=[("v_d", blk)])
                for qb in range(4):
                    I = t * 4 + qb
                    for h_ in range(NH):
                        P.pool(lambda e, qb=qb, I=I, h_=h_: e.tensor_scalar(out=bias4[qb][:, 0:I + 1, h_], in0=call[:, 0:I + 1, h_],
                                                                            scalar1=runtot[:, I + 1, h_:h_ + 1], scalar2=None, op0=ALU.subtract),
                               reads=[("call", jb) for jb in range(t * 4, I + 1)] + [("runtot", I + 1)], writes=[("bias", qb)])
                for c in range(KC):
                    q = qn[c % 2]
                    qtok = ("qn", c % 2)
                    qk_proj(c, q, qtok, qg8[:, 0:1])
                    P.dve(lambda e: e.memset(po[0][:], 0.0), writes=[("po", 0)])
                    P.dve(lambda e: e.memset(po[1][:], 0.0), writes=[("po", 1)])
                    nblk = (t + 1) * 4
                    for j0 in range(0, nblk, PB):
                        nj = min(PB, nblk - j0)
                        kvi = kv_ctr[0] % 2
                        kv_ctr[0] += 1
                        kb, vb = kbuf[kvi], vbuf[kvi]
                        ktoks = [("kt_d", c, tt_) for tt_ in range(j0 // 4, (j0 + nj + 3) // 4)]
                        P.dma("sp", "ld_k%d" % kvi,
                              lambda e, kb=kb, j0=j0, nj=nj, c=c: e.dma_start(out=kb[:, 0:nj * 128], in_=kt_d[c][:, j0 * 128:(j0 + nj) * 128]),
                              reads=ktoks, writes=[("kbuf", kvi)])
                        P.dma("sp", "ld_v%d" % kvi,
                              lambda e, vb=vb, j0=j0, nj=nj, c=c: e.dma_start(
                                  out=vb[:, 0:nj, :], in_=v_d[:, j0:j0 + nj, 2 * c * 128:(2 * c + 2) * 128]),
                              reads=[("v_d", jj) for jj in range(j0, j0 + nj)], writes=[("vbuf", kvi)])
                        pairs = []
                        for hh in range(2):
                            for qb in range(4):
                                I = t * 4 + qb
                                for jj in range(nj):
                                    if j0 + jj > I:
                                        break
                                    pairs.append((hh, qb, jj, j0 + jj, I))
                        LA = 2
                        slots_of = {}

                        def emit_S(n):
                            hh, qb, jj, jblk, I = pairs[n]
                            h_ = 2 * c + hh
                            p0 = 64 * hh
                            pi = pt_ctr[0] % 4
                            pt_ctr[0] += 1
                            slots_of[n] = pi
                            sbank = pg[1] if (pi % 2 == 0) else pu[1]
                            stok = ("S", pi)
                            scol = (pi // 2) * 128
                            P.pe(lambda e, jj=jj, qb=qb, p0=p0, sbank=sbank, scol=scol, kb=kb, q=q:
                                 e.matmul(sbank[:, scol:scol + 128], lhsT=kb[p0:p0 + 64, jj * 128:(jj + 1) * 128],
                                          rhs=q[p0:p0 + 64, qb * 128:(qb + 1) * 128], start=True, stop=True, skip_group_check=True),
                                 reads=[("kbuf", kvi), qtok], writes=[stok])
                            P.act(lambda e, sbank=sbank, scol=scol, pi=pi, qb=qb, jblk=jblk, h_=h_:
                                  e.activation(out=pT[pi][:], in_=sbank[:, scol:scol + 128], func=AF.Exp, bias=bias4[qb][:, jblk, h_:h_ + 1]),
                                  reads=[stok, ("bias", qb)], writes=[("pT", pi)])
                            if jblk == I:
                                P.pool(lambda e, pi=pi: e.tensor_tensor(out=pT[pi][:], in0=pT[pi][:], in1=tri_bf, op=ALU.mult),
                                       reads=[("pT", pi), "cb"], writes=[("pT", pi)])

                        def emit_PV(n):
                            hh, qb, jj, jblk, I = pairs[n]
                            pi = slots_of[n]
                            P.pe(lambda e, jj=jj, hh=hh, pi=pi, qb=qb, vb=vb:
                                 e.matmul(po[hh][:, qb * 128:(qb + 1) * 128], lhsT=vb[:, jj, hh * 128:(hh + 1) * 128], rhs=pT[pi][:],
                                          start=False, stop=True, skip_group_check=True),
                                 reads=[("vbuf", kvi), ("pT", pi)], writes=[("po", hh)])

                        for n in range(len(pairs) + LA):
                            if n < len(pairs):
                                emit_S(n)
                            if n - LA >= 0:
                                emit_PV(n - LA)
                    P.dve(lambda e: e.reciprocal(out=rc[0:64, :], in_=po[0][64:128, :]), reads=[("po", 0)], writes=["rc0"])
                    P.dve(lambda e, c=c: e.tensor_tensor(out=ot[0:64, c, :], in0=po[0][0:64, :], in1=rc[0:64, :], op=ALU.mult),
                          reads=[("po", 0), "rc0"], writes=[("ot", c)])
                    P.dve(lambda e: e.reciprocal(out=rc[64:128, :], in_=po[1][0:64, :]), reads=[("po", 1)], writes=["rc1"])
                    P.dve(lambda e, c=c: e.tensor_tensor(out=ot[64:128, c, :], in0=po[1][64:128, :], in1=rc[64:128, :], op=ALU.mult),
                          reads=[("po", 1), "rc1", ("ot", c)], writes=[("ot", c)])
                u2, b2 = out_proj(t, "fox_wout", 0, ot, "ot", 1.0)
                run_units(WOUT, u2, b2, depth=2)
            if DEBUG is not None and FOXDBG:
                DEBUG.append(("call", lambda: call[:, 0:4, :], 64))
                DEBUG.append(("runtot", lambda: runtot[:, 0:5, :], 80))
                DEBUG.append(("bias0", lambda: bias4[0][:, 0:4, :], 64))
                DEBUG.append(("bias3", lambda: bias4[3][:, 0:4, :], 64))
                DEBUG.append(("rc", lambda: rc[:], 512))
                DEBUG.append(("po0", lambda: rs0[:], 512))
                DEBUG.append(("lnv", lambda: lnv[:], 16))
                DEBUG.append(("kbuf0", lambda: kbuf[0][:], 512))
                DEBUG.append(("kbuf1", lambda: kbuf[1][:], 512))
                DEBUG.append(("vbuf0", lambda: vbuf[0][:, 0:2, :], 512))
                DEBUG.append(("vbuf1", lambda: vbuf[1][:, 0:2, :], 512))
                for i_ in range(4):
                    DEBUG.append(("pT%d" % i_, lambda i_=i_: pT[i_][:], 128))
                DEBUG.append(("qn0", lambda: qn[0][:], 512))
                DEBUG.append(("qn1", lambda: qn[1][:], 512))
                DEBUG.append(("ot7", lambda: ot[:, 7, :], 512))
                DEBUG.append(("ot0", lambda: ot[:, 0, :], 512))
                DEBUG.append(("kst0", lambda: kst[0][:], 512))
                DEBUG.append(("vst", lambda: vst[:, 0:512], 512))

        def fox2(l, stack):
            KA = 70
            qb_ = [sb("q%d" % i, [128, TT], BF16, stack) for i in range(4)]
            kst = [sb("kst%d" % i, [128, TT], BF16, stack) for i in range(2)]
            vst = [sb("vst%d" % i, [128, 4 * 128], BF16, stack) for i in range(2)]
            ot = sb("ot", [128, KC, TT], BF16, stack)
            PB = 4
            kbuf = [[sb("kb%d_%d" % (i, hh), [128, PB * 128], BF16, stack) for hh in range(2)] for i in range(2)]
            vbuf = [sb("vbuf%d" % i, [128, PB, 256], BF16, stack) for i in range(2)]
            pT = [sb("pT%d" % i, [128, TT], BF16, stack) for i in range(3)]
            rc = sb("rc", [128, TT], F32, stack)
            sqq = sb("sqq", [128, TT], BF16, stack)
            lnv = sb("lnv", [NH, TT], F32, stack)
            Cc = sb("Cc", [NH, TT], F32, stack)
            cs = sb("cs", [NH, 3, TT], BF16, stack)
            ncs = sb("ncs", [NH, 3, TT], BF16, stack)
            ctot = sb("ctot", [NH, 1], F32, stack)
            nbf = sb("nbf", [NH, 1], F32, stack)
            wf = sb("wf", [128, KC * NH], BF16, stack)
            qg8 = sb("qg8", [128, 1], F32, stack)
            Sb = [pg[1], pu[1], px]
            P.dma("sp", "ld_wf", lambda e: e.dma_start(out=wf[:], in_=ws["fox_wf"][0]), reads=[("ws", "fox_wf", 0)], writes=["wf"])
            P.dve(lambda e: e.tensor_scalar(out=qg8[:], in0=pvc("fqg"), scalar1=DH ** -0.5, scalar2=None, op0=ALU.mult), reads=["pvt"], writes=["qg8"])
            P.dve(lambda e: e.tensor_scalar(out=nbf[:], in0=pvt[0:NH, pvidx["fbfc"]:pvidx["fbfc"] + 1], scalar1=-1.0, scalar2=None, op0=ALU.mult),
                  reads=["pvt"], writes=["nbf"])
            P.pool(lambda e: e.memset(ctot[:], 0.0), writes=["ctot"])
            for i in range(2):
                P.pool(lambda e, i=i: e.memset(vst[i][:], 1.0), writes=[("vst", i)])
            P.pool(lambda e: e.memset(cs[:], 1.0), writes=["cs"])
            for t in range(NT):
                P.dma("sp", "st_ko", lambda e, t=t: e.dma_start(out=kt2_d[:, 67:70, t * TT:(t + 1) * TT], in_=cs[:]), reads=["cs"], writes=[("kt2o", t)])
            for i in range(4):
                P.pool(lambda e, i=i: e.memset(qb_[i][64:67, :], 1.0), writes=[("qaug1", i)])
            q_ctr = [0]
            s_ctr = [0]
            kv_ctr = [0]
            vst_ctr = [0]

            def head_norm(src_ps, src_tok, gain_ap, dsts, square_done=False):
                if not square_done:
                    P.act(lambda e: e.activation(out=sqq[:], in_=src_ps[:], func=AF.Square), reads=[src_tok], writes=["sqq"])
                P.pe(lambda e: e.matmul(pu[0][:], lhsT=bd_bf, rhs=sqq[:], start=True, stop=True), reads=["sqq", "cb"], writes=[("pu", 0)])
                P.act(lambda e: e.activation(out=rs0[:], in_=pu[0][:], func=AF.Ln, scale=1.0 / DH, bias=pvc("eps")),
                      reads=[("pu", 0), "pvt2"], writes=["rs0"])
                P.act(lambda e: e.activation(out=rs1[:], in_=rs0[:], func=AF.Exp, scale=-0.5), reads=["rs0"], writes=["rs1"])
                for (p0, p1, dst_ap, dst_tok) in dsts:
                    P.dve(lambda e, p0=p0, p1=p1, dst_ap=dst_ap: e.scalar_tensor_tensor(out=dst_ap, in0=src_ps[p0:p1, :], scalar=gain_ap[p0:p1, 0:1],
                                                                                         in1=rs1[p0:p1, :], op0=ALU.mult, op1=ALU.mult),
                          reads=[src_tok, "rs1", "qg8", "pvt", "sqq"], writes=[dst_tok])

            for t in range(NT):
                c0 = t * TT
                rmsnorm(t, "mn%d" % l)
                for k in range(KC):
                    P.pe(lambda e, k=k: e.matmul(pu[0][0:NH, :], lhsT=wf[:, k * NH:(k + 1) * NH], rhs=xn[:, k, :], start=(k == 0), stop=(k == KC - 1)),
                         reads=["wf", ("xn", k)], writes=[("pu", 0)])
                P.act(lambda e: e.activation(out=lnv[:], in_=pu[0][0:NH, :], func=AF.Exp, scale=-1.0, bias=nbf[:, 0:1]), reads=[("pu", 0), "nbf"], writes=["lnv"])
                P.act(lambda e: e.activation(out=lnv[:], in_=lnv[:], func=AF.Ln, bias=1.0), reads=["lnv"], writes=["lnv"])
                for qq in range(4):
                    ini = ctot[:, 0:1] if qq == 0 else Cc[:, qq * 128 - 1:qq * 128]
                    P.dve(lambda e, qq=qq, ini=ini: e.tensor_tensor_scan(out=Cc[:, qq * 128:(qq + 1) * 128], data0=ones_f[0:NH, :],
                                                                         data1=lnv[:, qq * 128:(qq + 1) * 128], initial=ini, op0=ALU.mult, op1=ALU.add),
                          reads=["lnv", "cf", "ctot", "Cc"], writes=["Cc"])
                P.dve(lambda e: e.tensor_copy(out=ctot[:], in_=Cc[:, TT - 1:TT]), reads=["Cc"], writes=["ctot"])
                P.dve(lambda e: e.tensor_copy(out=cs[:, 0, :], in_=Cc[:]), reads=["Cc"], writes=["cs"])
                P.dve(lambda e: e.tensor_tensor(out=lnv[:], in0=Cc[:], in1=cs[:, 0, :], op=ALU.subtract), reads=["Cc", "cs"], writes=["lnv"])
                P.dve(lambda e: e.tensor_copy(out=cs[:, 1, :], in_=lnv[:]), reads=["lnv"], writes=["cs"])
                P.dve(lambda e: e.tensor_tensor(out=lnv[:], in0=lnv[:], in1=cs[:, 1, :], op=ALU.subtract), reads=["lnv", "cs"], writes=["lnv"])
                P.dve(lambda e: e.tensor_copy(out=cs[:, 2, :], in_=lnv[:]), reads=["lnv"], writes=["cs"])
                P.dve(lambda e: e.tensor_scalar(out=ncs[:], in0=cs[:], scalar1=-1.0, scalar2=None, op0=ALU.mult), reads=["cs"], writes=["ncs"])
                P.dma("sp", "st_kc", lambda e, c0=c0: e.dma_start(out=kt2_d[:, 64:67, c0:c0 + TT], in_=cs[:]), reads=["cs"], writes=[("kt2c", t)])
                kunits = [(ws["fox_wqk"][8 + c], 1024, ("ws", "fox_wqk", 8 + c), c) for c in range(KC)]

                def kbody(slot, tok, c, c0=c0, t=t):
                    ks = kst[c % 2]
                    pj, pjt = (pg[0], ("pg", 0)) if c % 2 == 0 else (pss, "pss")
                    for k in range(KC):
                        P.pe(lambda e, k=k, slot=slot, pj=pj: e.matmul(pj[:], lhsT=slot[:, k * 128:(k + 1) * 128], rhs=xn[:, k, :],
                                                                      start=(k == 0), stop=(k == KC - 1)),
                             reads=[tok, ("xn", k)], writes=[pjt])
                    head_norm(pj, pjt, pvc("fkg"), [(0, 128, ks[:], ("kst", c % 2))])
                    for hh in range(2):
                        P.dma("sp", "st_k%d" % (c % 2), lambda e, c=c, hh=hh, ks=ks, c0=c0: e.dma_start(out=kt2_d[2 * c + hh][0:64, c0:c0 + TT], in_=ks[64 * hh:64 * hh + 64, :]),
                              reads=[("kst", c % 2)], writes=[("kt2", 2 * c + hh, t)])
                def vbody(slot, tok, g, t=t):
                    for b in range(4):
                        blk = t * 4 + b
                        pb, ptok = (pg[1], ("S", 0)) if b % 2 == 0 else (pu[1], ("S", 1))
                        for k in range(KC):
                            P.pe(lambda e, k=k, slot=slot, pb=pb, b=b: e.matmul(pb[:, 0:256], lhsT=xn[:, k, b * 128:(b + 1) * 128],
                                                                               rhs=slot[:, k * 256:(k + 1) * 256], start=(k == 0), stop=(k == KC - 1)),
                                 reads=[tok, ("xn", k)], writes=[ptok])
                        vi = vst_ctr[0] % 2
                        vst_ctr[0] += 1
                        vs = vst[vi]
                        for i in range(4):
                            d0 = i * 128 + (i % 2) * 64
                            P.dve(lambda e, pb=pb, i=i, d0=d0, vs=vs: e.tensor_copy(out=vs[:, d0:d0 + 64], in_=pb[:, i * 64:(i + 1) * 64]),
                                  reads=[ptok], writes=[("vst", vi)])
                        P.dma("sp", "st_v%d" % vi, lambda e, blk=blk, g=g, vs=vs: e.dma_start(out=v_d[:, blk, g * 512:(g + 1) * 512], in_=vs[:]),
                              reads=[("vst", vi)], writes=[("v_d", blk, g)])

                vunits = [(ws["fox_wv"][g], 2048, ("ws", "fox_wv", g), g) for g in range(4)]
                merged = []
                for c in range(KC):
                    merged.append(kunits[c][:3] + (("k", kunits[c][3]),))
                    if c % 2 == 1:
                        u = vunits[c // 2]
                        merged.append(u[:3] + (("v", u[3]),))

                def kvbody(slot, tok, pl):
                    if pl[0] == "k":
                        kbody(slot, tok, pl[1])
                    else:
                        vbody(slot, tok, pl[1])
                run_units(WIN, merged, kvbody, depth=2)
                nblk = (t + 1) * 4

                def q_make_a(c):
                    slot, tok = WIN.get()
                    if c + 2 < KC:
                        WIN.prefetch(ws["fox_wqk"][c + 2], 1024, ("ws", "fox_wqk", c + 2))
                    pj, pjt = (pg[0], ("pg", 0)) if c % 2 == 0 else (pss, "pss")
                    for k in range(KC):
                        P.pe(lambda e, k=k, slot=slot, pj=pj: e.matmul(pj[:], lhsT=slot[:, k * 128:(k + 1) * 128], rhs=xn[:, k, :],
                                                                      start=(k == 0), stop=(k == KC - 1)),
                             reads=[tok, ("xn", k)], writes=[pjt])
                    P.act(lambda e, pj=pj: e.activation(out=sqq[:], in_=pj[:], func=AF.Square), reads=[pjt], writes=["sqq"])
                    return (pj, pjt)

                def q_make_b(c, pjx):
                    pj, pjt = pjx
                    qi0 = q_ctr[0] % 4
                    qi1 = (q_ctr[0] + 1) % 4
                    q_ctr[0] += 2
                    qA, qB = qb_[qi0], qb_[qi1]
                    head_norm(pj, pjt, qg8, [(0, 64, qA[0:64, :], ("q", qi0)), (64, 128, sqq[64:128, :], "sqq")], square_done=True)
                    P.act(lambda e, qB=qB: e.activation(out=qB[0:64, :], in_=sqq[64:128, :], func=AF.Copy), reads=["sqq"], writes=[("q", qi1)])
                    res = []
                    for hh, (qt, qi) in enumerate(((qA, qi0), (qB, qi1))):
                        h_ = 2 * c + hh
                        P.dma("sp", "ld_qa%d" % qi, lambda e, qt=qt, h_=h_: e.dma_start(out=qt[67:70, :], in_=ncs[h_:h_ + 1, :, :]),
                              reads=["ncs"], writes=[("qaug2", qi)])
                        res.append((qt, [("q", qi), ("qaug1", qi), ("qaug2", qi)]))
                    return res

                def q_make(c):
                    return q_make_b(c, q_make_a(c))

                pieces = []
                for c in range(KC):
                    for j0 in range(0, nblk, PB):
                        pieces.append((c, j0, min(PB, nblk - j0)))
                piece_kvi = {}

                def load_piece(ip):
                    c, j0, nj = pieces[ip]
                    kvi = kv_ctr[0] % 2
                    kv_ctr[0] += 1
                    piece_kvi[ip] = kvi
                    vb = vbuf[kvi]
                    tts = list(range(j0 // 4, (j0 + nj + 3) // 4))
                    for hh in range(2):
                        kb = kbuf[kvi][hh]
                        P.dma("sp", "ld_k%d_%d" % (kvi, hh),
                              lambda e, kb=kb, j0=j0, nj=nj, hd=2 * c + hh: e.dma_start(out=kb[0:KA, 0:nj * 128], in_=kt2_d[hd][:, j0 * 128:(j0 + nj) * 128]),
                              reads=[("kt2", 2 * c + hh, tt_) for tt_ in tts] + [("kt2c", tt_) for tt_ in tts] + [("kt2o", tt_) for tt_ in tts],
                              writes=[("kbuf", kvi, hh)])
                    P.dma("sp", "ld_v%d" % kvi,
                          lambda e, vb=vb, j0=j0, nj=nj, c=c: e.dma_start(out=vb[:, 0:nj, :], in_=v_d[:, j0:j0 + nj, 2 * c * 128:(2 * c + 2) * 128]),
                          reads=[("v_d", jj, c // 2) for jj in range(j0, j0 + nj)], writes=[("vbuf", kvi)])

                WIN.prefetch(ws["fox_wqk"][0], 1024, ("ws", "fox_wqk", 0))
                WIN.prefetch(ws["fox_wqk"][1], 1024, ("ws", "fox_wqk", 1))
                qs_of = {0: q_make(0)}
                qpend = None
                pending = []
                LA = 2

                def emit_S(ip, hh, jj, jblk, lo):
                    c = pieces[ip][0]
                    kvi = piece_kvi[ip]
                    si = s_ctr[0] % 3
                    s_ctr[0] += 1
                    sbank = Sb[si]
                    stok = ("S", si)
                    kb = kbuf[kvi][hh]
                    qt, qtoks = qs_of[c][hh]
                    dg_ = jblk >= 4 * t
                    P.pe(lambda e, jj=jj, lo=lo, sbank=sbank, kb=kb, qt=qt, dg_=dg_:
                         e.matmul(sbank[:, lo:TT], lhsT=kb[0:KA, jj * 128:(jj + 1) * 128], rhs=qt[0:KA, lo:TT], start=True, stop=not dg_,
                                  skip_group_check=True),
                         reads=[("kbuf", kvi, hh)] + qtoks, writes=[stok])
                    if dg_:
                        P.pe(lambda e, lo=lo, sbank=sbank: e.matmul(sbank[:, lo:lo + 128], lhsT=ident_bf, rhs=negm_bf, start=False, stop=True,
                                                                   skip_group_check=True),
                             reads=["cb"], writes=[stok])
                    P.act(lambda e, sbank=sbank, si=si, lo=lo: e.activation(out=pT[si][:, lo:TT], in_=sbank[:, lo:TT], func=AF.Exp),
                          reads=[stok], writes=[("pT", si)])
                    pending.append((c, kvi, hh, jj, jblk, lo, si, ip))

                def emit_norm(c):
                    P.act(lambda e: e.activation(out=rc[0:64, :], in_=po[0][64:128, :], func=AF.Ln), reads=[("po", 0)], writes=["rc0"])
                    P.act(lambda e: e.activation(out=rc[0:64, :], in_=rc[0:64, :], func=AF.Exp, scale=-1.0), reads=["rc0"], writes=["rc0"])
                    P.dve(lambda e, c=c: e.tensor_tensor(out=ot[0:64, c, :], in0=po[0][0:64, :], in1=rc[0:64, :], op=ALU.mult),
                          reads=[("po", 0), "rc0"], writes=[("ot", c)])
                    P.act(lambda e: e.activation(out=rc[64:128, :], in_=po[1][0:64, :], func=AF.Ln), reads=[("po", 1)], writes=["rc1"])
                    P.act(lambda e: e.activation(out=rc[64:128, :], in_=rc[64:128, :], func=AF.Exp, scale=-1.0), reads=["rc1"], writes=["rc1"])
                    P.dve(lambda e, c=c: e.tensor_tensor(out=ot[64:128, c, :], in0=po[1][64:128, :], in1=rc[64:128, :], op=ALU.mult),
                          reads=[("po", 1), "rc1", ("ot", c)], writes=[("ot", c)])

                def emit_PV():
                    c, kvi, hh, jj, jblk, lo, si, _ip = pending.pop(0)
                    vb = vbuf[kvi]
                    P.pe(lambda e, jj=jj, hh=hh, si=si, lo=lo, vb=vb, st=(jblk == 0), last=(jblk == nblk - 1):
                         e.matmul(po[hh][:, lo:TT], lhsT=vb[:, jj, hh * 128:(hh + 1) * 128], rhs=pT[si][:, lo:TT],
                                  start=st, stop=last, skip_group_check=True),
                         reads=[("vbuf", kvi), ("pT", si)], writes=[("po", hh)])
                    if hh == 1 and jblk == nblk - 1:
                        emit_norm(c)

                load_piece(0)
                if len(pieces) > 1:
                    load_piece(1)
                for ip, (c, j0, nj) in enumerate(pieces):
                    if j0 == 0 and c + 1 < KC:
                        qpend = [c + 1, q_make_a(c + 1), 0]
                    nxt_loaded = (ip == 0) or (ip + 1 >= len(pieces))
                    for hh in range(2):
                        for jj in range(nj):
                            jblk = j0 + jj
                            lo = 0 if jblk < 4 * t else 128 * (jblk - 4 * t)
                            emit_S(ip, hh, jj, jblk, lo)
                            if len(pending) > LA:
                                emit_PV()
                            if qpend is not None:
                                qpend[2] += 1
                                if qpend[2] >= 4:
                                    qs_of[qpend[0]] = q_make_b(qpend[0], qpend[1])
                                    qpend = None
                            if not nxt_loaded and all(pe_[7] >= ip for pe_ in pending):
                                load_piece(ip + 1)
                                nxt_loaded = True
                    assert nxt_loaded
                while pending:
                    emit_PV()
                u2, b2 = out_proj(t, "fox_wout", 0, ot, "ot", 1.0)
                run_units(WOUT, u2, b2, depth=2)

        if plan[0][1] in ("f0", "f1"):
            ls0 = plan[0][0] * 2 + (0 if plan[0][1] == "f0" else 1)
            for (a_, b_) in ((0, 2), (2, 6), (6, 14), (14, FC)):
                cast("ffn_win", ls0 * FC + a_, ls0 * FC + b_)
            cast("ffn_wout", ls0 * KC * 2, ls0 * KC * 2 + 4)
            cast("ffn_wout", ls0 * KC * 2 + 4, (ls0 + 1) * KC * 2)
            cast_done.add(("ffn_win", ls0 * FC, (ls0 + 1) * FC))
            cast_done.add(("ffn_wout", ls0 * KC * 2, (ls0 + 1) * KC * 2))
        casts_for(0)
        defer0 = plan[0][1] in ("f0", "f1") and NT >= 4
        if not defer0:
            casts_for(1)
        for pi_, (l, sub) in enumerate(plan):
            tile_hook[0] = None
            if pi_ == 0 and defer0:
                def _hook(t_):
                    if t_ == 0:
                        cast_after[0] = [("X", KC - 1, t_)]
                        casts_for(1)
                        cast_after[0] = ()
                tile_hook[0] = _hook
            if sub == "mix":
                for d_ in ((1, 2, 3, 4) if l % 3 == 1 else (1, 2, 3)):
                    casts_for(pi_ + d_)
            elif not any(s_ == "mix" for (_, s_) in plan):
                casts_for(pi_ + 1)
            with contextlib.ExitStack() as ph:
                if sub == "f0":
                    ffn(l, 0, ph)
                elif sub == "f1":
                    ffn(l, 1, ph)
                else:
                    kind = l % 3
                    if kind == 0:
                        rglru(l, ph)
                    elif kind == 1:
                        (fox2 if USE_FOX2 else fox)(l, ph)
                    else:
                        convmod(l, ph)
                P.barrier()

        fin = []
        if DEBUG:
            dbg_d = nc.dram_tensor("dbg", [len(DEBUG), 128, 512], F32, kind="ExternalOutput").ap()
            for i, (nm, apf, w) in enumerate(DEBUG):
                fin.append(P.dma("pool", "dbg%d" % i, lambda e, i=i, apf=apf, w=w: e.dma_start(out=dbg_d[i][:, 0:w], in_=apf()), writes=[("dbg", i)]))
        for k in range(KC):
            for t in range(NT):
                fin.append(P.dma("sp", "st_x%d_%d" % (k, t % 2),
                                 lambda e, k=k, t=t: e.dma_start(out=out_d[k * 128:(k + 1) * 128, t * TT:(t + 1) * TT],
                                                                  in_=X[:, k, t * TT:(t + 1) * TT]),
                                 reads=[("X", k, t)], writes=[("out", k, t)]))
        P.emit(final_wait_ops=fin)
    return nc, P


FULL_PLAN = [(l, s) for l in range(DEPTH) for s in ("f0", "mix", "f1")]


def prepare_inputs(inp):
    pv = pv_layout(inp)
    w = weight_layouts(inp)
    shared = dict(w)
    shared["pv"] = pv.pack()
    shared.update(const_inputs())
    return shared, pv.idx, pv.n, {k: v.shape for k, v in w.items()}


def kernel(**inputs):
    inp = {k: np.asarray(v) for k, v in inputs.items()}
    x = inp["x"]
    B, S, _ = x.shape
    shared, pvidx, npv, wshapes = prepare_inputs(inp)
    nc, _ = build_program(S, FULL_PLAN, wshapes, npv, pvidx)
    in_maps = []
    for b in range(B):
        m = dict(shared)
        m["xT"] = np.ascontiguousarray(x[b].T)
        in_maps.append(m)
    res = run_bass_kernel_spmd(nc, in_maps, core_ids=list(range(B)))
    out = np.stack([np.ascontiguousarray(res.results[b]["outT"].T) for b in range(B)], axis=0)
    return out.astype(np.float32)
```

```python
import contextlib
import numpy as np
import concourse.bass as bass
import concourse.mybir as mybir
from concourse.bass_utils import run_bass_kernel_spmd

F32 = mybir.dt.float32
BF16 = mybir.dt.bfloat16
AF = mybir.ActivationFunctionType
ALU = mybir.AluOpType

D = 1024
KC = 8
DFF = 2816
FC = 22
TT = 512
EPS = 1e-6
DEPTH = 4
SEQ = 4096
NH = 16
DH = 64
CVK = 31
GELU_C = 1.5957691216057308


class _Op:
    __slots__ = ("eng", "fn", "deps", "signals", "sem", "val", "is_dma", "chan", "idx")

    def __init__(self, eng, fn, is_dma=False, chan=None):
        self.eng = eng
        self.fn = fn
        self.deps = []
        self.signals = False
        self.sem = None
        self.val = 0
        self.is_dma = is_dma
        self.chan = chan


class Prog:
    ENGS = ("pe", "act", "dve", "pool", "sp")

    def __init__(self, nc):
        self.nc = nc
        self.ops = []
        self.last_writer = {}
        self.readers = {}
        self.chans = {}
        self.last_on_eng = {}
        self.dmas_since_barrier = []
        self.pending_barrier = {}
        self.exempt = False

    def op(self, eng, fn, reads=(), writes=(), dma_chan=None):
        o = _Op(eng, fn, is_dma=dma_chan is not None, chan=dma_chan)
        deps = []
        for r in reads:
            w = self.last_writer.get(r)
            if w is not None:
                deps.append(w)
        for w_ in writes:
            w = self.last_writer.get(w_)
            if w is not None:
                deps.append(w)
            deps.extend(self.readers.get(w_, ()))
        if not (self.exempt or eng == "pe"):
            pb = self.pending_barrier.pop(eng, None)
            if pb:
                deps.extend(pb)
        seen = set()
        for d in deps:
            if id(d) in seen or d is o:
                continue
            seen.add(id(d))
            if d.eng == "pe" and eng == "pe" and not d.is_dma and not o.is_dma:
                continue
            o.deps.append(d)
            d.signals = True
        for r in reads:
            lst = self.readers.get(r)
            if lst is None:
                self.readers[r] = [o]
            else:
                if not o.is_dma:
                    lst[:] = [x for x in lst if x.is_dma or x.eng != eng]
                lst.append(o)
        for w_ in writes:
            self.last_writer[w_] = o
            self.readers[w_] = []
        o.idx = len(self.ops)
        self.ops.append(o)
        if o.is_dma:
            self.dmas_since_barrier.append(o)
        else:
            self.last_on_eng[eng] = o
        return o

    def barrier(self):
        deps = [o for o in self.last_on_eng.values()] + list(self.dmas_since_barrier)
        self.dmas_since_barrier = []
        for e in self.ENGS:
            self.pending_barrier[e] = list(deps) + self.pending_barrier.get(e, [])

    def pe(self, fn, reads=(), writes=()):
        return self.op("pe", fn, reads, writes)

    def act(self, fn, reads=(), writes=()):
        return self.op("act", fn, reads, writes)

    def dve(self, fn, reads=(), writes=()):
        return self.op("dve", fn, reads, writes)

    def pool(self, fn, reads=(), writes=()):
        return self.op("pool", fn, reads, writes)

    def dma(self, queue, chan, fn, reads=(), writes=()):
        return self.op(queue, fn, reads, writes, dma_chan=chan)

    def emit(self, final_wait_ops=()):
        nc = self.nc
        with contextlib.ExitStack() as es:
            eng_sem = {e: es.enter_context(nc.semaphore("sem_" + e)) for e in self.ENGS}
            chan_names = []
            for o in self.ops:
                if o.is_dma and o.chan not in self.chans:
                    self.chans[o.chan] = None
                    chan_names.append(o.chan)
            for i, c in enumerate(chan_names):
                self.chans[c] = es.enter_context(nc.semaphore("dch%d" % i))
            cnt = {e: 0 for e in self.ENGS}
            ccnt = {c: 0 for c in self.chans}
            for o in self.ops:
                if o.is_dma:
                    ccnt[o.chan] += 16
                    o.sem = self.chans[o.chan]
                    o.val = ccnt[o.chan]
                elif o.signals:
                    cnt[o.eng] += 1
                    o.sem = eng_sem[o.eng]
                    o.val = cnt[o.eng]
            run_c = {c: 0 for c in self.chans}
            dma_need = {}
            for o in self.ops:
                for d in o.deps:
                    if d.is_dma:
                        dma_need[(o.idx, d.chan)] = run_c[d.chan]
                if o.is_dma:
                    run_c[o.chan] += 16
            per_eng = {e: [] for e in self.ENGS}
            for o in self.ops:
                per_eng[o.eng].append(o)
            self.stats = {e: len(per_eng[e]) for e in self.ENGS}
            self.stats["sig"] = dict(cnt)
            block = es.enter_context(nc.Block())

            def run(engname, eng):
                waited = {}
                for o in per_eng[engname]:
                    need = {}
                    for d in o.deps:
                        k = id(d.sem)
                        v = dma_need[(o.idx, d.chan)] if d.is_dma else d.val
                        if k not in need or need[k][1] < v:
                            need[k] = (d.sem, v)
                    for k, (s, v) in need.items():
                        if waited.get(k, 0) >= v:
                            continue
                        eng.wait_ge(s, v)
                        waited[k] = v
                    inst = o.fn(eng)
                    if o.is_dma:
                        inst.then_inc(o.sem, 16)
                    elif o.signals:
                        inst.then_inc(o.sem, 1)
                if engname == "sp":
                    for o in final_wait_ops:
                        eng.wait_ge(o.sem, o.val)

            @block.tensor
            def _(e):
                run("pe", e)

            @block.scalar
            def _(e):
                run("act", e)

            @block.vector
            def _(e):
                run("dve", e)

            @block.gpsimd
            def _(e):
                run("pool", e)

            @block.sync
            def _(e):
                run("sp", e)


class Stream:
    def __init__(self, P, name, slots):
        self.P = P
        self.name = name
        self.slots = slots
        self.n = len(slots)
        self.issued = 0
        self.taken = 0

    def prefetch(self, src_ap, ncols, src_token):
        i = self.issued % self.n
        slot = self.slots[i]
        ex = self.P.exempt
        self.P.exempt = True
        self.P.dma("sp", "%s%d" % (self.name, i),
                   lambda e, slot=slot, src_ap=src_ap, ncols=ncols: e.dma_start(out=slot[:, 0:ncols], in_=src_ap),
                   reads=[src_token], writes=[(self.name, i)])
        self.P.exempt = ex
        self.issued += 1

    def get(self):
        assert self.taken < self.issued
        i = self.taken % self.n
        self.taken += 1
        return self.slots[i], (self.name, i)


def run_units(stream, units, body, depth, after=None):
    n = len(units)
    for i in range(min(depth, n)):
        stream.prefetch(*units[i][:3])
    for i in range(n):
        slot, tok = stream.get()
        body(slot, tok, units[i][3])
        if i + depth < n:
            stream.prefetch(*units[i + depth][:3])
        if after and i in after:
            after[i]()


class PV:
    def __init__(self):
        self.cols = []
        self.idx = {}
        self.n = 0

    def add(self, name, arr2d):
        arr2d = np.ascontiguousarray(arr2d, dtype=np.float32)
        assert arr2d.shape[0] == 128
        self.idx[name] = self.n
        self.n += arr2d.shape[1]
        self.cols.append(arr2d)

    def pack(self):
        return np.ascontiguousarray(np.concatenate(self.cols, axis=1))


def fm(v):
    return np.asarray(v, np.float32).reshape(KC, 128).T


def pv_layout(inp):
    pv = PV()
    for l in range(DEPTH):
        pv.add("fn%d0" % l, fm(inp["ffn_norm"][l, 0]))
        pv.add("fn%d1" % l, fm(inp["ffn_norm"][l, 1]))
        pv.add("mn%d" % l, fm(inp["mix_norm"][l]))
    for j in range(inp["rg_w_in"].shape[0]):
        for k in range(4):
            pv.add("rgcw%d_%d" % (j, k), fm(inp["rg_conv_w"][j, k]))
        pv.add("rgcb%d" % j, fm(inp["rg_conv_b"][j]))
        pv.add("rgba%d" % j, fm(inp["rg_b_a"][j]))
        pv.add("rgbx%d" % j, fm(inp["rg_b_x"][j]))
        pv.add("rglam%d" % j, fm(inp["rg_lambda"][j]))
    g2 = lambda v: np.concatenate([v, v]).reshape(128, 1)
    pv.add("fqg", g2(inp["fox_q_norm"][0]))
    pv.add("fkg", g2(inp["fox_k_norm"][0]))
    pv.add("fbf", np.broadcast_to(inp["fox_b_f"][0][None, :], (128, NH)))
    pv.add("fbfc", np.pad(inp["fox_b_f"][0], (0, 128 - NH)).reshape(128, 1))
    pv.add("cvbin", fm(inp["cv_b_in"][0][:D]))
    pv.add("cvbing", fm(inp["cv_b_in"][0][D:]))
    for k in range(CVK):
        pv.add("cvdw%d" % k, fm(inp["cv_dw_w"][0, k]))
    pv.add("cvdwb", fm(inp["cv_dw_b"][0]))
    pv.add("cvlng", fm(inp["cv_ln_g"][0]))
    pv.add("cvlnb", fm(inp["cv_ln_b"][0]))
    pv.add("cvbout", fm(inp["cv_b_out"][0]))
    return pv


def weight_layouts(inp):
    w = {}
    a = inp["ffn_w_in"].reshape(DEPTH * 2, KC, 128, 2, FC, 128)
    w["ffn_win"] = np.ascontiguousarray(a.transpose(0, 4, 2, 1, 3, 5)).reshape(DEPTH * 2 * FC, 128, 2048)
    a = inp["ffn_w_out"].reshape(DEPTH * 2, 2, FC // 2, 128, KC, 128)
    w["ffn_wout"] = np.ascontiguousarray(a.transpose(0, 4, 1, 3, 2, 5)).reshape(DEPTH * 2 * KC * 2, 128, (FC // 2) * 128)

    def in2(x):
        n = x.shape[0]
        a = x.reshape(n, KC, 128, 2, KC, 128)
        return np.ascontiguousarray(a.transpose(0, 4, 2, 1, 3, 5)).reshape(n * KC, 128, 2048)

    def out1(x):
        n = x.shape[0]
        a = x.reshape(n, KC, 128, KC, 128)
        return np.ascontiguousarray(a.transpose(0, 3, 2, 1, 4)).reshape(n * KC, 128, 1024)

    w["rg_win"] = in2(inp["rg_w_in"])
    w["rg_wout"] = out1(inp["rg_w_out"])
    w["cv_win"] = in2(inp["cv_w_in"])
    w["cv_wout"] = out1(inp["cv_w_out"])
    w["fox_wout"] = out1(inp["fox_w_out"])
    fw = inp["fox_w_in"][0]
    a = fw[:, 0:2048].reshape(KC, 128, 16, 128)
    w["fox_wqk"] = np.ascontiguousarray(a.transpose(2, 1, 0, 3)).reshape(16, 128, 1024)
    a = fw[:, 2048:3072].reshape(KC, 128, 4, 256)
    w["fox_wv"] = np.ascontiguousarray(a.transpose(2, 1, 0, 3)).reshape(4, 128, 2048)
    a = fw[:, 3072:3088].reshape(KC, 128, NH)
    w["fox_wf"] = np.ascontiguousarray(a.transpose(1, 0, 2)).reshape(1, 128, KC * NH)
    for nm, key in (("rg_wa", "rg_w_a"), ("rg_wx", "rg_w_x")):
        x = inp[key]
        n = x.shape[0]
        w[nm] = np.ascontiguousarray(x.transpose(2, 0, 1, 3)).reshape(1, 128, n * KC * 128)
    return w


def const_inputs():
    c = {}
    tri = (np.arange(128)[:, None] <= np.arange(128)[None, :]).astype(np.float32)
    bd = np.zeros((128, 128), np.float32)
    bd[:64, :64] = 1.0
    bd[64:, 64:] = 1.0
    c["c_f32"] = np.ascontiguousarray(np.concatenate([np.ones((128, 128), np.float32), np.eye(128, dtype=np.float32), tri, bd,
                                                     (1.0 - tri) * np.float32(-30000.0)], axis=1))
    return c


DEBUG = []
FOXDBG = False
USE_FOX2 = True


def build_program(S, plan, wshapes, npv, pvidx):
    nc = bass.Bass("TRN2", target_bir_lowering=False)
    NT = S // TT
    NB = S // 128
    xT_d = nc.dram_tensor("xT", [D, S], F32, kind="ExternalInput").ap()
    out_d = nc.dram_tensor("outT", [D, S], F32, kind="ExternalOutput").ap()
    pv_d = nc.dram_tensor("pv", [128, npv], F32, kind="ExternalInput").ap()
    cf_d = nc.dram_tensor("c_f32", [128, 640], F32, kind="ExternalInput").ap()
    wd = {}
    ws = {}
    for nm, shp in wshapes.items():
        wd[nm] = nc.dram_tensor(nm, list(shp), F32, kind="ExternalInput").ap()
        ws[nm] = nc.dram_tensor(nm + "_bf", list(shp), BF16).ap()
    need_fox = any(sub == "mix" and l % 3 == 1 for l, sub in plan)
    if need_fox:
        kt_d = nc.dram_tensor("kt_s", [KC, 128, S], BF16).ap()
        kt2_d = nc.dram_tensor("kt2_s", [NH, 70, S], BF16).ap()
        v_d = nc.dram_tensor("v_s", [128, NB, NH * 128], BF16).ap()

    with contextlib.ExitStack() as es:
        sb_ctr = [0]

        def sb(name, shape, dt, stack=es):
            sb_ctr[0] += 1
            return stack.enter_context(nc.sbuf_tensor("%s_%d" % (name, sb_ctr[0]), shape, dt))

        X = sb("X", [128, KC, S], F32)
        xn = sb("xn", [128, KC, TT], BF16)
        sq = sb("sq", [128, 2, TT], BF16)
        rs0 = sb("rs0", [128, TT], F32)
        rs1 = sb("rs1", [128, TT], F32)
        pvt = sb("pvt", [128, npv], F32)
        cf = sb("cf", [128, 256], F32)
        cb = sb("cb", [128, 640], BF16)
        win_slots = [sb("win%d" % i, [128, 2048], BF16) for i in range(3)]
        wout_slots = [sb("wout%d" % i, [128, (FC // 2) * 128], BF16) for i in range(3)]
        psb = [es.enter_context(nc.psum_tensor("ps%d" % i, [128, 512], F32)) for i in range(8)]
        pss, pg, pu, po, px = psb[0], psb[1:3], psb[3:5], psb[5:7], psb[7]
        ones_bf = cb[:, 0:128]
        tri_bf = cb[:, 256:384]
        bd_bf = cb[:, 384:512]
        negm_bf = cb[:, 512:640]
        ident_bf = cb[:, 128:256]
        ones_f = cf[:, 0:128]
        ident_f = cf[:, 128:256]
        tri_f = None

        P = Prog(nc)
        WIN = Stream(P, "win", win_slots)
        WOUT = Stream(P, "wout", wout_slots)

        epsn = sb("epsn", [128, 1], F32)

        def pvc(name, k=0, w=1):
            if name == "eps":
                return epsn[:, 0:1]
            c0 = pvidx[name] + k
            return pvt[:, c0:c0 + w]

        P.pool(lambda e: e.memset(epsn[:], EPS), writes=["pvt2"])
        P.dma("sp", "ld_pv", lambda e: e.dma_start(out=pvt[:], in_=pv_d), writes=["pvt"])
        P.dma("sp", "ld_cf", lambda e: e.dma_start(out=cf[:], in_=cf_d[:, 0:256]), writes=["cf"])
        P.dma("pool", "ld_cb", lambda e: e.dma_start(out=cb[:], in_=cf_d), writes=["cb"])
        for t in range(NT):
            for k in range(KC):
                P.dma("sp", "ld_xt%d" % t,
                      lambda e, k=k, t=t: e.dma_start(out=X[:, k, t * TT:(t + 1) * TT],
                                                       in_=xT_d[k * 128:(k + 1) * 128, t * TT:(t + 1) * TT]),
                      writes=[("X", k, t)])

        cast_done = set()
        cast_ctr = [0]

        cast_after = [()]

        def cast(nm, lo, hi):
            key = (nm, lo, hi)
            if key in cast_done:
                return
            cast_done.add(key)
            shp = wshapes[nm]
            ncol = shp[2]
            piece = 1024 if ncol % 256 == 0 else 1408
            if (hi - lo) > 1 and (hi - lo) * 128 * ((ncol * 4 + piece - 1) // piece) >= 16000:
                mid = (lo + hi) // 2
                cast(nm, lo, mid)
                cast(nm, mid, hi)
                return
            src = wd[nm][lo:hi]
            dst = ws[nm][lo:hi]
            ch = "cast%d" % cast_ctr[0]
            cast_ctr[0] += 1
            kw = dict(max_dma_last_dim=piece)
            ex = P.exempt
            P.exempt = True
            P.dma("pool", ch, lambda e, src=src, dst=dst, kw=kw: e.dma_start(out=dst, in_=src, **kw),
                  reads=list(cast_after[0]), writes=[("ws", nm, u) for u in range(lo, hi)])
            P.exempt = ex

        tile_hook = [None]

        def casts_for(i):
            if i >= len(plan):
                return
            l, sub = plan[i]
            if sub in ("f0", "f1"):
                ls = l * 2 + (0 if sub == "f0" else 1)
                cast("ffn_win", ls * FC, (ls + 1) * FC)
                cast("ffn_wout", ls * KC * 2, (ls + 1) * KC * 2)
            else:
                kind = l % 3
                j = l // 3
                if kind == 0:
                    cast("rg_win", j * KC, (j + 1) * KC)
                    cast("rg_wa", 0, 1)
                    cast("rg_wx", 0, 1)
                    cast("rg_wout", j * KC, (j + 1) * KC)
                elif kind == 1:
                    cast("fox_wqk", 0, 16)
                    cast("fox_wv", 0, 4)
                    cast("fox_wf", 0, 1)
                    cast("fox_wout", 0, KC)
                else:
                    cast("cv_win", 0, KC)
                    cast("cv_wout", 0, KC)

        def xtoks(t):
            return [("X", k, t) for k in range(KC)]

        def rmsnorm(t, gname, xo=None, xtok="xn"):
            if xo is None:
                xo = xn
            ex = P.exempt
            P.exempt = (xo is xn)
            c0, c1 = t * TT, (t + 1) * TT
            for k in range(KC):
                P.act(lambda e, k=k: e.activation(out=sq[:, k % 2, :], in_=X[:, k, c0:c1], func=AF.Square),
                      reads=[("X", k, t)], writes=[("sq", k % 2)])
                P.pe(lambda e, k=k: e.matmul(pss[:], lhsT=ones_bf, rhs=sq[:, k % 2, :], start=(k == 0), stop=(k == KC - 1)),
                     reads=[("sq", k % 2), "cb"], writes=["pss"])
            P.act(lambda e: e.activation(out=rs0[:], in_=pss[:], func=AF.Sqrt, scale=1.0 / D, bias=pvc("eps")),
                  reads=["pss", "pvt2"], writes=["rs0"])
            P.dve(lambda e: e.reciprocal(out=rs1[:], in_=rs0[:]), reads=["rs0"], writes=["rs1"])
            for k in range(KC):
                P.dve(lambda e, k=k: e.scalar_tensor_tensor(out=xo[:, k, :], in0=X[:, k, c0:c1], scalar=pvc(gname, k),
                                                            in1=rs1[:], op0=ALU.mult, op1=ALU.mult),
                      reads=[("X", k, t), "rs1", "pvt"], writes=[(xtok, k)])
            P.exempt = ex

        def resid_add(m, t, pbank, ptok, scale):
            c0, c1 = t * TT, (t + 1) * TT
            P.dve(lambda e: e.scalar_tensor_tensor(out=X[:, m, c0:c1], in0=pbank[:], scalar=float(scale),
                                                   in1=X[:, m, c0:c1], op0=ALU.mult, op1=ALU.add),
                  reads=[ptok, ("X", m, t)], writes=[("X", m, t)])

        def out_proj(t, wname, ubase, rhs_tile, rhs_tok, scale, nk=KC, bias_name=None):
            units = [(ws[wname][ubase + m], nk * 128, ("ws", wname, ubase + m), m) for m in range(KC)]

            def body(slot, tok, m):
                pb = po[m % 2]
                ptok = ("po", m % 2)
                for k in range(nk):
                    P.pe(lambda e, k=k, slot=slot, pb=pb: e.matmul(pb[:], lhsT=slot[:, k * 128:(k + 1) * 128], rhs=rhs_tile[:, k, :],
                                                                  start=(k == 0), stop=(k == nk - 1)),
                         reads=[tok, (rhs_tok, k)], writes=[ptok])
                if bias_name is not None:
                    c0, c1 = t * TT, (t + 1) * TT
                    P.dve(lambda e, m=m, pb=pb: e.scalar_tensor_tensor(out=X[:, m, c0:c1], in0=pb[:], scalar=pvc(bias_name, m),
                                                                       in1=X[:, m, c0:c1], op0=ALU.add, op1=ALU.add),
                          reads=[ptok, ("X", m, t), "pvt"], writes=[("X", m, t)])
                else:
                    resid_add(m, t, pb, ptok, scale)
            return units, body

        def ffn(l, s, stack):
            ls = l * 2 + s
            h = sb("h", [128, FC, TT], BF16, stack)
            sg = [sb("sg%d" % i, [128, TT], F32, stack) for i in range(2)]
            xn2 = sb("xn2", [128, KC, TT], BF16, stack)
            xnb = [(xn, "xn"), (xn2, "xn2")]
            gname = "fn%d%d" % (l, s)
            rmsnorm(0, gname, *xnb[0])
            for t in range(NT):
                xcur, xtk = xnb[t % 2]
                units = [(ws["ffn_win"][ls * FC + c], 2048, ("ws", "ffn_win", ls * FC + c), c) for c in range(FC)]

                def body(slot, tok, c, xcur=xcur, xtk=xtk):
                    b = c % 2
                    for g, pb, pn in ((0, pg[b], "pg"), (1, pu[b], "pu")):
                        for k in range(KC):
                            off = (k * 2 + g) * 128
                            P.pe(lambda e, k=k, off=off, pb=pb, slot=slot: e.matmul(pb[:], lhsT=slot[:, off:off + 128], rhs=xcur[:, k, :],
                                                                                    start=(k == 0), stop=(k == KC - 1)),
                                 reads=[tok, (xtk, k)], writes=[(pn, b)])
                    P.act(lambda e, b=b: e.activation(out=sg[b][:], in_=pg[b][:], func=AF.Silu),
                          reads=[("pg", b)], writes=[("sg", b)])
                    P.dve(lambda e, b=b, c=c: e.tensor_tensor(out=h[:, c, :], in0=sg[b][:], in1=pu[b][:], op=ALU.mult),
                          reads=[("sg", b), ("pu", b)], writes=[("h", c)])
                run_units(WIN, units, body, depth=2)
                HF = FC // 2
                units2 = [(ws["ffn_wout"][(ls * KC + m) * 2 + hf], HF * 128, ("ws", "ffn_wout", (ls * KC + m) * 2 + hf), (m, hf))
                          for m in range(KC) for hf in range(2)]

                def body2(slot, tok, mh, t=t):
                    m, hf = mh
                    pb = po[m % 2]
                    ptok = ("po", m % 2)
                    for ci in range(HF):
                        c = hf * HF + ci
                        P.pe(lambda e, c=c, ci=ci, slot=slot, pb=pb: e.matmul(pb[:], lhsT=slot[:, ci * 128:(ci + 1) * 128], rhs=h[:, c, :],
                                                                             start=(c == 0), stop=(c == FC - 1)),
                             reads=[tok, ("h", c)], writes=[ptok])
                    if hf == 1:
                        resid_add(m, t, pb, ptok, 0.5)
                hook = None
                if t + 1 < NT:
                    nxt = xnb[(t + 1) % 2]
                    hook = {3: (lambda t=t, nxt=nxt: rmsnorm(t + 1, gname, *nxt))}
                run_units(WOUT, units2, body2, depth=2, after=hook)
                if tile_hook[0] is not None:
                    tile_hook[0](t)

        def rglru(l, stack):
            j = l // 3
            f32t = lambda nm, w=TT: [sb(nm + str(i), [128, w], F32, stack) for i in range(2)]
            uext = f32t("uext", TT + 3)
            acc = f32t("acc")
            ra = f32t("ra")
            ib = f32t("ib")
            hh = f32t("hh")
            gt = f32t("gt")
            ub = [sb("ub%d" % i, [128, TT], BF16, stack) for i in range(2)]
            yb = sb("yb", [128, KC, TT], BF16, stack)
            ucar = sb("ucar", [128, KC, 3], F32, stack)
            hst = sb("hst", [128, KC], F32, stack)
            cl = sb("cl", [128, KC], F32, stack)
            wg = sb("wg", [128, 2, KC * 128], BF16, stack)
            Rb = [(px, "px"), (po[0], ("po", 0))]
            Ib = [(pss, "pss"), (po[1], ("po", 1))]
            P.dma("sp", "ld_wg0", lambda e: e.dma_start(out=wg[:, 0, :], in_=ws["rg_wa"][0][:, j * KC * 128:(j + 1) * KC * 128]),
                  reads=[("ws", "rg_wa", 0)], writes=["wg0"])
            P.dma("sp", "ld_wg1", lambda e: e.dma_start(out=wg[:, 1, :], in_=ws["rg_wx"][0][:, j * KC * 128:(j + 1) * KC * 128]),
                  reads=[("ws", "rg_wx", 0)], writes=["wg1"])
            P.act(lambda e: e.activation(out=cl[:], in_=pvc("rglam%d" % j, 0, KC), func=AF.Exp, scale=-1.0), reads=["pvt"], writes=["cl"])
            P.act(lambda e: e.activation(out=cl[:], in_=cl[:], func=AF.Ln, bias=1.0), reads=["cl"], writes=["cl"])
            P.dve(lambda e: e.tensor_scalar(out=cl[:], in0=cl[:], scalar1=-8.0, scalar2=None, op0=ALU.mult), reads=["cl"], writes=["cl"])
            P.pool(lambda e: e.memset(ucar[:], 0.0), writes=["ucar"])
            P.pool(lambda e: e.memset(hst[:], 0.0), writes=["hst"])

            def st_proj(c, b, slot, tok):
                for g, pb, pn in ((0, pg[b], "pg"), (1, pu[b], "pu")):
                    for k in range(KC):
                        off = (k * 2 + g) * 128
                        P.pe(lambda e, k=k, off=off, pb=pb, slot=slot: e.matmul(pb[:], lhsT=slot[:, off:off + 128], rhs=xn[:, k, :],
                                                                                start=(k == 0), stop=(k == KC - 1)),
                             reads=[tok, ("xn", k)], writes=[(pn, b)])

            def st_u(c, b):
                P.pool(lambda e: e.tensor_copy(out=uext[b][:, 0:3], in_=ucar[:, c, :]), reads=[("ucar", c)], writes=[("uext_c", b)])
                P.act(lambda e: e.activation(out=uext[b][:, 3:TT + 3], in_=pu[b][:], func=AF.Copy), reads=[("pu", b)], writes=[("uext", b)])
                P.act(lambda e: e.activation(out=acc[b][:], in_=pu[b][:], func=AF.Identity, scale=pvc("rgcw%d_3" % j, c), bias=pvc("rgcb%d" % j, c)),
                      reads=[("pu", b), "pvt"], writes=[("acc", b)])
                P.pool(lambda e: e.tensor_copy(out=ucar[:, c, :], in_=uext[b][:, TT:TT + 3]), reads=[("uext", b), ("uext_c", b)], writes=[("ucar", c)])

            def st_tap(c, b, kk):
                P.dve(lambda e: e.scalar_tensor_tensor(out=acc[b][:], in0=uext[b][:, kk:kk + TT], scalar=pvc("rgcw%d_%d" % (j, kk), c),
                                                       in1=acc[b][:], op0=ALU.mult, op1=ALU.add),
                      reads=[("uext", b), ("uext_c", b), ("acc", b), "pvt"], writes=[("acc", b)])

            def st_ub(c, b):
                P.act(lambda e: e.activation(out=ub[b][:], in_=acc[b][:], func=AF.Copy), reads=[("acc", b)], writes=[("ub", b)])

            def st_gates(c, b):
                P.pe(lambda e: e.matmul(Rb[b][0][:], lhsT=wg[:, 0, c * 128:(c + 1) * 128], rhs=ub[b][:], start=True, stop=True),
                     reads=["wg0", ("ub", b)], writes=[Rb[b][1]])
                P.pe(lambda e: e.matmul(Ib[b][0][:], lhsT=wg[:, 1, c * 128:(c + 1) * 128], rhs=ub[b][:], start=True, stop=True),
                     reads=["wg1", ("ub", b)], writes=[Ib[b][1]])

            def st_sig_r(c, b):
                P.act(lambda e: e.activation(out=ra[b][:], in_=Rb[b][0][:], func=AF.Sigmoid, bias=pvc("rgba%d" % j, c)),
                      reads=[Rb[b][1], "pvt"], writes=[("ra", b)])

            def st_sig_i(c, b):
                P.act(lambda e: e.activation(out=ib[b][:], in_=Ib[b][0][:], func=AF.Sigmoid, bias=pvc("rgbx%d" % j, c)),
                      reads=[Ib[b][1], "pvt"], writes=[("ib", b)])

            def st_a(c, b):
                P.act(lambda e: e.activation(out=ra[b][:], in_=ra[b][:], func=AF.Exp, scale=cl[:, c:c + 1]), reads=[("ra", b), "cl"], writes=[("ra", b)])

            def st_a2(c, b):
                P.act(lambda e: e.activation(out=uext[b][:, 0:TT], in_=ra[b][:], func=AF.Square), reads=[("ra", b), ("acc", b), ("ucar", c)],
                      writes=[("uext", b), ("uext_c", b)])

            def st_m(c, b):
                P.act(lambda e: e.activation(out=uext[b][:, 0:TT], in_=uext[b][:, 0:TT], func=AF.Sqrt, scale=-1.0, bias=1.0),
                      reads=[("uext", b), ("uext_c", b)], writes=[("uext", b), ("uext_c", b)])

            def st_b1(c, b):
                P.dve(lambda e: e.tensor_tensor(out=ib[b][:], in0=ib[b][:], in1=acc[b][:], op=ALU.mult), reads=[("ib", b), ("acc", b)], writes=[("ib", b)])

            def st_b2(c, b):
                P.dve(lambda e: e.tensor_tensor(out=ib[b][:], in0=ib[b][:], in1=uext[b][:, 0:TT], op=ALU.mult),
                      reads=[("ib", b), ("uext", b), ("uext_c", b)], writes=[("ib", b)])

            def st_scan(c, b):
                P.dve(lambda e: e.tensor_tensor_scan(out=hh[b][:], data0=ra[b][:], data1=ib[b][:], initial=hst[:, c:c + 1], op0=ALU.mult, op1=ALU.add),
                      reads=[("ra", b), ("ib", b), ("hst", c)], writes=[("hh", b)])
                P.pool(lambda e: e.tensor_copy(out=hst[:, c:c + 1], in_=hh[b][:, TT - 1:TT]), reads=[("hh", b)], writes=[("hst", c)])

            def st_g1(c, b):
                P.act(lambda e: e.activation(out=gt[b][:], in_=pg[b][:], func=AF.Square), reads=[("pg", b)], writes=[("gt", b)])

            def st_g2(c, b):
                P.act(lambda e: e.activation(out=gt[b][:], in_=gt[b][:], func=AF.Identity, scale=0.044715, bias=1.0), reads=[("gt", b)], writes=[("gt", b)])

            def st_g3(c, b):
                P.dve(lambda e: e.tensor_tensor(out=gt[b][:], in0=gt[b][:], in1=pg[b][:], op=ALU.mult), reads=[("gt", b), ("pg", b)], writes=[("gt", b)])

            def st_g4(c, b):
                P.act(lambda e: e.activation(out=gt[b][:], in_=gt[b][:], func=AF.Sigmoid, scale=GELU_C), reads=[("gt", b)], writes=[("gt", b)])

            def st_g5(c, b):
                P.dve(lambda e: e.tensor_tensor(out=gt[b][:], in0=gt[b][:], in1=pg[b][:], op=ALU.mult), reads=[("gt", b), ("pg", b)], writes=[("gt", b)])

            def st_y(c, b):
                P.dve(lambda e: e.tensor_tensor(out=yb[:, c, :], in0=gt[b][:], in1=hh[b][:], op=ALU.mult), reads=[("gt", b), ("hh", b)], writes=[("yb", c)])

            stages = [st_u, st_g1, lambda c, b: st_tap(c, b, 0), st_g2, lambda c, b: st_tap(c, b, 1), lambda c, b: st_tap(c, b, 2),
                      st_ub, st_g3, st_gates, st_g4, st_sig_r, st_sig_i, st_a, st_g5, st_a2, st_b1, st_m, st_b2, st_scan, st_y]

            for t in range(NT):
                rmsnorm(t, "mn%d" % l)
                for c2 in range(0, KC, 2):
                    cs_ = (c2, c2 + 1)
                    got = []
                    if t == 0 and c2 == 0:
                        for c in cs_:
                            WIN.prefetch(ws["rg_win"][j * KC + c], 2048, ("ws", "rg_win", j * KC + c))
                    for c in cs_:
                        got.append(WIN.get())
                    for c, (slot, tok) in zip(cs_, got):
                        st_proj(c, c % 2, slot, tok)
                    nxt = None
                    if c2 + 2 < KC:
                        nxt = (c2 + 2, c2 + 3)
                    elif t + 1 < NT:
                        nxt = (0, 1)
                    if nxt:
                        for c in nxt:
                            WIN.prefetch(ws["rg_win"][j * KC + c], 2048, ("ws", "rg_win", j * KC + c))
                    for stg in stages:
                        for c in cs_:
                            stg(c, c % 2)
                u2, b2 = out_proj(t, "rg_wout", j * KC, yb, "yb", 1.0)
                run_units(WOUT, u2, b2, depth=2)

        def convmod(l, stack):
            f32t = lambda nm, w=TT: sb(nm, [128, w], F32, stack)
            PAD = CVK - 1
            gext = [sb("gext%d" % i, [128, TT + PAD], BF16, stack) for i in range(2)]
            sgm = f32t("sgm")
            cvo = sb("cvo", [128, KC, TT], F32, stack)
            mu = f32t("mu")
            rstd = f32t("rstd")
            ybc = xn
            gcar = sb("gcar", [128, KC, PAD], BF16, stack)
            dg = sb("dg", [128, CVK, 128], BF16, stack)
            P.pool(lambda e: e.memset(gcar[:], 0.0), writes=["gcar"])
            for t in range(NT):
                rmsnorm(t, "mn%d" % l)
                units = [(ws["cv_win"][c], 2048, ("ws", "cv_win", c), c) for c in range(KC)]

                def body(slot, tok, c):
                    b = c % 2
                    T_ = lambda nm: (nm, b)
                    for g, pb, pn in ((0, pg[b], "pg"), (1, pu[b], "pu")):
                        for k in range(KC):
                            off = (k * 2 + g) * 128
                            P.pe(lambda e, k=k, off=off, pb=pb, slot=slot: e.matmul(pb[:], lhsT=slot[:, off:off + 128], rhs=xn[:, k, :],
                                                                                    start=(k == 0), stop=(k == KC - 1)),
                                 reads=[tok, ("xn", k)], writes=[(pn, b)])
                    P.act(lambda e, c=c, b=b: e.activation(out=sgm[:], in_=pu[b][:], func=AF.Sigmoid, bias=pvc("cvbing", c)),
                          reads=[("pu", b), "pvt"], writes=["sgm"])
                    P.pool(lambda e, c=c, b=b: e.tensor_copy(out=gext[b][:, 0:PAD], in_=gcar[:, c, :]), reads=["gcar"], writes=[T_("gext_c")])
                    P.dve(lambda e, c=c, b=b: e.scalar_tensor_tensor(out=gext[b][:, PAD:PAD + TT], in0=pg[b][:], scalar=pvc("cvbin", c), in1=sgm[:],
                                                                     op0=ALU.add, op1=ALU.mult),
                          reads=[("pg", b), "sgm", "pvt"], writes=[T_("gext")])
                    P.pool(lambda e, c=c, b=b: e.tensor_copy(out=gcar[:, c, :], in_=gext[b][:, TT:TT + PAD]), reads=[T_("gext"), T_("gext_c")], writes=["gcar"])
                    for kk in range(CVK):
                        if kk % 2 == 0:
                            P.act(lambda e, c=c, kk=kk: e.activation(out=dg[:, kk, :], in_=ident_f, func=AF.Copy, scale=pvc("cvdw%d" % kk, c)),
                                  reads=["cf", "pvt"], writes=[("dg", kk)])
                        else:
                            P.dve(lambda e, c=c, kk=kk: e.tensor_scalar(out=dg[:, kk, :], in0=ident_f, scalar1=pvc("cvdw%d" % kk, c), scalar2=None, op0=ALU.mult),
                                  reads=["cf", "pvt"], writes=[("dg", kk)])
                    for kk in range(CVK):
                        P.pe(lambda e, kk=kk, b=b: e.matmul(po[b][:], lhsT=dg[:, kk, :], rhs=gext[b][:, kk:kk + TT], start=(kk == 0), stop=(kk == CVK - 1)),
                             reads=[("dg", kk), T_("gext"), T_("gext_c")], writes=[("po", b)])
                    P.dve(lambda e, c=c, b=b: e.tensor_scalar(out=cvo[:, c, :], in0=po[b][:], scalar1=pvc("cvdwb", c), scalar2=None, op0=ALU.add),
                          reads=[("po", b), "pvt"], writes=[("cvo", c)])
                run_units(WIN, units, body, depth=2)
                for c in range(KC):
                    P.pe(lambda e, c=c: e.matmul(pss[:], lhsT=ones_f, rhs=cvo[:, c, :], start=(c == 0), stop=(c == KC - 1)),
                         reads=[("cvo", c), "cf"], writes=["pss"])
                for c in range(KC):
                    P.act(lambda e, c=c: e.activation(out=sgm[:], in_=cvo[:, c, :], func=AF.Square), reads=[("cvo", c)], writes=["sgm"])
                    P.pe(lambda e, c=c: e.matmul(px[:], lhsT=ones_f, rhs=sgm[:], start=(c == 0), stop=(c == KC - 1)),
                         reads=["sgm", "cf"], writes=["px"])
                P.dve(lambda e: e.tensor_scalar(out=mu[:], in0=pss[:], scalar1=1.0 / D, scalar2=None, op0=ALU.mult), reads=["pss"], writes=["mu"])
                P.dve(lambda e: e.tensor_tensor(out=rstd[:], in0=mu[:], in1=mu[:], op=ALU.mult), reads=["mu"], writes=["rstd"])
                P.dve(lambda e: e.scalar_tensor_tensor(out=rstd[:], in0=px[:], scalar=1.0 / D, in1=rstd[:], op0=ALU.mult, op1=ALU.subtract),
                      reads=["px", "rstd"], writes=["rstd"])
                P.act(lambda e: e.activation(out=rstd[:], in_=rstd[:], func=AF.Sqrt, bias=pvc("eps")), reads=["rstd", "pvt2"], writes=["rstd"])
                P.dve(lambda e: e.reciprocal(out=rstd[:], in_=rstd[:]), reads=["rstd"], writes=["rstd"])
                for c in range(KC):
                    P.dve(lambda e, c=c: e.tensor_tensor(out=sgm[:], in0=cvo[:, c, :], in1=mu[:], op=ALU.subtract),
                          reads=[("cvo", c), "mu"], writes=["sgm"])
                    P.dve(lambda e, c=c: e.scalar_tensor_tensor(out=sgm[:], in0=sgm[:], scalar=pvc("cvlng", c), in1=rstd[:], op0=ALU.mult, op1=ALU.mult),
                          reads=["sgm", "rstd", "pvt"], writes=["sgm"])
                    P.act(lambda e, c=c: e.activation(out=ybc[:, c, :], in_=sgm[:], func=AF.Silu, bias=pvc("cvlnb", c)),
                          reads=["sgm", "pvt"], writes=[("xn", c)])
                u2, b2 = out_proj(t, "cv_wout", 0, ybc, "xn", 1.0, bias_name="cvbout")
                run_units(WOUT, u2, b2, depth=2)

        def fox(l, stack):
            qn = [sb("qn%d" % i, [128, TT], BF16, stack) for i in range(2)]
            kst = [sb("kst%d" % i, [128, TT], BF16, stack) for i in range(2)]
            vst = sb("vst", [128, NH * 128], BF16, stack)
            ot = sb("ot", [128, KC, TT], BF16, stack)
            PB = 4
            kbuf = [sb("kbuf%d" % i, [128, PB * 128], BF16, stack) for i in range(2)]
            vbuf = [sb("vbuf%d" % i, [128, PB, 256], BF16, stack) for i in range(2)]
            pT = [sb("pT%d" % i, [128, 128], BF16, stack) for i in range(4)]
            call = sb("call", [128, NB, NH], F32, stack)
            runtot = sb("runtot", [128, NB + 1, NH], F32, stack)
            bias4 = [sb("bias%d" % i, [128, NB, NH], F32, stack) for i in range(4)]
            rc = sb("rc", [128, TT], F32, stack)
            lnv = sb("lnv", [128, NH], F32, stack)
            wf = sb("wf", [128, KC * NH], BF16, stack)
            qg8 = sb("qg8", [128, 1], F32, stack)
            sqq = sb("sqq", [128, TT], BF16, stack)
            P.dma("sp", "ld_wf", lambda e: e.dma_start(out=wf[:], in_=ws["fox_wf"][0]), reads=[("ws", "fox_wf", 0)], writes=["wf"])
            P.dve(lambda e: e.tensor_scalar(out=qg8[:], in0=pvc("fqg"), scalar1=DH ** -0.5, scalar2=None, op0=ALU.mult), reads=["pvt"], writes=["qg8"])
            P.pool(lambda e: e.memset(runtot[:, 0, :], 0.0), writes=[("runtot", 0)])
            P.pool(lambda e: e.memset(vst[:], 1.0), writes=["vst"])
            pt_ctr = [0]
            kv_ctr = [0]

            def qk_proj(unit_idx, dst, dst_tok, gain_ap):
                WIN.prefetch(ws["fox_wqk"][unit_idx], 1024, ("ws", "fox_wqk", unit_idx))
                slot, tok = WIN.get()
                for k in range(KC):
                    P.pe(lambda e, k=k, slot=slot: e.matmul(pg[0][:], lhsT=slot[:, k * 128:(k + 1) * 128], rhs=xn[:, k, :],
                                                            start=(k == 0), stop=(k == KC - 1)),
                         reads=[tok, ("xn", k)], writes=[("pg", 0)])
                P.act(lambda e: e.activation(out=sqq[:], in_=pg[0][:], func=AF.Square), reads=[("pg", 0)], writes=["sqq"])
                P.pe(lambda e: e.matmul(pu[0][:], lhsT=bd_bf, rhs=sqq[:], start=True, stop=True), reads=["sqq", "cb"], writes=[("pu", 0)])
                P.act(lambda e: e.activation(out=rs0[:], in_=pu[0][:], func=AF.Sqrt, scale=1.0 / DH, bias=pvc("eps")),
                      reads=[("pu", 0), "pvt2"], writes=["rs0"])
                P.dve(lambda e: e.reciprocal(out=rs1[:], in_=rs0[:]), reads=["rs0"], writes=["rs1"])
                P.dve(lambda e: e.scalar_tensor_tensor(out=dst[:], in0=pg[0][:], scalar=gain_ap, in1=rs1[:], op0=ALU.mult, op1=ALU.mult),
                      reads=[("pg", 0), "rs1", "qg8", "pvt"], writes=[dst_tok])

            for t in range(NT):
                c0 = t * TT
                rmsnorm(t, "mn%d" % l)
                for b in range(4):
                    blk = t * 4 + b
                    for k in range(KC):
                        P.pe(lambda e, k=k, b=b: e.matmul(px[:, 0:NH], lhsT=xn[:, k, b * 128:(b + 1) * 128], rhs=wf[:, k * NH:(k + 1) * NH],
                                                          start=(k == 0), stop=(k == KC - 1)),
                             reads=["wf", ("xn", k)], writes=["px"])
                    P.dve(lambda e: e.tensor_tensor(out=lnv[:], in0=px[:, 0:NH], in1=pvc("fbf", 0, NH), op=ALU.add), reads=["px", "pvt"], writes=["lnv"])
                    P.act(lambda e: e.activation(out=lnv[:], in_=lnv[:], func=AF.Exp, scale=-1.0), reads=["lnv"], writes=["lnv"])
                    P.act(lambda e: e.activation(out=lnv[:], in_=lnv[:], func=AF.Ln, bias=1.0), reads=["lnv"], writes=["lnv"])
                    P.pe(lambda e: e.matmul(pss[:, 0:NH], lhsT=tri_f, rhs=lnv[:], start=True, stop=True, skip_group_check=True), reads=["lnv", "cf"], writes=["pss"])
                    P.pe(lambda e: e.matmul(pss[:, NH:2 * NH], lhsT=ones_f, rhs=lnv[:], start=False, stop=True, skip_group_check=True),
                         reads=["lnv", "cf"], writes=["pss"])
                    P.dve(lambda e, blk=blk: e.tensor_tensor(out=call[:, blk, :], in0=pss[:, 0:NH], in1=runtot[:, blk, :], op=ALU.add),
                          reads=["pss", ("runtot", blk)], writes=[("call", blk)])
                    P.dve(lambda e, blk=blk: e.tensor_tensor(out=runtot[:, blk + 1, :], in0=pss[:, NH:2 * NH], in1=runtot[:, blk, :], op=ALU.add),
                          reads=["pss", ("runtot", blk)], writes=[("runtot", blk + 1)])
                for c in range(KC):
                    ks = kst[c % 2]
                    qk_proj(8 + c, ks, ("kst", c % 2), pvc("fkg"))
                    P.dma("sp", "st_k%d" % (c % 2), lambda e, c=c, ks=ks, c0=c0: e.dma_start(out=kt_d[c][:, c0:c0 + TT], in_=ks[:]),
                          reads=[("kst", c % 2)], writes=[("kt_d", c, t)])
                for b in range(4):
                    blk = t * 4 + b
                    units = [(ws["fox_wv"][g], 2048, ("ws", "fox_wv", g), g) for g in range(4)]

                    def vbody(slot, tok, g, b=b):
                        pb = pu[0] if g % 2 == 0 else pg[0]
                        ptok = ("pu", 0) if g % 2 == 0 else ("pg", 0)
                        for k in range(KC):
                            P.pe(lambda e, k=k, slot=slot, pb=pb: e.matmul(pb[:, 0:256], lhsT=xn[:, k, b * 128:(b + 1) * 128],
                                                                          rhs=slot[:, k * 256:(k + 1) * 256], start=(k == 0), stop=(k == KC - 1)),
                                 reads=[tok, ("xn", k)], writes=[ptok])
                        for i in range(4):
                            hd = 4 * g + i
                            d0 = hd * 128 + (hd % 2) * 64
                            P.act(lambda e, pb=pb, i=i, d0=d0: e.activation(out=vst[:, d0:d0 + 64], in_=pb[:, i * 64:(i + 1) * 64], func=AF.Copy),
                                  reads=[ptok], writes=["vst"])
                    run_units(WIN, units, vbody, depth=2)
                    P.dma("sp", "st_v", lambda e, blk=blk: e.dma_start(out=v_d[:, blk, :], in_=vst[:]), reads=["vst"], writes=[("v_d", blk)])
                for qb in range(4):
                    I = t * 4 + qb
                    for h_ in range(NH):
                        P.pool(lambda e, qb=qb, I=I, h_=h_: e.tensor_scalar(out=bias4[qb][:, 0:I + 1, h_], in0=call[:, 0:I + 1, h_],
                                                                            scalar1=runtot[:, I + 1, h_:h_ + 1], scalar2=None, op0=ALU.subtract),
                               reads=[("call", jb) for jb in range(t * 4, I + 1)] + [("runtot", I + 1)], writes=[("bias", qb)])
                for c in range(KC):
                    q = qn[c % 2]
                    qtok = ("qn", c % 2)
                    qk_proj(c, q, qtok, qg8[:, 0:1])
                    P.dve(lambda e: e.memset(po[0][:], 0.0), writes=[("po", 0)])
                    P.dve(lambda e: e.memset(po[1][:], 0.0), writes=[("po", 1)])
                    nblk = (t + 1) * 4
                    for j0 in range(0, nblk, PB):
                        nj = min(PB, nblk - j0)
                        kvi = kv_ctr[0] % 2
                        kv_ctr[0] += 1
                        kb, vb = kbuf[kvi], vbuf[kvi]
                        ktoks = [("kt_d", c, tt_) for tt_ in range(j0 // 4, (j0 + nj + 3) // 4)]
                        P.dma("sp", "ld_k%d" % kvi,
                              lambda e, kb=kb, j0=j0, nj=nj, c=c: e.dma_start(out=kb[:, 0:nj * 128], in_=kt_d[c][:, j0 * 128:(j0 + nj) * 128]),
                              reads=ktoks, writes=[("kbuf", kvi)])
                        P.dma("sp", "ld_v%d" % kvi,
                              lambda e, vb=vb, j0=j0, nj=nj, c=c: e.dma_start(
                                  out=vb[:, 0:nj, :], in_=v_d[:, j0:j0 + nj, 2 * c * 128:(2 * c + 2) * 128]),
                              reads=[("v_d", jj) for jj in range(j0, j0 + nj)], writes=[("vbuf", kvi)])
                        pairs = []
                        for hh in range(2):
                            for qb in range(4):
                                I = t * 4 + qb
                                for jj in range(nj):
                                    if j0 + jj > I:
                                        break
                                    pairs.append((hh, qb, jj, j0 + jj, I))
                        LA = 2
                        slots_of = {}

                        def emit_S(n):
                            hh, qb, jj, jblk, I = pairs[n]
                            h_ = 2 * c + hh
                            p0 = 64 * hh
                            pi = pt_ctr[0] % 4
                            pt_ctr[0] += 1
                            slots_of[n] = pi
                            sbank = pg[1] if (pi % 2 == 0) else pu[1]
                            stok = ("S", pi)
                            scol = (pi // 2) * 128
                            P.pe(lambda e, jj=jj, qb=qb, p0=p0, sbank=sbank, scol=scol, kb=kb, q=q:
                                 e.matmul(sbank[:, scol:scol + 128], lhsT=kb[p0:p0 + 64, jj * 128:(jj + 1) * 128],
                                          rhs=q[p0:p0 + 64, qb * 128:(qb + 1) * 128], start=True, stop=True, skip_group_check=True),
                                 reads=[("kbuf", kvi), qtok], writes=[stok])
                            P.act(lambda e, sbank=sbank, scol=scol, pi=pi, qb=qb, jblk=jblk, h_=h_:
                                  e.activation(out=pT[pi][:], in_=sbank[:, scol:scol + 128], func=AF.Exp, bias=bias4[qb][:, jblk, h_:h_ + 1]),
                                  reads=[stok, ("bias", qb)], writes=[("pT", pi)])
                            if jblk == I:
                                P.pool(lambda e, pi=pi: e.tensor_tensor(out=pT[pi][:], in0=pT[pi][:], in1=tri_bf, op=ALU.mult),
                                       reads=[("pT", pi), "cb"], writes=[("pT", pi)])

                        def emit_PV(n):
                            hh, qb, jj, jblk, I = pairs[n]
                            pi = slots_of[n]
                            P.pe(lambda e, jj=jj, hh=hh, pi=pi, qb=qb, vb=vb:
                                 e.matmul(po[hh][:, qb * 128:(qb + 1) * 128], lhsT=vb[:, jj, hh * 128:(hh + 1) * 128], rhs=pT[pi][:],
                                          start=False, stop=True, skip_group_check=True),
                                 reads=[("vbuf", kvi), ("pT", pi)], writes=[("po", hh)])

                        for n in range(len(pairs) + LA):
                            if n < len(pairs):
                                emit_S(n)
                            if n - LA >= 0:
                                emit_PV(n - LA)
                    P.dve(lambda e: e.reciprocal(out=rc[0:64, :], in_=po[0][64:128, :]), reads=[("po", 0)], writes=["rc0"])
                    P.dve(lambda e, c=c: e.tensor_tensor(out=ot[0:64, c, :], in0=po[0][0:64, :], in1=rc[0:64, :], op=ALU.mult),
                          reads=[("po", 0), "rc0"], writes=[("ot", c)])
                    P.dve(lambda e: e.reciprocal(out=rc[64:128, :], in_=po[1][0:64, :]), reads=[("po", 1)], writes=["rc1"])
                    P.dve(lambda e, c=c: e.tensor_tensor(out=ot[64:128, c, :], in0=po[1][64:128, :], in1=rc[64:128, :], op=ALU.mult),
                          reads=[("po", 1), "rc1", ("ot", c)], writes=[("ot", c)])
                u2, b2 = out_proj(t, "fox_wout", 0, ot, "ot", 1.0)
                run_units(WOUT, u2, b2, depth=2)
            if DEBUG is not None and FOXDBG:
                DEBUG.append(("call", lambda: call[:, 0:4, :], 64))
                DEBUG.append(("runtot", lambda: runtot[:, 0:5, :], 80))
                DEBUG.append(("bias0", lambda: bias4[0][:, 0:4, :], 64))
                DEBUG.append(("bias3", lambda: bias4[3][:, 0:4, :], 64))
                DEBUG.append(("rc", lambda: rc[:], 512))
                DEBUG.append(("po0", lambda: rs0[:], 512))
                DEBUG.append(("lnv", lambda: lnv[:], 16))
                DEBUG.append(("kbuf0", lambda: kbuf[0][:], 512))
                DEBUG.append(("kbuf1", lambda: kbuf[1][:], 512))
                DEBUG.append(("vbuf0", lambda: vbuf[0][:, 0:2, :], 512))
                DEBUG.append(("vbuf1", lambda: vbuf[1][:, 0:2, :], 512))
                for i_ in range(4):
                    DEBUG.append(("pT%d" % i_, lambda i_=i_: pT[i_][:], 128))
                DEBUG.append(("qn0", lambda: qn[0][:], 512))
                DEBUG.append(("qn1", lambda: qn[1][:], 512))
                DEBUG.append(("ot7", lambda: ot[:, 7, :], 512))
                DEBUG.append(("ot0", lambda: ot[:, 0, :], 512))
                DEBUG.append(("kst0", lambda: kst[0][:], 512))
                DEBUG.append(("vst", lambda: vst[:, 0:512], 512))

        def fox2(l, stack):
            KA = 70
            qb_ = [sb("q%d" % i, [128, TT], BF16, stack) for i in range(4)]
            kst = [sb("kst%d" % i, [128, TT], BF16, stack) for i in range(2)]
            vst = [sb("vst%d" % i, [128, 4 * 128], BF16, stack) for i in range(2)]
            ot = sb("ot", [128, KC, TT], BF16, stack)
            PB = 4
            kbuf = [[sb("kb%d_%d" % (i, hh), [128, PB * 128], BF16, stack) for hh in range(2)] for i in range(2)]
            vbuf = [sb("vbuf%d" % i, [128, PB, 256], BF16, stack) for i in range(2)]
            pT = [sb("pT%d" % i, [128, TT], BF16, stack) for i in range(3)]
            rc = sb("rc", [128, TT], F32, stack)
            sqq = sb("sqq", [128, TT], BF16, stack)
            lnv = sb("lnv", [NH, TT], F32, stack)
            Cc = sb("Cc", [NH, TT], F32, stack)
            cs = sb("cs", [NH, 3, TT], BF16, stack)
            ncs = sb("ncs", [NH, 3, TT], BF16, stack)
            ctot = sb("ctot", [NH, 1], F32, stack)
            nbf = sb("nbf", [NH, 1], F32, stack)
            wf = sb("wf", [128, KC * NH], BF16, stack)
            qg8 = sb("qg8", [128, 1], F32, stack)
            Sb = [pg[1], pu[1], px]
            P.dma("sp", "ld_wf", lambda e: e.dma_start(out=wf[:], in_=ws["fox_wf"][0]), reads=[("ws", "fox_wf", 0)], writes=["wf"])
            P.dve(lambda e: e.tensor_scalar(out=qg8[:], in0=pvc("fqg"), scalar1=DH ** -0.5, scalar2=None, op0=ALU.mult), reads=["pvt"], writes=["qg8"])
            P.dve(lambda e: e.tensor_scalar(out=nbf[:], in0=pvt[0:NH, pvidx["fbfc"]:pvidx["fbfc"] + 1], scalar1=-1.0, scalar2=None, op0=ALU.mult),
                  reads=["pvt"], writes=["nbf"])
            P.pool(lambda e: e.memset(ctot[:], 0.0), writes=["ctot"])
            for i in range(2):
                P.pool(lambda e, i=i: e.memset(vst[i][:], 1.0), writes=[("vst", i)])
            P.pool(lambda e: e.memset(cs[:], 1.0), writes=["cs"])
            for t in range(NT):
                P.dma("sp", "st_ko", lambda e, t=t: e.dma_start(out=kt2_d[:, 67:70, t * TT:(t + 1) * TT], in_=cs[:]), reads=["cs"], writes=[("kt2o", t)])
            for i in range(4):
                P.pool(lambda e, i=i: e.memset(qb_[i][64:67, :], 1.0), writes=[("qaug1", i)])
            q_ctr = [0]
            s_ctr = [0]
            kv_ctr = [0]
            vst_ctr = [0]

            def head_norm(src_ps, src_tok, gain_ap, dsts, square_done=False):
                if not square_done:
                    P.act(lambda e: e.activation(out=sqq[:], in_=src_ps[:], func=AF.Square), reads=[src_tok], writes=["sqq"])
                P.pe(lambda e: e.matmul(pu[0][:], lhsT=bd_bf, rhs=sqq[:], start=True, stop=True), reads=["sqq", "cb"], writes=[("pu", 0)])
                P.act(lambda e: e.activation(out=rs0[:], in_=pu[0][:], func=AF.Ln, scale=1.0 / DH, bias=pvc("eps")),
                      reads=[("pu", 0), "pvt2"], writes=["rs0"])
                P.act(lambda e: e.activation(out=rs1[:], in_=rs0[:], func=AF.Exp, scale=-0.5), reads=["rs0"], writes=["rs1"])
                for (p0, p1, dst_ap, dst_tok) in dsts:
                    P.dve(lambda e, p0=p0, p1=p1, dst_ap=dst_ap: e.scalar_tensor_tensor(out=dst_ap, in0=src_ps[p0:p1, :], scalar=gain_ap[p0:p1, 0:1],
                                                                                         in1=rs1[p0:p1, :], op0=ALU.mult, op1=ALU.mult),
                          reads=[src_tok, "rs1", "qg8", "pvt", "sqq"], writes=[dst_tok])

            for t in range(NT):
                c0 = t * TT
                rmsnorm(t, "mn%d" % l)
                for k in range(KC):
                    P.pe(lambda e, k=k: e.matmul(pu[0][0:NH, :], lhsT=wf[:, k * NH:(k + 1) * NH], rhs=xn[:, k, :], start=(k == 0), stop=(k == KC - 1)),
                         reads=["wf", ("xn", k)], writes=[("pu", 0)])
                P.act(lambda e: e.activation(out=lnv[:], in_=pu[0][0:NH, :], func=AF.Exp, scale=-1.0, bias=nbf[:, 0:1]), reads=[("pu", 0), "nbf"], writes=["lnv"])
                P.act(lambda e: e.activation(out=lnv[:], in_=lnv[:], func=AF.Ln, bias=1.0), reads=["lnv"], writes=["lnv"])
                for qq in range(4):
                    ini = ctot[:, 0:1] if qq == 0 else Cc[:, qq * 128 - 1:qq * 128]
                    P.dve(lambda e, qq=qq, ini=ini: e.tensor_tensor_scan(out=Cc[:, qq * 128:(qq + 1) * 128], data0=ones_f[0:NH, :],
                                                                         data1=lnv[:, qq * 128:(qq + 1) * 128], initial=ini, op0=ALU.mult, op1=ALU.add),
                          reads=["lnv", "cf", "ctot", "Cc"], writes=["Cc"])
                P.dve(lambda e: e.tensor_copy(out=ctot[:], in_=Cc[:, TT - 1:TT]), reads=["Cc"], writes=["ctot"])
                P.dve(lambda e: e.tensor_copy(out=cs[:, 0, :], in_=Cc[:]), reads=["Cc"], writes=["cs"])
                P.dve(lambda e: e.tensor_tensor(out=lnv[:], in0=Cc[:], in1=cs[:, 0, :], op=ALU.subtract), reads=["Cc", "cs"], writes=["lnv"])
                P.dve(lambda e: e.tensor_copy(out=cs[:, 1, :], in_=lnv[:]), reads=["lnv"], writes=["cs"])
                P.dve(lambda e: e.tensor_tensor(out=lnv[:], in0=lnv[:], in1=cs[:, 1, :], op=ALU.subtract), reads=["lnv", "cs"], writes=["lnv"])
                P.dve(lambda e: e.tensor_copy(out=cs[:, 2, :], in_=lnv[:]), reads=["lnv"], writes=["cs"])
                P.dve(lambda e: e.tensor_scalar(out=ncs[:], in0=cs[:], scalar1=-1.0, scalar2=None, op0=ALU.mult), reads=["cs"], writes=["ncs"])
                P.dma("sp", "st_kc", lambda e, c0=c0: e.dma_start(out=kt2_d[:, 64:67, c0:c0 + TT], in_=cs[:]), reads=["cs"], writes=[("kt2c", t)])
                kunits = [(ws["fox_wqk"][8 + c], 1024, ("ws", "fox_wqk", 8 + c), c) for c in range(KC)]

                def kbody(slot, tok, c, c0=c0, t=t):
                    ks = kst[c % 2]
                    pj, pjt = (pg[0], ("pg", 0)) if c % 2 == 0 else (pss, "pss")
                    for k in range(KC):
                        P.pe(lambda e, k=k, slot=slot, pj=pj: e.matmul(pj[:], lhsT=slot[:, k * 128:(k + 1) * 128], rhs=xn[:, k, :],
                                                                      start=(k == 0), stop=(k == KC - 1)),
                             reads=[tok, ("xn", k)], writes=[pjt])
                    head_norm(pj, pjt, pvc("fkg"), [(0, 128, ks[:], ("kst", c % 2))])
                    for hh in range(2):
                        P.dma("sp", "st_k%d" % (c % 2), lambda e, c=c, hh=hh, ks=ks, c0=c0: e.dma_start(out=kt2_d[2 * c + hh][0:64, c0:c0 + TT], in_=ks[64 * hh:64 * hh + 64, :]),
                              reads=[("kst", c % 2)], writes=[("kt2", 2 * c + hh, t)])
                def vbody(slot, tok, g, t=t):
                    for b in range(4):
                        blk = t * 4 + b
                        pb, ptok = (pg[1], ("S", 0)) if b % 2 == 0 else (pu[1], ("S", 1))
                        for k in range(KC):
                            P.pe(lambda e, k=k, slot=slot, pb=pb, b=b: e.matmul(pb[:, 0:256], lhsT=xn[:, k, b * 128:(b + 1) * 128],
                                                                               rhs=slot[:, k * 256:(k + 1) * 256], start=(k == 0), stop=(k == KC - 1)),
                                 reads=[tok, ("xn", k)], writes=[ptok])
                        vi = vst_ctr[0] % 2
                        vst_ctr[0] += 1
                        vs = vst[vi]
                        for i in range(4):
                            d0 = i * 128 + (i % 2) * 64
                            P.dve(lambda e, pb=pb, i=i, d0=d0, vs=vs: e.tensor_copy(out=vs[:, d0:d0 + 64], in_=pb[:, i * 64:(i + 1) * 64]),
                                  reads=[ptok], writes=[("vst", vi)])
                        P.dma("sp", "st_v%d" % vi, lambda e, blk=blk, g=g, vs=vs: e.dma_start(out=v_d[:, blk, g * 512:(g + 1) * 512], in_=vs[:]),
                              reads=[("vst", vi)], writes=[("v_d", blk, g)])

                vunits = [(ws["fox_wv"][g], 2048, ("ws", "fox_wv", g), g) for g in range(4)]
                merged = []
                for c in range(KC):
                    merged.append(kunits[c][:3] + (("k", kunits[c][3]),))
                    if c % 2 == 1:
                        u = vunits[c // 2]
                        merged.append(u[:3] + (("v", u[3]),))

                def kvbody(slot, tok, pl):
                    if pl[0] == "k":
                        kbody(slot, tok, pl[1])
                    else:
                        vbody(slot, tok, pl[1])
                run_units(WIN, merged, kvbody, depth=2)
                nblk = (t + 1) * 4

                def q_make_a(c):
                    slot, tok = WIN.get()
                    if c + 2 < KC:
                        WIN.prefetch(ws["fox_wqk"][c + 2], 1024, ("ws", "fox_wqk", c + 2))
                    pj, pjt = (pg[0], ("pg", 0)) if c % 2 == 0 else (pss, "pss")
                    for k in range(KC):
                        P.pe(lambda e, k=k, slot=slot, pj=pj: e.matmul(pj[:], lhsT=slot[:, k * 128:(k + 1) * 128], rhs=xn[:, k, :],
                                                                      start=(k == 0), stop=(k == KC - 1)),
                             reads=[tok, ("xn", k)], writes=[pjt])
                    P.act(lambda e, pj=pj: e.activation(out=sqq[:], in_=pj[:], func=AF.Square), reads=[pjt], writes=["sqq"])
                    return (pj, pjt)

                def q_make_b(c, pjx):
                    pj, pjt = pjx
                    qi0 = q_ctr[0] % 4
                    qi1 = (q_ctr[0] + 1) % 4
                    q_ctr[0] += 2
                    qA, qB = qb_[qi0], qb_[qi1]
                    head_norm(pj, pjt, qg8, [(0, 64, qA[0:64, :], ("q", qi0)), (64, 128, sqq[64:128, :], "sqq")], square_done=True)
                    P.act(lambda e, qB=qB: e.activation(out=qB[0:64, :], in_=sqq[64:128, :], func=AF.Copy), reads=["sqq"], writes=[("q", qi1)])
                    res = []
                    for hh, (qt, qi) in enumerate(((qA, qi0), (qB, qi1))):
                        h_ = 2 * c + hh
                        P.dma("sp", "ld_qa%d" % qi, lambda e, qt=qt, h_=h_: e.dma_start(out=qt[67:70, :], in_=ncs[h_:h_ + 1, :, :]),
                              reads=["ncs"], writes=[("qaug2", qi)])
                        res.append((qt, [("q", qi), ("qaug1", qi), ("qaug2", qi)]))
                    return res

                def q_make(c):
                    return q_make_b(c, q_make_a(c))

                pieces = []
                for c in range(KC):
                    for j0 in range(0, nblk, PB):
                        pieces.append((c, j0, min(PB, nblk - j0)))
                piece_kvi = {}

                def load_piece(ip):
                    c, j0, nj = pieces[ip]
                    kvi = kv_ctr[0] % 2
                    kv_ctr[0] += 1
                    piece_kvi[ip] = kvi
                    vb = vbuf[kvi]
                    tts = list(range(j0 // 4, (j0 + nj + 3) // 4))
                    for hh in range(2):
                        kb = kbuf[kvi][hh]
                        P.dma("sp", "ld_k%d_%d" % (kvi, hh),
                              lambda e, kb=kb, j0=j0, nj=nj, hd=2 * c + hh: e.dma_start(out=kb[0:KA, 0:nj * 128], in_=kt2_d[hd][:, j0 * 128:(j0 + nj) * 128]),
                              reads=[("kt2", 2 * c + hh, tt_) for tt_ in tts] + [("kt2c", tt_) for tt_ in tts] + [("kt2o", tt_) for tt_ in tts],
                              writes=[("kbuf", kvi, hh)])
                    P.dma("sp", "ld_v%d" % kvi,
                          lambda e, vb=vb, j0=j0, nj=nj, c=c: e.dma_start(out=vb[:, 0:nj, :], in_=v_d[:, j0:j0 + nj, 2 * c * 128:(2 * c + 2) * 128]),
                          reads=[("v_d", jj, c // 2) for jj in range(j0, j0 + nj)], writes=[("vbuf", kvi)])

                WIN.prefetch(ws["fox_wqk"][0], 1024, ("ws", "fox_wqk", 0))
                WIN.prefetch(ws["fox_wqk"][1], 1024, ("ws", "fox_wqk", 1))
                qs_of = {0: q_make(0)}
                qpend = None
                pending = []
                LA = 2

                def emit_S(ip, hh, jj, jblk, lo):
                    c = pieces[ip][0]
                    kvi = piece_kvi[ip]
                    si = s_ctr[0] % 3
                    s_ctr[0] += 1
                    sbank = Sb[si]
                    stok = ("S", si)
                    kb = kbuf[kvi][hh]
                    qt, qtoks = qs_of[c][hh]
                    dg_ = jblk >= 4 * t
                    P.pe(lambda e, jj=jj, lo=lo, sbank=sbank, kb=kb, qt=qt, dg_=dg_:
                         e.matmul(sbank[:, lo:TT], lhsT=kb[0:KA, jj * 128:(jj + 1) * 128], rhs=qt[0:KA, lo:TT], start=True, stop=not dg_,
                                  skip_group_check=True),
                         reads=[("kbuf", kvi, hh)] + qtoks, writes=[stok])
                    if dg_:
                        P.pe(lambda e, lo=lo, sbank=sbank: e.matmul(sbank[:, lo:lo + 128], lhsT=ident_bf, rhs=negm_bf, start=False, stop=True,
                                                                   skip_group_check=True),
                             reads=["cb"], writes=[stok])
                    P.act(lambda e, sbank=sbank, si=si, lo=lo: e.activation(out=pT[si][:, lo:TT], in_=sbank[:, lo:TT], func=AF.Exp),
                          reads=[stok], writes=[("pT", si)])
                    pending.append((c, kvi, hh, jj, jblk, lo, si, ip))

                def emit_norm(c):
                    P.act(lambda e: e.activation(out=rc[0:64, :], in_=po[0][64:128, :], func=AF.Ln), reads=[("po", 0)], writes=["rc0"])
                    P.act(lambda e: e.activation(out=rc[0:64, :], in_=rc[0:64, :], func=AF.Exp, scale=-1.0), reads=["rc0"], writes=["rc0"])
                    P.dve(lambda e, c=c: e.tensor_tensor(out=ot[0:64, c, :], in0=po[0][0:64, :], in1=rc[0:64, :], op=ALU.mult),
                          reads=[("po", 0), "rc0"], writes=[("ot", c)])
                    P.act(lambda e: e.activation(out=rc[64:128, :], in_=po[1][0:64, :], func=AF.Ln), reads=[("po", 1)], writes=["rc1"])
                    P.act(lambda e: e.activation(out=rc[64:128, :], in_=rc[64:128, :], func=AF.Exp, scale=-1.0), reads=["rc1"], writes=["rc1"])
                    P.dve(lambda e, c=c: e.tensor_tensor(out=ot[64:128, c, :], in0=po[1][64:128, :], in1=rc[64:128, :], op=ALU.mult),
                          reads=[("po", 1), "rc1", ("ot", c)], writes=[("ot", c)])

                def emit_PV():
                    c, kvi, hh, jj, jblk, lo, si, _ip = pending.pop(0)
                    vb = vbuf[kvi]
                    P.pe(lambda e, jj=jj, hh=hh, si=si, lo=lo, vb=vb, st=(jblk == 0), last=(jblk == nblk - 1):
                         e.matmul(po[hh][:, lo:TT], lhsT=vb[:, jj, hh * 128:(hh + 1) * 128], rhs=pT[si][:, lo:TT],
                                  start=st, stop=last, skip_group_check=True),
                         reads=[("vbuf", kvi), ("pT", si)], writes=[("po", hh)])
                    if hh == 1 and jblk == nblk - 1:
                        emit_norm(c)

                load_piece(0)
                if len(pieces) > 1:
                    load_piece(1)
                for ip, (c, j0, nj) in enumerate(pieces):
                    if j0 == 0 and c + 1 < KC:
                        qpend = [c + 1, q_make_a(c + 1), 0]
                    nxt_loaded = (ip == 0) or (ip + 1 >= len(pieces))
                    for hh in range(2):
                        for jj in range(nj):
                            jblk = j0 + jj
                            lo = 0 if jblk < 4 * t else 128 * (jblk - 4 * t)
                            emit_S(ip, hh, jj, jblk, lo)
                            if len(pending) > LA:
                                emit_PV()
                            if qpend is not None:
                                qpend[2] += 1
                                if qpend[2] >= 4:
                                    qs_of[qpend[0]] = q_make_b(qpend[0], qpend[1])
                                    qpend = None
                            if not nxt_loaded and all(pe_[7] >= ip for pe_ in pending):
                                load_piece(ip + 1)
                                nxt_loaded = True
                    assert nxt_loaded
                while pending:
                    emit_PV()
                u2, b2 = out_proj(t, "fox_wout", 0, ot, "ot", 1.0)
                run_units(WOUT, u2, b2, depth=2)

        if plan[0][1] in ("f0", "f1"):
            ls0 = plan[0][0] * 2 + (0 if plan[0][1] == "f0" else 1)
            for (a_, b_) in ((0, 2), (2, 6), (6, 14), (14, FC)):
                cast("ffn_win", ls0 * FC + a_, ls0 * FC + b_)
            cast("ffn_wout", ls0 * KC * 2, ls0 * KC * 2 + 4)
            cast("ffn_wout", ls0 * KC * 2 + 4, (ls0 + 1) * KC * 2)
            cast_done.add(("ffn_win", ls0 * FC, (ls0 + 1) * FC))
            cast_done.add(("ffn_wout", ls0 * KC * 2, (ls0 + 1) * KC * 2))
        casts_for(0)
        defer0 = plan[0][1] in ("f0", "f1") and NT >= 4
        if not defer0:
            casts_for(1)
        for pi_, (l, sub) in enumerate(plan):
            tile_hook[0] = None
            if pi_ == 0 and defer0:
                def _hook(t_):
                    if t_ == 0:
                        cast_after[0] = [("X", KC - 1, t_)]
                        casts_for(1)
                        cast_after[0] = ()
                tile_hook[0] = _hook
            if sub == "mix":
                for d_ in ((1, 2, 3, 4) if l % 3 == 1 else (1, 2, 3)):
                    casts_for(pi_ + d_)
            elif not any(s_ == "mix" for (_, s_) in plan):
                casts_for(pi_ + 1)
            with contextlib.ExitStack() as ph:
                if sub == "f0":
                    ffn(l, 0, ph)
                elif sub == "f1":
                    ffn(l, 1, ph)
                else:
                    kind = l % 3
                    if kind == 0:
                        rglru(l, ph)
                    elif kind == 1:
                        (fox2 if USE_FOX2 else fox)(l, ph)
                    else:
                        convmod(l, ph)
                P.barrier()

        fin = []
        if DEBUG:
            dbg_d = nc.dram_tensor("dbg", [len(DEBUG), 128, 512], F32, kind="ExternalOutput").ap()
            for i, (nm, apf, w) in enumerate(DEBUG):
                fin.append(P.dma("pool", "dbg%d" % i, lambda e, i=i, apf=apf, w=w: e.dma_start(out=dbg_d[i][:, 0:w], in_=apf()), writes=[("dbg", i)]))
        for k in range(KC):
            for t in range(NT):
                fin.append(P.dma("sp", "st_x%d_%d" % (k, t % 2),
                                 lambda e, k=k, t=t: e.dma_start(out=out_d[k * 128:(k + 1) * 128, t * TT:(t + 1) * TT],
                                                                  in_=X[:, k, t * TT:(t + 1) * TT]),
                                 reads=[("X", k, t)], writes=[("out", k, t)]))
        P.emit(final_wait_ops=fin)
    return nc, P


FULL_PLAN = [(l, s) for l in range(DEPTH) for s in ("f0", "mix", "f1")]


def prepare_inputs(inp):
    pv = pv_layout(inp)
    w = weight_layouts(inp)
    shared = dict(w)
    shared["pv"] = pv.pack()
    shared.update(const_inputs())
    return shared, pv.idx, pv.n, {k: v.shape for k, v in w.items()}


def kernel(**inputs):
    inp = {k: np.asarray(v) for k, v in inputs.items()}
    x = inp["x"]
    B, S, _ = x.shape
    shared, pvidx, npv, wshapes = prepare_inputs(inp)
    nc, _ = build_program(S, FULL_PLAN, wshapes, npv, pvidx)
    in_maps = []
    for b in range(B):
        m = dict(shared)
        m["xT"] = np.ascontiguousarray(x[b].T)
        in_maps.append(m)
    res = run_bass_kernel_spmd(nc, in_maps, core_ids=list(range(B)))
    out = np.stack([np.ascontiguousarray(res.results[b]["outT"].T) for b in range(B)], axis=0)
    return out.astype(np.float32)
```

```python
import contextlib
import numpy as np
import concourse.bass as bass
import concourse.mybir as mybir
from concourse.bass_utils import run_bass_kernel_spmd

F32 = mybir.dt.float32
BF16 = mybir.dt.bfloat16
AF = mybir.ActivationFunctionType
ALU = mybir.AluOpType

D = 1024
KC = 8
DFF = 2816
FC = 22
TT = 512
EPS = 1e-6
DEPTH = 4
SEQ = 4096
NH = 16
DH = 64
CVK = 31
GELU_C = 1.5957691216057308


class _Op:
    __slots__ = ("eng", "fn", "deps", "signals", "sem", "val", "is_dma", "chan", "idx")

    def __init__(self, eng, fn, is_dma=False, chan=None):
        self.eng = eng
        self.fn = fn
        self.deps = []
        self.signals = False
        self.sem = None
        self.val = 0
        self.is_dma = is_dma
        self.chan = chan


class Prog:
    ENGS = ("pe", "act", "dve", "pool", "sp")

    def __init__(self, nc):
        self.nc = nc
        self.ops = []
        self.last_writer = {}
        self.readers = {}
        self.chans = {}
        self.last_on_eng = {}
        self.dmas_since_barrier = []
        self.pending_barrier = {}
        self.exempt = False

    def op(self, eng, fn, reads=(), writes=(), dma_chan=None):
        o = _Op(eng, fn, is_dma=dma_chan is not None, chan=dma_chan)
        deps = []
        for r in reads:
            w = self.last_writer.get(r)
            if w is not None:
                deps.append(w)
        for w_ in writes:
            w = self.last_writer.get(w_)
            if w is not None:
                deps.append(w)
            deps.extend(self.readers.get(w_, ()))
        if not (self.exempt or eng == "pe"):
            pb = self.pending_barrier.pop(eng, None)
            if pb:
                deps.extend(pb)
        seen = set()
        for d in deps:
            if id(d) in seen or d is o:
                continue
            seen.add(id(d))
            if d.eng == "pe" and eng == "pe" and not d.is_dma and not o.is_dma:
                continue
            o.deps.append(d)
            d.signals = True
        for r in reads:
            lst = self.readers.get(r)
            if lst is None:
                self.readers[r] = [o]
            else:
                if not o.is_dma:
                    lst[:] = [x for x in lst if x.is_dma or x.eng != eng]
                lst.append(o)
        for w_ in writes:
            self.last_writer[w_] = o
            self.readers[w_] = []
        o.idx = len(self.ops)
        self.ops.append(o)
        if o.is_dma:
            self.dmas_since_barrier.append(o)
        else:
            self.last_on_eng[eng] = o
        return o

    def barrier(self):
        deps = [o for o in self.last_on_eng.values()] + list(self.dmas_since_barrier)
        self.dmas_since_barrier = []
        for e in self.ENGS:
            self.pending_barrier[e] = list(deps) + self.pending_barrier.get(e, [])

    def pe(self, fn, reads=(), writes=()):
        return self.op("pe", fn, reads, writes)

    def act(self, fn, reads=(), writes=()):
        return self.op("act", fn, reads, writes)

    def dve(self, fn, reads=(), writes=()):
        return self.op("dve", fn, reads, writes)

    def pool(self, fn, reads=(), writes=()):
        return self.op("pool", fn, reads, writes)

    def dma(self, queue, chan, fn, reads=(), writes=()):
        return self.op(queue, fn, reads, writes, dma_chan=chan)

    def emit(self, final_wait_ops=()):
        nc = self.nc
        with contextlib.ExitStack() as es:
            eng_sem = {e: es.enter_context(nc.semaphore("sem_" + e)) for e in self.ENGS}
            chan_names = []
            for o in self.ops:
                if o.is_dma and o.chan not in self.chans:
                    self.chans[o.chan] = None
                    chan_names.append(o.chan)
            for i, c in enumerate(chan_names):
                self.chans[c] = es.enter_context(nc.semaphore("dch%d" % i))
            cnt = {e: 0 for e in self.ENGS}
            ccnt = {c: 0 for c in self.chans}
            for o in self.ops:
                if o.is_dma:
                    ccnt[o.chan] += 16
                    o.sem = self.chans[o.chan]
                    o.val = ccnt[o.chan]
                elif o.signals:
                    cnt[o.eng] += 1
                    o.sem = eng_sem[o.eng]
                    o.val = cnt[o.eng]
            run_c = {c: 0 for c in self.chans}
            dma_need = {}
            for o in self.ops:
                for d in o.deps:
                    if d.is_dma:
                        dma_need[(o.idx, d.chan)] = run_c[d.chan]
                if o.is_dma:
                    run_c[o.chan] += 16
            per_eng = {e: [] for e in self.ENGS}
            for o in self.ops:
                per_eng[o.eng].append(o)
            self.stats = {e: len(per_eng[e]) for e in self.ENGS}
            self.stats["sig"] = dict(cnt)
            block = es.enter_context(nc.Block())

            def run(engname, eng):
                waited = {}
                for o in per_eng[engname]:
                    need = {}
                    for d in o.deps:
                        k = id(d.sem)
                        v = dma_need[(o.idx, d.chan)] if d.is_dma else d.val
                        if k not in need or need[k][1] < v:
                            need[k] = (d.sem, v)
                    for k, (s, v) in need.items():
                        if waited.get(k, 0) >= v:
                            continue
                        eng.wait_ge(s, v)
                        waited[k] = v
                    inst = o.fn(eng)
                    if o.is_dma:
                        inst.then_inc(o.sem, 16)
                    elif o.signals:
                        inst.then_inc(o.sem, 1)
                if engname == "sp":
                    for o in final_wait_ops:
                        eng.wait_ge(o.sem, o.val)

            @block.tensor
            def _(e):
                run("pe", e)

            @block.scalar
            def _(e):
                run("act", e)

            @block.vector
            def _(e):
                run("dve", e)

            @block.gpsimd
            def _(e):
                run("pool", e)

            @block.sync
            def _(e):
                run("sp", e)


class Stream:
    def __init__(self, P, name, slots):
        self.P = P
        self.name = name
        self.slots = slots
        self.n = len(slots)
        self.issued = 0
        self.taken = 0

    def prefetch(self, src_ap, ncols, src_token):
        i = self.issued % self.n
        slot = self.slots[i]
        ex = self.P.exempt
        self.P.exempt = True
        self.P.dma("sp", "%s%d" % (self.name, i),
                   lambda e, slot=slot, src_ap=src_ap, ncols=ncols: e.dma_start(out=slot[:, 0:ncols], in_=src_ap),
                   reads=[src_token], writes=[(self.name, i)])
        self.P.exempt = ex
        self.issued += 1

    def get(self):
        assert self.taken < self.issued
        i = self.taken % self.n
        self.taken += 1
        return self.slots[i], (self.name, i)


def run_units(stream, units, body, depth, after=None):
    n = len(units)
    for i in range(min(depth, n)):
        stream.prefetch(*units[i][:3])
    for i in range(n):
        slot, tok = stream.get()
        body(slot, tok, units[i][3])
        if i + depth < n:
            stream.prefetch(*units[i + depth][:3])
        if after and i in after:
            after[i]()


class PV:
    def __init__(self):
        self.cols = []
        self.idx = {}
        self.n = 0

    def add(self, name, arr2d):
        arr2d = np.ascontiguousarray(arr2d, dtype=np.float32)
        assert arr2d.shape[0] == 128
        self.idx[name] = self.n
        self.n += arr2d.shape[1]
        self.cols.append(arr2d)

    def pack(self):
        return np.ascontiguousarray(np.concatenate(self.cols, axis=1))


def fm(v):
    return np.asarray(v, np.float32).reshape(KC, 128).T


def pv_layout(inp):
    pv = PV()
    for l in range(DEPTH):
        pv.add("fn%d0" % l, fm(inp["ffn_norm"][l, 0]))
        pv.add("fn%d1" % l, fm(inp["ffn_norm"][l, 1]))
        pv.add("mn%d" % l, fm(inp["mix_norm"][l]))
    for j in range(inp["rg_w_in"].shape[0]):
        for k in range(4):
            pv.add("rgcw%d_%d" % (j, k), fm(inp["rg_conv_w"][j, k]))
        pv.add("rgcb%d" % j, fm(inp["rg_conv_b"][j]))
        pv.add("rgba%d" % j, fm(inp["rg_b_a"][j]))
        pv.add("rgbx%d" % j, fm(inp["rg_b_x"][j]))
        pv.add("rglam%d" % j, fm(inp["rg_lambda"][j]))
    g2 = lambda v: np.concatenate([v, v]).reshape(128, 1)
    pv.add("fqg", g2(inp["fox_q_norm"][0]))
    pv.add("fkg", g2(inp["fox_k_norm"][0]))
    pv.add("fbf", np.broadcast_to(inp["fox_b_f"][0][None, :], (128, NH)))
    pv.add("fbfc", np.pad(inp["fox_b_f"][0], (0, 128 - NH)).reshape(128, 1))
    pv.add("cvbin", fm(inp["cv_b_in"][0][:D]))
    pv.add("cvbing", fm(inp["cv_b_in"][0][D:]))
    for k in range(CVK):
        pv.add("cvdw%d" % k, fm(inp["cv_dw_w"][0, k]))
    pv.add("cvdwb", fm(inp["cv_dw_b"][0]))
    pv.add("cvlng", fm(inp["cv_ln_g"][0]))
    pv.add("cvlnb", fm(inp["cv_ln_b"][0]))
    pv.add("cvbout", fm(inp["cv_b_out"][0]))
    return pv


def weight_layouts(inp):
    w = {}
    a = inp["ffn_w_in"].reshape(DEPTH * 2, KC, 128, 2, FC, 128)
    w["ffn_win"] = np.ascontiguousarray(a.transpose(0, 4, 2, 1, 3, 5)).reshape(DEPTH * 2 * FC, 128, 2048)
    a = inp["ffn_w_out"].reshape(DEPTH * 2, 2, FC // 2, 128, KC, 128)
    w["ffn_wout"] = np.ascontiguousarray(a.transpose(0, 4, 1, 3, 2, 5)).reshape(DEPTH * 2 * KC * 2, 128, (FC // 2) * 128)

    def in2(x):
        n = x.shape[0]
        a = x.reshape(n, KC, 128, 2, KC, 128)
        return np.ascontiguousarray(a.transpose(0, 4, 2, 1, 3, 5)).reshape(n * KC, 128, 2048)

    def out1(x):
        n = x.shape[0]
        a = x.reshape(n, KC, 128, KC, 128)
        return np.ascontiguousarray(a.transpose(0, 3, 2, 1, 4)).reshape(n * KC, 128, 1024)

    w["rg_win"] = in2(inp["rg_w_in"])
    w["rg_wout"] = out1(inp["rg_w_out"])
    w["cv_win"] = in2(inp["cv_w_in"])
    w["cv_wout"] = out1(inp["cv_w_out"])
    w["fox_wout"] = out1(inp["fox_w_out"])
    fw = inp["fox_w_in"][0]
    a = fw[:, 0:2048].reshape(KC, 128, 16, 128)
    w["fox_wqk"] = np.ascontiguousarray(a.transpose(2, 1, 0, 3)).reshape(16, 128, 1024)
    a = fw[:, 2048:3072].reshape(KC, 128, 4, 256)
    w["fox_wv"] = np.ascontiguousarray(a.transpose(2, 1, 0, 3)).reshape(4, 128, 2048)
    a = fw[:, 3072:3088].reshape(KC, 128, NH)
    w["fox_wf"] = np.ascontiguousarray(a.transpose(1, 0, 2)).reshape(1, 128, KC * NH)
    for nm, key in (("rg_wa", "rg_w_a"), ("rg_wx", "rg_w_x")):
        x = inp[key]
        n = x.shape[0]
        w[nm] = np.ascontiguousarray(x.transpose(2, 0, 1, 3)).reshape(1, 128, n * KC * 128)
    return w


def const_inputs():
    c = {}
    tri = (np.arange(128)[:, None] <= np.arange(128)[None, :]).astype(np.float32)
    bd = np.zeros((128, 128), np.float32)
    bd[:64, :64] = 1.0
    bd[64:, 64:] = 1.0
    c["c_f32"] = np.ascontiguousarray(np.concatenate([np.ones((128, 128), np.float32), np.eye(128, dtype=np.float32), tri, bd,
                                                     (1.0 - tri) * np.float32(-30000.0)], axis=1))
    return c


DEBUG = []
FOXDBG = False
USE_FOX2 = True


def build_program(S, plan, wshapes, npv, pvidx):
    nc = bass.Bass("TRN2", target_bir_lowering=False)
    NT = S // TT
    NB = S // 128
    xT_d = nc.dram_tensor("xT", [D, S], F32, kind="ExternalInput").ap()
    out_d = nc.dram_tensor("outT", [D, S], F32, kind="ExternalOutput").ap()
    pv_d = nc.dram_tensor("pv", [128, npv], F32, kind="ExternalInput").ap()
    cf_d = nc.dram_tensor("c_f32", [128, 640], F32, kind="ExternalInput").ap()
    wd = {}
    ws = {}
    for nm, shp in wshapes.items():
        wd[nm] = nc.dram_tensor(nm, list(shp), F32, kind="ExternalInput").ap()
        ws[nm] = nc.dram_tensor(nm + "_bf", list(shp), BF16).ap()
    need_fox = any(sub == "mix" and l % 3 == 1 for l, sub in plan)
    if need_fox:
        kt_d = nc.dram_tensor("kt_s", [KC, 128, S], BF16).ap()
        kt2_d = nc.dram_tensor("kt2_s", [NH, 70, S], BF16).ap()
        v_d = nc.dram_tensor("v_s", [128, NB, NH * 128], BF16).ap()

    with contextlib.ExitStack() as es:
        sb_ctr = [0]

        def sb(name, shape, dt, stack=es):
            sb_ctr[0] += 1
            return stack.enter_context(nc.sbuf_tensor("%s_%d" % (name, sb_ctr[0]), shape, dt))

        X = sb("X", [128, KC, S], F32)
        xn = sb("xn", [128, KC, TT], BF16)
        sq = sb("sq", [128, 2, TT], BF16)
        rs0 = sb("rs0", [128, TT], F32)
        rs1 = sb("rs1", [128, TT], F32)
        pvt = sb("pvt", [128, npv], F32)
        cf = sb("cf", [128, 256], F32)
        cb = sb("cb", [128, 640], BF16)
        win_slots = [sb("win%d" % i, [128, 2048], BF16) for i in range(3)]
        wout_slots = [sb("wout%d" % i, [128, (FC // 2) * 128], BF16) for i in range(3)]
        psb = [es.enter_context(nc.psum_tensor("ps%d" % i, [128, 512], F32)) for i in range(8)]
        pss, pg, pu, po, px = psb[0], psb[1:3], psb[3:5], psb[5:7], psb[7]
        ones_bf = cb[:, 0:128]
        tri_bf = cb[:, 256:384]
        bd_bf = cb[:, 384:512]
        negm_bf = cb[:, 512:640]
        ident_bf = cb[:, 128:256]
        ones_f = cf[:, 0:128]
        ident_f = cf[:, 128:256]
        tri_f = None

        P = Prog(nc)
        WIN = Stream(P, "win", win_slots)
        WOUT = Stream(P, "wout", wout_slots)

        epsn = sb("epsn", [128, 1], F32)

        def pvc(name, k=0, w=1):
            if name == "eps":
                return epsn[:, 0:1]
            c0 = pvidx[name] + k
            return pvt[:, c0:c0 + w]

        P.pool(lambda e: e.memset(epsn[:], EPS), writes=["pvt2"])
        P.dma("sp", "ld_pv", lambda e: e.dma_start(out=pvt[:], in_=pv_d), writes=["pvt"])
        P.dma("sp", "ld_cf", lambda e: e.dma_start(out=cf[:], in_=cf_d[:, 0:256]), writes=["cf"])
        P.dma("pool", "ld_cb", lambda e: e.dma_start(out=cb[:], in_=cf_d), writes=["cb"])
        for t in range(NT):
            for k in range(KC):
                P.dma("sp", "ld_xt%d" % t,
                      lambda e, k=k, t=t: e.dma_start(out=X[:, k, t * TT:(t + 1) * TT],
                                                       in_=xT_d[k * 128:(k + 1) * 128, t * TT:(t + 1) * TT]),
                      writes=[("X", k, t)])

        cast_done = set()
        cast_ctr = [0]

        cast_after = [()]

        def cast(nm, lo, hi):
            key = (nm, lo, hi)
            if key in cast_done:
                return
            cast_done.add(key)
            shp = wshapes[nm]
            ncol = shp[2]
            src = wd[nm][lo:hi]
            dst = ws[nm][lo:hi]
            ch = "cast%d" % cast_ctr[0]
            cast_ctr[0] += 1
            kw = dict(max_dma_last_dim=2048) if ncol % 512 == 0 else dict(max_dma_last_dim=1408)
            ex = P.exempt
            P.exempt = True
            P.dma("pool", ch, lambda e, src=src, dst=dst, kw=kw: e.dma_start(out=dst, in_=src, **kw),
                  reads=list(cast_after[0]), writes=[("ws", nm, u) for u in range(lo, hi)])
            P.exempt = ex

        tile_hook = [None]

        def casts_for(i):
            if i >= len(plan):
                return
            l, sub = plan[i]
            if sub in ("f0", "f1"):
                ls = l * 2 + (0 if sub == "f0" else 1)
                cast("ffn_win", ls * FC, (ls + 1) * FC)
                cast("ffn_wout", ls * KC * 2, (ls + 1) * KC * 2)
            else:
                kind = l % 3
                j = l // 3
                if kind == 0:
                    cast("rg_win", j * KC, (j + 1) * KC)
                    cast("rg_wa", 0, 1)
                    cast("rg_wx", 0, 1)
                    cast("rg_wout", j * KC, (j + 1) * KC)
                elif kind == 1:
                    cast("fox_wqk", 0, 16)
                    cast("fox_wv", 0, 4)
                    cast("fox_wf", 0, 1)
                    cast("fox_wout", 0, KC)
                else:
                    cast("cv_win", 0, KC)
                    cast("cv_wout", 0, KC)

        def xtoks(t):
            return [("X", k, t) for k in range(KC)]

        def rmsnorm(t, gname, xo=None, xtok="xn"):
            if xo is None:
                xo = xn
            ex = P.exempt
            P.exempt = (xo is xn)
            c0, c1 = t * TT, (t + 1) * TT
            for k in range(KC):
                P.act(lambda e, k=k: e.activation(out=sq[:, k % 2, :], in_=X[:, k, c0:c1], func=AF.Square),
                      reads=[("X", k, t)], writes=[("sq", k % 2)])
                P.pe(lambda e, k=k: e.matmul(pss[:], lhsT=ones_bf, rhs=sq[:, k % 2, :], start=(k == 0), stop=(k == KC - 1)),
                     reads=[("sq", k % 2), "cb"], writes=["pss"])
            P.act(lambda e: e.activation(out=rs0[:], in_=pss[:], func=AF.Sqrt, scale=1.0 / D, bias=pvc("eps")),
                  reads=["pss", "pvt2"], writes=["rs0"])
            P.dve(lambda e: e.reciprocal(out=rs1[:], in_=rs0[:]), reads=["rs0"], writes=["rs1"])
            for k in range(KC):
                P.dve(lambda e, k=k: e.scalar_tensor_tensor(out=xo[:, k, :], in0=X[:, k, c0:c1], scalar=pvc(gname, k),
                                                            in1=rs1[:], op0=ALU.mult, op1=ALU.mult),
                      reads=[("X", k, t), "rs1", "pvt"], writes=[(xtok, k)])
            P.exempt = ex

        def resid_add(m, t, pbank, ptok, scale):
            c0, c1 = t * TT, (t + 1) * TT
            P.dve(lambda e: e.scalar_tensor_tensor(out=X[:, m, c0:c1], in0=pbank[:], scalar=float(scale),
                                                   in1=X[:, m, c0:c1], op0=ALU.mult, op1=ALU.add),
                  reads=[ptok, ("X", m, t)], writes=[("X", m, t)])

        def out_proj(t, wname, ubase, rhs_tile, rhs_tok, scale, nk=KC, bias_name=None):
            units = [(ws[wname][ubase + m], nk * 128, ("ws", wname, ubase + m), m) for m in range(KC)]

            def body(slot, tok, m):
                pb = po[m % 2]
                ptok = ("po", m % 2)
                for k in range(nk):
                    P.pe(lambda e, k=k, slot=slot, pb=pb: e.matmul(pb[:], lhsT=slot[:, k * 128:(k + 1) * 128], rhs=rhs_tile[:, k, :],
                                                                  start=(k == 0), stop=(k == nk - 1)),
                         reads=[tok, (rhs_tok, k)], writes=[ptok])
                if bias_name is not None:
                    c0, c1 = t * TT, (t + 1) * TT
                    P.dve(lambda e, m=m, pb=pb: e.scalar_tensor_tensor(out=X[:, m, c0:c1], in0=pb[:], scalar=pvc(bias_name, m),
                                                                       in1=X[:, m, c0:c1], op0=ALU.add, op1=ALU.add),
                          reads=[ptok, ("X", m, t), "pvt"], writes=[("X", m, t)])
                else:
                    resid_add(m, t, pb, ptok, scale)
            return units, body

        def ffn(l, s, stack):
            ls = l * 2 + s
            h = sb("h", [128, FC, TT], BF16, stack)
            sg = [sb("sg%d" % i, [128, TT], F32, stack) for i in range(2)]
            xn2 = sb("xn2", [128, KC, TT], BF16, stack)
            xnb = [(xn, "xn"), (xn2, "xn2")]
            gname = "fn%d%d" % (l, s)
            rmsnorm(0, gname, *xnb[0])
            for t in range(NT):
                xcur, xtk = xnb[t % 2]
                units = [(ws["ffn_win"][ls * FC + c], 2048, ("ws", "ffn_win", ls * FC + c), c) for c in range(FC)]

                def body(slot, tok, c, xcur=xcur, xtk=xtk):
                    b = c % 2
                    for g, pb, pn in ((0, pg[b], "pg"), (1, pu[b], "pu")):
                        for k in range(KC):
                            off = (k * 2 + g) * 128
                            P.pe(lambda e, k=k, off=off, pb=pb, slot=slot: e.matmul(pb[:], lhsT=slot[:, off:off + 128], rhs=xcur[:, k, :],
                                                                                    start=(k == 0), stop=(k == KC - 1)),
                                 reads=[tok, (xtk, k)], writes=[(pn, b)])
                    P.act(lambda e, b=b: e.activation(out=sg[b][:], in_=pg[b][:], func=AF.Silu),
                          reads=[("pg", b)], writes=[("sg", b)])
                    P.dve(lambda e, b=b, c=c: e.tensor_tensor(out=h[:, c, :], in0=sg[b][:], in1=pu[b][:], op=ALU.mult),
                          reads=[("sg", b), ("pu", b)], writes=[("h", c)])
                run_units(WIN, units, body, depth=2)
                HF = FC // 2
                units2 = [(ws["ffn_wout"][(ls * KC + m) * 2 + hf], HF * 128, ("ws", "ffn_wout", (ls * KC + m) * 2 + hf), (m, hf))
                          for m in range(KC) for hf in range(2)]

                def body2(slot, tok, mh, t=t):
                    m, hf = mh
                    pb = po[m % 2]
                    ptok = ("po", m % 2)
                    for ci in range(HF):
                        c = hf * HF + ci
                        P.pe(lambda e, c=c, ci=ci, slot=slot, pb=pb: e.matmul(pb[:], lhsT=slot[:, ci * 128:(ci + 1) * 128], rhs=h[:, c, :],
                                                                             start=(c == 0), stop=(c == FC - 1)),
                             reads=[tok, ("h", c)], writes=[ptok])
                    if hf == 1:
                        resid_add(m, t, pb, ptok, 0.5)
                hook = None
                if t + 1 < NT:
                    nxt = xnb[(t + 1) % 2]
                    hook = {3: (lambda t=t, nxt=nxt: rmsnorm(t + 1, gname, *nxt))}
                run_units(WOUT, units2, body2, depth=2, after=hook)
                if tile_hook[0] is not None:
                    tile_hook[0](t)

        def rglru(l, stack):
            j = l // 3
            f32t = lambda nm, w=TT: [sb(nm + str(i), [128, w], F32, stack) for i in range(2)]
            uext = f32t("uext", TT + 3)
            acc = f32t("acc")
            ra = f32t("ra")
            ib = f32t("ib")
            hh = f32t("hh")
            gt = f32t("gt")
            ub = [sb("ub%d" % i, [128, TT], BF16, stack) for i in range(2)]
            yb = sb("yb", [128, KC, TT], BF16, stack)
            ucar = sb("ucar", [128, KC, 3], F32, stack)
            hst = sb("hst", [128, KC], F32, stack)
            cl = sb("cl", [128, KC], F32, stack)
            wg = sb("wg", [128, 2, KC * 128], BF16, stack)
            Rb = [(px, "px"), (po[0], ("po", 0))]
            Ib = [(pss, "pss"), (po[1], ("po", 1))]
            P.dma("sp", "ld_wg0", lambda e: e.dma_start(out=wg[:, 0, :], in_=ws["rg_wa"][0][:, j * KC * 128:(j + 1) * KC * 128]),
                  reads=[("ws", "rg_wa", 0)], writes=["wg0"])
            P.dma("sp", "ld_wg1", lambda e: e.dma_start(out=wg[:, 1, :], in_=ws["rg_wx"][0][:, j * KC * 128:(j + 1) * KC * 128]),
                  reads=[("ws", "rg_wx", 0)], writes=["wg1"])
            P.act(lambda e: e.activation(out=cl[:], in_=pvc("rglam%d" % j, 0, KC), func=AF.Exp, scale=-1.0), reads=["pvt"], writes=["cl"])
            P.act(lambda e: e.activation(out=cl[:], in_=cl[:], func=AF.Ln, bias=1.0), reads=["cl"], writes=["cl"])
            P.dve(lambda e: e.tensor_scalar(out=cl[:], in0=cl[:], scalar1=-8.0, scalar2=None, op0=ALU.mult), reads=["cl"], writes=["cl"])
            P.pool(lambda e: e.memset(ucar[:], 0.0), writes=["ucar"])
            P.pool(lambda e: e.memset(hst[:], 0.0), writes=["hst"])

            def st_proj(c, b, slot, tok):
                for g, pb, pn in ((0, pg[b], "pg"), (1, pu[b], "pu")):
                    for k in range(KC):
                        off = (k * 2 + g) * 128
                        P.pe(lambda e, k=k, off=off, pb=pb, slot=slot: e.matmul(pb[:], lhsT=slot[:, off:off + 128], rhs=xn[:, k, :],
                                                                                start=(k == 0), stop=(k == KC - 1)),
                             reads=[tok, ("xn", k)], writes=[(pn, b)])

            def st_u(c, b):
                P.pool(lambda e: e.tensor_copy(out=uext[b][:, 0:3], in_=ucar[:, c, :]), reads=[("ucar", c)], writes=[("uext_c", b)])
                P.act(lambda e: e.activation(out=uext[b][:, 3:TT + 3], in_=pu[b][:], func=AF.Copy), reads=[("pu", b)], writes=[("uext", b)])
                P.act(lambda e: e.activation(out=acc[b][:], in_=pu[b][:], func=AF.Identity, scale=pvc("rgcw%d_3" % j, c), bias=pvc("rgcb%d" % j, c)),
                      reads=[("pu", b), "pvt"], writes=[("acc", b)])
                P.pool(lambda e: e.tensor_copy(out=ucar[:, c, :], in_=uext[b][:, TT:TT + 3]), reads=[("uext", b), ("uext_c", b)], writes=[("ucar", c)])

            def st_tap(c, b, kk):
                P.dve(lambda e: e.scalar_tensor_tensor(out=acc[b][:], in0=uext[b][:, kk:kk + TT], scalar=pvc("rgcw%d_%d" % (j, kk), c),
                                                       in1=acc[b][:], op0=ALU.mult, op1=ALU.add),
                      reads=[("uext", b), ("uext_c", b), ("acc", b), "pvt"], writes=[("acc", b)])

            def st_ub(c, b):
                P.act(lambda e: e.activation(out=ub[b][:], in_=acc[b][:], func=AF.Copy), reads=[("acc", b)], writes=[("ub", b)])

            def st_gates(c, b):
                P.pe(lambda e: e.matmul(Rb[b][0][:], lhsT=wg[:, 0, c * 128:(c + 1) * 128], rhs=ub[b][:], start=True, stop=True),
                     reads=["wg0", ("ub", b)], writes=[Rb[b][1]])
                P.pe(lambda e: e.matmul(Ib[b][0][:], lhsT=wg[:, 1, c * 128:(c + 1) * 128], rhs=ub[b][:], start=True, stop=True),
                     reads=["wg1", ("ub", b)], writes=[Ib[b][1]])

            def st_sig_r(c, b):
                P.act(lambda e: e.activation(out=ra[b][:], in_=Rb[b][0][:], func=AF.Sigmoid, bias=pvc("rgba%d" % j, c)),
                      reads=[Rb[b][1], "pvt"], writes=[("ra", b)])

            def st_sig_i(c, b):
                P.act(lambda e: e.activation(out=ib[b][:], in_=Ib[b][0][:], func=AF.Sigmoid, bias=pvc("rgbx%d" % j, c)),
                      reads=[Ib[b][1], "pvt"], writes=[("ib", b)])

            def st_a(c, b):
                P.act(lambda e: e.activation(out=ra[b][:], in_=ra[b][:], func=AF.Exp, scale=cl[:, c:c + 1]), reads=[("ra", b), "cl"], writes=[("ra", b)])

            def st_a2(c, b):
                P.act(lambda e: e.activation(out=uext[b][:, 0:TT], in_=ra[b][:], func=AF.Square), reads=[("ra", b), ("acc", b), ("ucar", c)],
                      writes=[("uext", b), ("uext_c", b)])

            def st_m(c, b):
                P.act(lambda e: e.activation(out=uext[b][:, 0:TT], in_=uext[b][:, 0:TT], func=AF.Sqrt, scale=-1.0, bias=1.0),
                      reads=[("uext", b), ("uext_c", b)], writes=[("uext", b), ("uext_c", b)])

            def st_b1(c, b):
                P.dve(lambda e: e.tensor_tensor(out=ib[b][:], in0=ib[b][:], in1=acc[b][:], op=ALU.mult), reads=[("ib", b), ("acc", b)], writes=[("ib", b)])

            def st_b2(c, b):
                P.dve(lambda e: e.tensor_tensor(out=ib[b][:], in0=ib[b][:], in1=uext[b][:, 0:TT], op=ALU.mult),
                      reads=[("ib", b), ("uext", b), ("uext_c", b)], writes=[("ib", b)])

            def st_scan(c, b):
                P.dve(lambda e: e.tensor_tensor_scan(out=hh[b][:], data0=ra[b][:], data1=ib[b][:], initial=hst[:, c:c + 1], op0=ALU.mult, op1=ALU.add),
                      reads=[("ra", b), ("ib", b), ("hst", c)], writes=[("hh", b)])
                P.pool(lambda e: e.tensor_copy(out=hst[:, c:c + 1], in_=hh[b][:, TT - 1:TT]), reads=[("hh", b)], writes=[("hst", c)])

            def st_g1(c, b):
                P.act(lambda e: e.activation(out=gt[b][:], in_=pg[b][:], func=AF.Square), reads=[("pg", b)], writes=[("gt", b)])

            def st_g2(c, b):
                P.act(lambda e: e.activation(out=gt[b][:], in_=gt[b][:], func=AF.Identity, scale=0.044715, bias=1.0), reads=[("gt", b)], writes=[("gt", b)])

            def st_g3(c, b):
                P.dve(lambda e: e.tensor_tensor(out=gt[b][:], in0=gt[b][:], in1=pg[b][:], op=ALU.mult), reads=[("gt", b), ("pg", b)], writes=[("gt", b)])

            def st_g4(c, b):
                P.act(lambda e: e.activation(out=gt[b][:], in_=gt[b][:], func=AF.Sigmoid, scale=GELU_C), reads=[("gt", b)], writes=[("gt", b)])

            def st_g5(c, b):
                P.dve(lambda e: e.tensor_tensor(out=gt[b][:], in0=gt[b][:], in1=pg[b][:], op=ALU.mult), reads=[("gt", b), ("pg", b)], writes=[("gt", b)])

            def st_y(c, b):
                P.dve(lambda e: e.tensor_tensor(out=yb[:, c, :], in0=gt[b][:], in1=hh[b][:], op=ALU.mult), reads=[("gt", b), ("hh", b)], writes=[("yb", c)])

            stages = [st_u, st_g1, lambda c, b: st_tap(c, b, 0), st_g2, lambda c, b: st_tap(c, b, 1), lambda c, b: st_tap(c, b, 2),
                      st_ub, st_g3, st_gates, st_g4, st_sig_r, st_sig_i, st_a, st_g5, st_a2, st_b1, st_m, st_b2, st_scan, st_y]

            for t in range(NT):
                rmsnorm(t, "mn%d" % l)
                for c2 in range(0, KC, 2):
                    cs_ = (c2, c2 + 1)
                    got = []
                    if t == 0 and c2 == 0:
                        for c in cs_:
                            WIN.prefetch(ws["rg_win"][j * KC + c], 2048, ("ws", "rg_win", j * KC + c))
                    for c in cs_:
                        got.append(WIN.get())
                    for c, (slot, tok) in zip(cs_, got):
                        st_proj(c, c % 2, slot, tok)
                    nxt = None
                    if c2 + 2 < KC:
                        nxt = (c2 + 2, c2 + 3)
                    elif t + 1 < NT:
                        nxt = (0, 1)
                    if nxt:
                        for c in nxt:
                            WIN.prefetch(ws["rg_win"][j * KC + c], 2048, ("ws", "rg_win", j * KC + c))
                    for stg in stages:
                        for c in cs_:
                            stg(c, c % 2)
                u2, b2 = out_proj(t, "rg_wout", j * KC, yb, "yb", 1.0)
                run_units(WOUT, u2, b2, depth=2)

        def convmod(l, stack):
            f32t = lambda nm, w=TT: sb(nm, [128, w], F32, stack)
            PAD = CVK - 1
            gext = [sb("gext%d" % i, [128, TT + PAD], BF16, stack) for i in range(2)]
            sgm = f32t("sgm")
            cvo = sb("cvo", [128, KC, TT], F32, stack)
            sq32 = f32t("sq32")
            mu = f32t("mu")
            rstd = f32t("rstd")
            ybc = xn
            gcar = sb("gcar", [128, KC, PAD], BF16, stack)
            dg = sb("dg", [128, CVK, 128], BF16, stack)
            P.pool(lambda e: e.memset(gcar[:], 0.0), writes=["gcar"])
            for t in range(NT):
                rmsnorm(t, "mn%d" % l)
                units = [(ws["cv_win"][c], 2048, ("ws", "cv_win", c), c) for c in range(KC)]

                def body(slot, tok, c):
                    b = c % 2
                    T_ = lambda nm: (nm, b)
                    for g, pb, pn in ((0, pg[b], "pg"), (1, pu[b], "pu")):
                        for k in range(KC):
                            off = (k * 2 + g) * 128
                            P.pe(lambda e, k=k, off=off, pb=pb, slot=slot: e.matmul(pb[:], lhsT=slot[:, off:off + 128], rhs=xn[:, k, :],
                                                                                    start=(k == 0), stop=(k == KC - 1)),
                                 reads=[tok, ("xn", k)], writes=[(pn, b)])
                    P.act(lambda e, c=c, b=b: e.activation(out=sgm[:], in_=pu[b][:], func=AF.Sigmoid, bias=pvc("cvbing", c)),
                          reads=[("pu", b), "pvt"], writes=["sgm"])
                    P.pool(lambda e, c=c, b=b: e.tensor_copy(out=gext[b][:, 0:PAD], in_=gcar[:, c, :]), reads=["gcar"], writes=[T_("gext_c")])
                    P.dve(lambda e, c=c, b=b: e.scalar_tensor_tensor(out=gext[b][:, PAD:PAD + TT], in0=pg[b][:], scalar=pvc("cvbin", c), in1=sgm[:],
                                                                     op0=ALU.add, op1=ALU.mult),
                          reads=[("pg", b), "sgm", "pvt"], writes=[T_("gext")])
                    P.pool(lambda e, c=c, b=b: e.tensor_copy(out=gcar[:, c, :], in_=gext[b][:, TT:TT + PAD]), reads=[T_("gext"), T_("gext_c")], writes=["gcar"])
                    for kk in range(CVK):
                        if kk % 2 == 0:
                            P.act(lambda e, c=c, kk=kk: e.activation(out=dg[:, kk, :], in_=ident_f, func=AF.Copy, scale=pvc("cvdw%d" % kk, c)),
                                  reads=["cf", "pvt"], writes=[("dg", kk)])
                        else:
                            P.dve(lambda e, c=c, kk=kk: e.tensor_scalar(out=dg[:, kk, :], in0=ident_f, scalar1=pvc("cvdw%d" % kk, c), scalar2=None, op0=ALU.mult),
                                  reads=["cf", "pvt"], writes=[("dg", kk)])
                    for kk in range(CVK):
                        P.pe(lambda e, kk=kk, b=b: e.matmul(po[b][:], lhsT=dg[:, kk, :], rhs=gext[b][:, kk:kk + TT], start=(kk == 0), stop=(kk == CVK - 1)),
                             reads=[("dg", kk), T_("gext"), T_("gext_c")], writes=[("po", b)])
                    P.dve(lambda e, c=c, b=b: e.tensor_scalar(out=cvo[:, c, :], in0=po[b][:], scalar1=pvc("cvdwb", c), scalar2=None, op0=ALU.add),
                          reads=[("po", b), "pvt"], writes=[("cvo", c)])
                    P.pe(lambda e, c=c: e.matmul(pss[:], lhsT=ones_f, rhs=cvo[:, c, :], start=(c == 0), stop=(c == KC - 1), skip_group_check=True),
                         reads=[("cvo", c), "cf"], writes=["pss"])
                    P.act(lambda e, c=c: e.activation(out=sq32[:], in_=cvo[:, c, :], func=AF.Square), reads=[("cvo", c)], writes=["sq32"])
                    P.pe(lambda e, c=c: e.matmul(px[:], lhsT=ones_f, rhs=sq32[:], start=(c == 0), stop=(c == KC - 1), skip_group_check=True),
                         reads=["sq32", "cf"], writes=["px"])
                run_units(WIN, units, body, depth=2)
                P.dve(lambda e: e.tensor_scalar(out=mu[:], in0=pss[:], scalar1=1.0 / D, scalar2=None, op0=ALU.mult), reads=["pss"], writes=["mu"])
                P.dve(lambda e: e.tensor_tensor(out=rstd[:], in0=mu[:], in1=mu[:], op=ALU.mult), reads=["mu"], writes=["rstd"])
                P.dve(lambda e: e.scalar_tensor_tensor(out=rstd[:], in0=px[:], scalar=1.0 / D, in1=rstd[:], op0=ALU.mult, op1=ALU.subtract),
                      reads=["px", "rstd"], writes=["rstd"])
                P.act(lambda e: e.activation(out=rstd[:], in_=rstd[:], func=AF.Sqrt, bias=pvc("eps")), reads=["rstd", "pvt2"], writes=["rstd"])
                P.dve(lambda e: e.reciprocal(out=rstd[:], in_=rstd[:]), reads=["rstd"], writes=["rstd"])
                for c in range(KC):
                    P.dve(lambda e, c=c: e.tensor_tensor(out=sgm[:], in0=cvo[:, c, :], in1=mu[:], op=ALU.subtract),
                          reads=[("cvo", c), "mu"], writes=["sgm"])
                    P.dve(lambda e, c=c: e.scalar_tensor_tensor(out=sgm[:], in0=sgm[:], scalar=pvc("cvlng", c), in1=rstd[:], op0=ALU.mult, op1=ALU.mult),
                          reads=["sgm", "rstd", "pvt"], writes=["sgm"])
                    P.act(lambda e, c=c: e.activation(out=ybc[:, c, :], in_=sgm[:], func=AF.Silu, bias=pvc("cvlnb", c)),
                          reads=["sgm", "pvt"], writes=[("xn", c)])
                u2, b2 = out_proj(t, "cv_wout", 0, ybc, "xn", 1.0, bias_name="cvbout")
                run_units(WOUT, u2, b2, depth=2)

        def fox(l, stack):
            qn = [sb("qn%d" % i, [128, TT], BF16, stack) for i in range(2)]
            kst = [sb("kst%d" % i, [128, TT], BF16, stack) for i in range(2)]
            vst = sb("vst", [128, NH * 128], BF16, stack)
            ot = sb("ot", [128, KC, TT], BF16, stack)
            PB = 4
            kbuf = [sb("kbuf%d" % i, [128, PB * 128], BF16, stack) for i in range(2)]
            vbuf = [sb("vbuf%d" % i, [128, PB, 256], BF16, stack) for i in range(2)]
            pT = [sb("pT%d" % i, [128, 128], BF16, stack) for i in range(4)]
            call = sb("call", [128, NB, NH], F32, stack)
            runtot = sb("runtot", [128, NB + 1, NH], F32, stack)
            bias4 = [sb("bias%d" % i, [128, NB, NH], F32, stack) for i in range(4)]
            rc = sb("rc", [128, TT], F32, stack)
            lnv = sb("lnv", [128, NH], F32, stack)
            wf = sb("wf", [128, KC * NH], BF16, stack)
            qg8 = sb("qg8", [128, 1], F32, stack)
            sqq = sb("sqq", [128, TT], BF16, stack)
            P.dma("sp", "ld_wf", lambda e: e.dma_start(out=wf[:], in_=ws["fox_wf"][0]), reads=[("ws", "fox_wf", 0)], writes=["wf"])
            P.dve(lambda e: e.tensor_scalar(out=qg8[:], in0=pvc("fqg"), scalar1=DH ** -0.5, scalar2=None, op0=ALU.mult), reads=["pvt"], writes=["qg8"])
            P.pool(lambda e: e.memset(runtot[:, 0, :], 0.0), writes=[("runtot", 0)])
            P.pool(lambda e: e.memset(vst[:], 1.0), writes=["vst"])
            pt_ctr = [0]
            kv_ctr = [0]

            def qk_proj(unit_idx, dst, dst_tok, gain_ap):
                WIN.prefetch(ws["fox_wqk"][unit_idx], 1024, ("ws", "fox_wqk", unit_idx))
                slot, tok = WIN.get()
                for k in range(KC):
                    P.pe(lambda e, k=k, slot=slot: e.matmul(pg[0][:], lhsT=slot[:, k * 128:(k + 1) * 128], rhs=xn[:, k, :],
                                                            start=(k == 0), stop=(k == KC - 1)),
                         reads=[tok, ("xn", k)], writes=[("pg", 0)])
                P.act(lambda e: e.activation(out=sqq[:], in_=pg[0][:], func=AF.Square), reads=[("pg", 0)], writes=["sqq"])
                P.pe(lambda e: e.matmul(pu[0][:], lhsT=bd_bf, rhs=sqq[:], start=True, stop=True), reads=["sqq", "cb"], writes=[("pu", 0)])
                P.act(lambda e: e.activation(out=rs0[:], in_=pu[0][:], func=AF.Sqrt, scale=1.0 / DH, bias=pvc("eps")),
                      reads=[("pu", 0), "pvt2"], writes=["rs0"])
                P.dve(lambda e: e.reciprocal(out=rs1[:], in_=rs0[:]), reads=["rs0"], writes=["rs1"])
                P.dve(lambda e: e.scalar_tensor_tensor(out=dst[:], in0=pg[0][:], scalar=gain_ap, in1=rs1[:], op0=ALU.mult, op1=ALU.mult),
                      reads=[("pg", 0), "rs1", "qg8", "pvt"], writes=[dst_tok])

            for t in range(NT):
                c0 = t * TT
                rmsnorm(t, "mn%d" % l)
                for b in range(4):
                    blk = t * 4 + b
                    for k in range(KC):
                        P.pe(lambda e, k=k, b=b: e.matmul(px[:, 0:NH], lhsT=xn[:, k, b * 128:(b + 1) * 128], rhs=wf[:, k * NH:(k + 1) * NH],
                                                          start=(k == 0), stop=(k == KC - 1)),
                             reads=["wf", ("xn", k)], writes=["px"])
                    P.dve(lambda e: e.tensor_tensor(out=lnv[:], in0=px[:, 0:NH], in1=pvc("fbf", 0, NH), op=ALU.add), reads=["px", "pvt"], writes=["lnv"])
                    P.act(lambda e: e.activation(out=lnv[:], in_=lnv[:], func=AF.Exp, scale=-1.0), reads=["lnv"], writes=["lnv"])
                    P.act(lambda e: e.activation(out=lnv[:], in_=lnv[:], func=AF.Ln, bias=1.0), reads=["lnv"], writes=["lnv"])
                    P.pe(lambda e: e.matmul(pss[:, 0:NH], lhsT=tri_f, rhs=lnv[:], start=True, stop=True, skip_group_check=True), reads=["lnv", "cf"], writes=["pss"])
                    P.pe(lambda e: e.matmul(pss[:, NH:2 * NH], lhsT=ones_f, rhs=lnv[:], start=False, stop=True, skip_group_check=True),
                         reads=["lnv", "cf"], writes=["pss"])
                    P.dve(lambda e, blk=blk: e.tensor_tensor(out=call[:, blk, :], in0=pss[:, 0:NH], in1=runtot[:, blk, :], op=ALU.add),
                          reads=["pss", ("runtot", blk)], writes=[("call", blk)])
                    P.dve(lambda e, blk=blk: e.tensor_tensor(out=runtot[:, blk + 1, :], in0=pss[:, NH:2 * NH], in1=runtot[:, blk, :], op=ALU.add),
                          reads=["pss", ("runtot", blk)], writes=[("runtot", blk + 1)])
                for c in range(KC):
                    ks = kst[c % 2]
                    qk_proj(8 + c, ks, ("kst", c % 2), pvc("fkg"))
                    P.dma("sp", "st_k%d" % (c % 2), lambda e, c=c, ks=ks, c0=c0: e.dma_start(out=kt_d[c][:, c0:c0 + TT], in_=ks[:]),
                          reads=[("kst", c % 2)], writes=[("kt_d", c, t)])
                for b in range(4):
                    blk = t * 4 + b
                    units = [(ws["fox_wv"][g], 2048, ("ws", "fox_wv", g), g) for g in range(4)]

                    def vbody(slot, tok, g, b=b):
                        pb = pu[0] if g % 2 == 0 else pg[0]
                        ptok = ("pu", 0) if g % 2 == 0 else ("pg", 0)
                        for k in range(KC):
                            P.pe(lambda e, k=k, slot=slot, pb=pb: e.matmul(pb[:, 0:256], lhsT=xn[:, k, b * 128:(b + 1) * 128],
                                                                          rhs=slot[:, k * 256:(k + 1) * 256], start=(k == 0), stop=(k == KC - 1)),
                                 reads=[tok, ("xn", k)], writes=[ptok])
                        for i in range(4):
                            hd = 4 * g + i
                            d0 = hd * 128 + (hd % 2) * 64
                            P.act(lambda e, pb=pb, i=i, d0=d0: e.activation(out=vst[:, d0:d0 + 64], in_=pb[:, i * 64:(i + 1) * 64], func=AF.Copy),
                                  reads=[ptok], writes=["vst"])
                    run_units(WIN, units, vbody, depth=2)
                    P.dma("sp", "st_v", lambda e, blk=blk: e.dma_start(out=v_d[:, blk, :], in_=vst[:]), reads=["vst"], writes=[("v_d", blk)])
                for qb in range(4):
                    I = t * 4 + qb
                    for h_ in range(NH):
                        P.pool(lambda e, qb=qb, I=I, h_=h_: e.tensor_scalar(out=bias4[qb][:, 0:I + 1, h_], in0=call[:, 0:I + 1, h_],
                                                                            scalar1=runtot[:, I + 1, h_:h_ + 1], scalar2=None, op0=ALU.subtract),
                               reads=[("call", jb) for jb in range(t * 4, I + 1)] + [("runtot", I + 1)], writes=[("bias", qb)])
                for c in range(KC):
                    q = qn[c % 2]
                    qtok = ("qn", c % 2)
                    qk_proj(c, q, qtok, qg8[:, 0:1])
                    P.dve(lambda e: e.memset(po[0][:], 0.0), writes=[("po", 0)])
                    P.dve(lambda e: e.memset(po[1][:], 0.0), writes=[("po", 1)])
                    nblk = (t + 1) * 4
                    for j0 in range(0, nblk, PB):
                        nj = min(PB, nblk - j0)
                        kvi = kv_ctr[0] % 2
                        kv_ctr[0] += 1
                        kb, vb = kbuf[kvi], vbuf[kvi]
                        ktoks = [("kt_d", c, tt_) for tt_ in range(j0 // 4, (j0 + nj + 3) // 4)]
                        P.dma("sp", "ld_k%d" % kvi,
                              lambda e, kb=kb, j0=j0, nj=nj, c=c: e.dma_start(out=kb[:, 0:nj * 128], in_=kt_d[c][:, j0 * 128:(j0 + nj) * 128]),
                              reads=ktoks, writes=[("kbuf", kvi)])
                        P.dma("sp", "ld_v%d" % kvi,
                              lambda e, vb=vb, j0=j0, nj=nj, c=c: e.dma_start(
                                  out=vb[:, 0:nj, :], in_=v_d[:, j0:j0 + nj, 2 * c * 128:(2 * c + 2) * 128]),
                              reads=[("v_d", jj) for jj in range(j0, j0 + nj)], writes=[("vbuf", kvi)])
                        pairs = []
                        for hh in range(2):
                            for qb in range(4):
                                I = t * 4 + qb
                                for jj in range(nj):
                                    if j0 + jj > I:
                                        break
                                    pairs.append((hh, qb, jj, j0 + jj, I))
                        LA = 2
                        slots_of = {}

                        def emit_S(n):
                            hh, qb, jj, jblk, I = pairs[n]
                            h_ = 2 * c + hh
                            p0 = 64 * hh
                            pi = pt_ctr[0] % 4
                            pt_ctr[0] += 1
                            slots_of[n] = pi
                            sbank = pg[1] if (pi % 2 == 0) else pu[1]
                            stok = ("S", pi)
                            scol = (pi // 2) * 128
                            P.pe(lambda e, jj=jj, qb=qb, p0=p0, sbank=sbank, scol=scol, kb=kb, q=q:
                                 e.matmul(sbank[:, scol:scol + 128], lhsT=kb[p0:p0 + 64, jj * 128:(jj + 1) * 128],
                                          rhs=q[p0:p0 + 64, qb * 128:(qb + 1) * 128], start=True, stop=True, skip_group_check=True),
                                 reads=[("kbuf", kvi), qtok], writes=[stok])
                            P.act(lambda e, sbank=sbank, scol=scol, pi=pi, qb=qb, jblk=jblk, h_=h_:
                                  e.activation(out=pT[pi][:], in_=sbank[:, scol:scol + 128], func=AF.Exp, bias=bias4[qb][:, jblk, h_:h_ + 1]),
                                  reads=[stok, ("bias", qb)], writes=[("pT", pi)])
                            if jblk == I:
                                P.pool(lambda e, pi=pi: e.tensor_tensor(out=pT[pi][:], in0=pT[pi][:], in1=tri_bf, op=ALU.mult),
                                       reads=[("pT", pi), "cb"], writes=[("pT", pi)])

                        def emit_PV(n):
                            hh, qb, jj, jblk, I = pairs[n]
                            pi = slots_of[n]
                            P.pe(lambda e, jj=jj, hh=hh, pi=pi, qb=qb, vb=vb:
                                 e.matmul(po[hh][:, qb * 128:(qb + 1) * 128], lhsT=vb[:, jj, hh * 128:(hh + 1) * 128], rhs=pT[pi][:],
                                          start=False, stop=True, skip_group_check=True),
                                 reads=[("vbuf", kvi), ("pT", pi)], writes=[("po", hh)])

                        for n in range(len(pairs) + LA):
                            if n < len(pairs):
                                emit_S(n)
                            if n - LA >= 0:
                                emit_PV(n - LA)
                    P.dve(lambda e: e.reciprocal(out=rc[0:64, :], in_=po[0][64:128, :]), reads=[("po", 0)], writes=["rc0"])
                    P.dve(lambda e, c=c: e.tensor_tensor(out=ot[0:64, c, :], in0=po[0][0:64, :], in1=rc[0:64, :], op=ALU.mult),
                          reads=[("po", 0), "rc0"], writes=[("ot", c)])
                    P.dve(lambda e: e.reciprocal(out=rc[64:128, :], in_=po[1][0:64, :]), reads=[("po", 1)], writes=["rc1"])
                    P.dve(lambda e, c=c: e.tensor_tensor(out=ot[64:128, c, :], in0=po[1][64:128, :], in1=rc[64:128, :], op=ALU.mult),
                          reads=[("po", 1), "rc1", ("ot", c)], writes=[("ot", c)])
                u2, b2 = out_proj(t, "fox_wout", 0, ot, "ot", 1.0)
                run_units(WOUT, u2, b2, depth=2)
            if DEBUG is not None and FOXDBG:
                DEBUG.append(("call", lambda: call[:, 0:4, :], 64))
                DEBUG.append(("runtot", lambda: runtot[:, 0:5, :], 80))
                DEBUG.append(("bias0", lambda: bias4[0][:, 0:4, :], 64))
                DEBUG.append(("bias3", lambda: bias4[3][:, 0:4, :], 64))
                DEBUG.append(("rc", lambda: rc[:], 512))
                DEBUG.append(("po0", lambda: rs0[:], 512))
                DEBUG.append(("lnv", lambda: lnv[:], 16))
                DEBUG.append(("kbuf0", lambda: kbuf[0][:], 512))
                DEBUG.append(("kbuf1", lambda: kbuf[1][:], 512))
                DEBUG.append(("vbuf0", lambda: vbuf[0][:, 0:2, :], 512))
                DEBUG.append(("vbuf1", lambda: vbuf[1][:, 0:2, :], 512))
                for i_ in range(4):
                    DEBUG.append(("pT%d" % i_, lambda i_=i_: pT[i_][:], 128))
                DEBUG.append(("qn0", lambda: qn[0][:], 512))
                DEBUG.append(("qn1", lambda: qn[1][:], 512))
                DEBUG.append(("ot7", lambda: ot[:, 7, :], 512))
                DEBUG.append(("ot0", lambda: ot[:, 0, :], 512))
                DEBUG.append(("kst0", lambda: kst[0][:], 512))
                DEBUG.append(("vst", lambda: vst[:, 0:512], 512))

        def fox2(l, stack):
            KA = 70
            qb_ = [sb("q%d" % i, [128, TT], BF16, stack) for i in range(4)]
            kst = [sb("kst%d" % i, [128, TT], BF16, stack) for i in range(2)]
            vst = [sb("vst%d" % i, [128, 4 * 128], BF16, stack) for i in range(2)]
            ot = sb("ot", [128, KC, TT], BF16, stack)
            PB = 4
            kbuf = [[sb("kb%d_%d" % (i, hh), [128, PB * 128], BF16, stack) for hh in range(2)] for i in range(2)]
            vbuf = [sb("vbuf%d" % i, [128, PB, 256], BF16, stack) for i in range(2)]
            pT = [sb("pT%d" % i, [128, TT], BF16, stack) for i in range(3)]
            rc = sb("rc", [128, TT], F32, stack)
            sqq = sb("sqq", [128, TT], BF16, stack)
            lnv = sb("lnv", [NH, TT], F32, stack)
            Cc = sb("Cc", [NH, TT], F32, stack)
            cs = sb("cs", [NH, 3, TT], BF16, stack)
            ncs = sb("ncs", [NH, 3, TT], BF16, stack)
            ctot = sb("ctot", [NH, 1], F32, stack)
            nbf = sb("nbf", [NH, 1], F32, stack)
            wf = sb("wf", [128, KC * NH], BF16, stack)
            qg8 = sb("qg8", [128, 1], F32, stack)
            Sb = [pg[1], pu[1], px]
            P.dma("sp", "ld_wf", lambda e: e.dma_start(out=wf[:], in_=ws["fox_wf"][0]), reads=[("ws", "fox_wf", 0)], writes=["wf"])
            P.dve(lambda e: e.tensor_scalar(out=qg8[:], in0=pvc("fqg"), scalar1=DH ** -0.5, scalar2=None, op0=ALU.mult), reads=["pvt"], writes=["qg8"])
            P.dve(lambda e: e.tensor_scalar(out=nbf[:], in0=pvt[0:NH, pvidx["fbfc"]:pvidx["fbfc"] + 1], scalar1=-1.0, scalar2=None, op0=ALU.mult),
                  reads=["pvt"], writes=["nbf"])
            P.pool(lambda e: e.memset(ctot[:], 0.0), writes=["ctot"])
            for i in range(2):
                P.pool(lambda e, i=i: e.memset(vst[i][:], 1.0), writes=[("vst", i)])
            P.pool(lambda e: e.memset(cs[:], 1.0), writes=["cs"])
            for t in range(NT):
                P.dma("sp", "st_ko", lambda e, t=t: e.dma_start(out=kt2_d[:, 67:70, t * TT:(t + 1) * TT], in_=cs[:]), reads=["cs"], writes=[("kt2o", t)])
            for i in range(4):
                P.pool(lambda e, i=i: e.memset(qb_[i][64:67, :], 1.0), writes=[("qaug1", i)])
            q_ctr = [0]
            s_ctr = [0]
            kv_ctr = [0]
            vst_ctr = [0]

            def head_norm(src_ps, src_tok, gain_ap, dsts, square_done=False):
                if not square_done:
                    P.act(lambda e: e.activation(out=sqq[:], in_=src_ps[:], func=AF.Square), reads=[src_tok], writes=["sqq"])
                P.pe(lambda e: e.matmul(pu[0][:], lhsT=bd_bf, rhs=sqq[:], start=True, stop=True), reads=["sqq", "cb"], writes=[("pu", 0)])
                P.act(lambda e: e.activation(out=rs0[:], in_=pu[0][:], func=AF.Ln, scale=1.0 / DH, bias=pvc("eps")),
                      reads=[("pu", 0), "pvt2"], writes=["rs0"])
                P.act(lambda e: e.activation(out=rs1[:], in_=rs0[:], func=AF.Exp, scale=-0.5), reads=["rs0"], writes=["rs1"])
                for (p0, p1, dst_ap, dst_tok) in dsts:
                    P.dve(lambda e, p0=p0, p1=p1, dst_ap=dst_ap: e.scalar_tensor_tensor(out=dst_ap, in0=src_ps[p0:p1, :], scalar=gain_ap[p0:p1, 0:1],
                                                                                         in1=rs1[p0:p1, :], op0=ALU.mult, op1=ALU.mult),
                          reads=[src_tok, "rs1", "qg8", "pvt", "sqq"], writes=[dst_tok])

            for t in range(NT):
                c0 = t * TT
                rmsnorm(t, "mn%d" % l)
                for k in range(KC):
                    P.pe(lambda e, k=k: e.matmul(pu[0][0:NH, :], lhsT=wf[:, k * NH:(k + 1) * NH], rhs=xn[:, k, :], start=(k == 0), stop=(k == KC - 1)),
                         reads=["wf", ("xn", k)], writes=[("pu", 0)])
                P.act(lambda e: e.activation(out=lnv[:], in_=pu[0][0:NH, :], func=AF.Exp, scale=-1.0, bias=nbf[:, 0:1]), reads=[("pu", 0), "nbf"], writes=["lnv"])
                P.act(lambda e: e.activation(out=lnv[:], in_=lnv[:], func=AF.Ln, bias=1.0), reads=["lnv"], writes=["lnv"])
                for qq in range(4):
                    ini = ctot[:, 0:1] if qq == 0 else Cc[:, qq * 128 - 1:qq * 128]
                    P.dve(lambda e, qq=qq, ini=ini: e.tensor_tensor_scan(out=Cc[:, qq * 128:(qq + 1) * 128], data0=ones_f[0:NH, :],
                                                                         data1=lnv[:, qq * 128:(qq + 1) * 128], initial=ini, op0=ALU.mult, op1=ALU.add),
                          reads=["lnv", "cf", "ctot", "Cc"], writes=["Cc"])
                P.dve(lambda e: e.tensor_copy(out=ctot[:], in_=Cc[:, TT - 1:TT]), reads=["Cc"], writes=["ctot"])
                P.dve(lambda e: e.tensor_copy(out=cs[:, 0, :], in_=Cc[:]), reads=["Cc"], writes=["cs"])
                P.dve(lambda e: e.tensor_tensor(out=lnv[:], in0=Cc[:], in1=cs[:, 0, :], op=ALU.subtract), reads=["Cc", "cs"], writes=["lnv"])
                P.dve(lambda e: e.tensor_copy(out=cs[:, 1, :], in_=lnv[:]), reads=["lnv"], writes=["cs"])
                P.dve(lambda e: e.tensor_tensor(out=lnv[:], in0=lnv[:], in1=cs[:, 1, :], op=ALU.subtract), reads=["lnv", "cs"], writes=["lnv"])
                P.dve(lambda e: e.tensor_copy(out=cs[:, 2, :], in_=lnv[:]), reads=["lnv"], writes=["cs"])
                P.dve(lambda e: e.tensor_scalar(out=ncs[:], in0=cs[:], scalar1=-1.0, scalar2=None, op0=ALU.mult), reads=["cs"], writes=["ncs"])
                P.dma("sp", "st_kc", lambda e, c0=c0: e.dma_start(out=kt2_d[:, 64:67, c0:c0 + TT], in_=cs[:]), reads=["cs"], writes=[("kt2c", t)])
                kunits = [(ws["fox_wqk"][8 + c], 1024, ("ws", "fox_wqk", 8 + c), c) for c in range(KC)]

                def kbody(slot, tok, c, c0=c0, t=t):
                    ks = kst[c % 2]
                    pj, pjt = (pg[0], ("pg", 0)) if c % 2 == 0 else (pss, "pss")
                    for k in range(KC):
                        P.pe(lambda e, k=k, slot=slot, pj=pj: e.matmul(pj[:], lhsT=slot[:, k * 128:(k + 1) * 128], rhs=xn[:, k, :],
                                                                      start=(k == 0), stop=(k == KC - 1)),
                             reads=[tok, ("xn", k)], writes=[pjt])
                    head_norm(pj, pjt, pvc("fkg"), [(0, 128, ks[:], ("kst", c % 2))])
                    for hh in range(2):
                        P.dma("sp", "st_k%d" % (c % 2), lambda e, c=c, hh=hh, ks=ks, c0=c0: e.dma_start(out=kt2_d[2 * c + hh][0:64, c0:c0 + TT], in_=ks[64 * hh:64 * hh + 64, :]),
                              reads=[("kst", c % 2)], writes=[("kt2", 2 * c + hh, t)])
                def vbody(slot, tok, g, t=t):
                    for b in range(4):
                        blk = t * 4 + b
                        pb, ptok = (pg[1], ("S", 0)) if b % 2 == 0 else (pu[1], ("S", 1))
                        for k in range(KC):
                            P.pe(lambda e, k=k, slot=slot, pb=pb, b=b: e.matmul(pb[:, 0:256], lhsT=xn[:, k, b * 128:(b + 1) * 128],
                                                                               rhs=slot[:, k * 256:(k + 1) * 256], start=(k == 0), stop=(k == KC - 1)),
                                 reads=[tok, ("xn", k)], writes=[ptok])
                        vi = vst_ctr[0] % 2
                        vst_ctr[0] += 1
                        vs = vst[vi]
                        for i in range(4):
                            d0 = i * 128 + (i % 2) * 64
                            P.dve(lambda e, pb=pb, i=i, d0=d0, vs=vs: e.tensor_copy(out=vs[:, d0:d0 + 64], in_=pb[:, i * 64:(i + 1) * 64]),
                                  reads=[ptok], writes=[("vst", vi)])
                        P.dma("sp", "st_v%d" % vi, lambda e, blk=blk, g=g, vs=vs: e.dma_start(out=v_d[:, blk, g * 512:(g + 1) * 512], in_=vs[:]),
                              reads=[("vst", vi)], writes=[("v_d", blk, g)])

                vunits = [(ws["fox_wv"][g], 2048, ("ws", "fox_wv", g), g) for g in range(4)]
                merged = []
                for c in range(KC):
                    merged.append(kunits[c][:3] + (("k", kunits[c][3]),))
                    if c % 2 == 1:
                        u = vunits[c // 2]
                        merged.append(u[:3] + (("v", u[3]),))

                def kvbody(slot, tok, pl):
                    if pl[0] == "k":
                        kbody(slot, tok, pl[1])
                    else:
                        vbody(slot, tok, pl[1])
                run_units(WIN, merged, kvbody, depth=2)
                nblk = (t + 1) * 4

                def q_make_a(c):
                    slot, tok = WIN.get()
                    if c + 2 < KC:
                        WIN.prefetch(ws["fox_wqk"][c + 2], 1024, ("ws", "fox_wqk", c + 2))
                    pj, pjt = (pg[0], ("pg", 0)) if c % 2 == 0 else (pss, "pss")
                    for k in range(KC):
                        P.pe(lambda e, k=k, slot=slot, pj=pj: e.matmul(pj[:], lhsT=slot[:, k * 128:(k + 1) * 128], rhs=xn[:, k, :],
                                                                      start=(k == 0), stop=(k == KC - 1)),
                             reads=[tok, ("xn", k)], writes=[pjt])
                    P.act(lambda e, pj=pj: e.activation(out=sqq[:], in_=pj[:], func=AF.Square), reads=[pjt], writes=["sqq"])
                    return (pj, pjt)

                def q_make_b(c, pjx):
                    pj, pjt = pjx
                    qi0 = q_ctr[0] % 4
                    qi1 = (q_ctr[0] + 1) % 4
                    q_ctr[0] += 2
                    qA, qB = qb_[qi0], qb_[qi1]
                    head_norm(pj, pjt, qg8, [(0, 64, qA[0:64, :], ("q", qi0)), (64, 128, sqq[64:128, :], "sqq")], square_done=True)
                    P.act(lambda e, qB=qB: e.activation(out=qB[0:64, :], in_=sqq[64:128, :], func=AF.Copy), reads=["sqq"], writes=[("q", qi1)])
                    res = []
                    for hh, (qt, qi) in enumerate(((qA, qi0), (qB, qi1))):
                        h_ = 2 * c + hh
                        P.dma("sp", "ld_qa%d" % qi, lambda e, qt=qt, h_=h_: e.dma_start(out=qt[67:70, :], in_=ncs[h_:h_ + 1, :, :]),
                              reads=["ncs"], writes=[("qaug2", qi)])
                        res.append((qt, [("q", qi), ("qaug1", qi), ("qaug2", qi)]))
                    return res

                def q_make(c):
                    return q_make_b(c, q_make_a(c))

                pieces = []
                for c in range(KC):
                    for j0 in range(0, nblk, PB):
                        pieces.append((c, j0, min(PB, nblk - j0)))
                piece_kvi = {}

                def load_piece(ip):
                    c, j0, nj = pieces[ip]
                    kvi = kv_ctr[0] % 2
                    kv_ctr[0] += 1
                    piece_kvi[ip] = kvi
                    vb = vbuf[kvi]
                    tts = list(range(j0 // 4, (j0 + nj + 3) // 4))
                    for hh in range(2):
                        kb = kbuf[kvi][hh]
                        P.dma("sp", "ld_k%d_%d" % (kvi, hh),
                              lambda e, kb=kb, j0=j0, nj=nj, hd=2 * c + hh: e.dma_start(out=kb[0:KA, 0:nj * 128], in_=kt2_d[hd][:, j0 * 128:(j0 + nj) * 128]),
                              reads=[("kt2", 2 * c + hh, tt_) for tt_ in tts] + [("kt2c", tt_) for tt_ in tts] + [("kt2o", tt_) for tt_ in tts],
                              writes=[("kbuf", kvi, hh)])
                    P.dma("sp", "ld_v%d" % kvi,
                          lambda e, vb=vb, j0=j0, nj=nj, c=c: e.dma_start(out=vb[:, 0:nj, :], in_=v_d[:, j0:j0 + nj, 2 * c * 128:(2 * c + 2) * 128]),
                          reads=[("v_d", jj, c // 2) for jj in range(j0, j0 + nj)], writes=[("vbuf", kvi)])

                WIN.prefetch(ws["fox_wqk"][0], 1024, ("ws", "fox_wqk", 0))
                WIN.prefetch(ws["fox_wqk"][1], 1024, ("ws", "fox_wqk", 1))
                qs_of = {0: q_make(0)}
                qpend = None
                pending = []
                LA = 2

                def emit_S(ip, hh, jj, jblk, lo):
                    c = pieces[ip][0]
                    kvi = piece_kvi[ip]
                    si = s_ctr[0] % 3
                    s_ctr[0] += 1
                    sbank = Sb[si]
                    stok = ("S", si)
                    kb = kbuf[kvi][hh]
                    qt, qtoks = qs_of[c][hh]
                    dg_ = jblk >= 4 * t
                    P.pe(lambda e, jj=jj, lo=lo, sbank=sbank, kb=kb, qt=qt, dg_=dg_:
                         e.matmul(sbank[:, lo:TT], lhsT=kb[0:KA, jj * 128:(jj + 1) * 128], rhs=qt[0:KA, lo:TT], start=True, stop=not dg_,
                                  skip_group_check=True),
                         reads=[("kbuf", kvi, hh)] + qtoks, writes=[stok])
                    if dg_:
                        P.pe(lambda e, lo=lo, sbank=sbank: e.matmul(sbank[:, lo:lo + 128], lhsT=ident_bf, rhs=negm_bf, start=False, stop=True,
                                                                   skip_group_check=True),
                             reads=["cb"], writes=[stok])
                    P.act(lambda e, sbank=sbank, si=si, lo=lo: e.activation(out=pT[si][:, lo:TT], in_=sbank[:, lo:TT], func=AF.Exp),
                          reads=[stok], writes=[("pT", si)])
                    pending.append((c, kvi, hh, jj, jblk, lo, si, ip))

                def emit_norm(c):
                    P.act(lambda e: e.activation(out=rc[0:64, :], in_=po[0][64:128, :], func=AF.Ln), reads=[("po", 0)], writes=["rc0"])
                    P.act(lambda e: e.activation(out=rc[0:64, :], in_=rc[0:64, :], func=AF.Exp, scale=-1.0), reads=["rc0"], writes=["rc0"])
                    P.dve(lambda e, c=c: e.tensor_tensor(out=ot[0:64, c, :], in0=po[0][0:64, :], in1=rc[0:64, :], op=ALU.mult),
                          reads=[("po", 0), "rc0"], writes=[("ot", c)])
                    P.act(lambda e: e.activation(out=rc[64:128, :], in_=po[1][0:64, :], func=AF.Ln), reads=[("po", 1)], writes=["rc1"])
                    P.act(lambda e: e.activation(out=rc[64:128, :], in_=rc[64:128, :], func=AF.Exp, scale=-1.0), reads=["rc1"], writes=["rc1"])
                    P.dve(lambda e, c=c: e.tensor_tensor(out=ot[64:128, c, :], in0=po[1][64:128, :], in1=rc[64:128, :], op=ALU.mult),
                          reads=[("po", 1), "rc1", ("ot", c)], writes=[("ot", c)])

                def emit_PV():
                    c, kvi, hh, jj, jblk, lo, si, _ip = pending.pop(0)
                    vb = vbuf[kvi]
                    P.pe(lambda e, jj=jj, hh=hh, si=si, lo=lo, vb=vb, st=(jblk == 0), last=(jblk == nblk - 1):
                         e.matmul(po[hh][:, lo:TT], lhsT=vb[:, jj, hh * 128:(hh + 1) * 128], rhs=pT[si][:, lo:TT],
                                  start=st, stop=last, skip_group_check=True),
                         reads=[("vbuf", kvi), ("pT", si)], writes=[("po", hh)])
                    if hh == 1 and jblk == nblk - 1:
                        emit_norm(c)

                load_piece(0)
                if len(pieces) > 1:
                    load_piece(1)
                for ip, (c, j0, nj) in enumerate(pieces):
                    if j0 == 0 and c + 1 < KC:
                        qpend = [c + 1, q_make_a(c + 1), 0]
                    nxt_loaded = (ip == 0) or (ip + 1 >= len(pieces))
                    for hh in range(2):
                        for jj in range(nj):
                            jblk = j0 + jj
                            lo = 0 if jblk < 4 * t else 128 * (jblk - 4 * t)
                            emit_S(ip, hh, jj, jblk, lo)
                            if len(pending) > LA:
                                emit_PV()
                            if qpend is not None:
                                qpend[2] += 1
                                if qpend[2] >= 4:
                                    qs_of[qpend[0]] = q_make_b(qpend[0], qpend[1])
                                    qpend = None
                            if not nxt_loaded and all(pe_[7] >= ip for pe_ in pending):
                                load_piece(ip + 1)
                                nxt_loaded = True
                    assert nxt_loaded
                while pending:
                    emit_PV()
                u2, b2 = out_proj(t, "fox_wout", 0, ot, "ot", 1.0)
                run_units(WOUT, u2, b2, depth=2)

        if plan[0][1] in ("f0", "f1"):
            ls0 = plan[0][0] * 2 + (0 if plan[0][1] == "f0" else 1)
            for (a_, b_) in ((0, 2), (2, 6), (6, 14), (14, FC)):
                cast("ffn_win", ls0 * FC + a_, ls0 * FC + b_)
            cast("ffn_wout", ls0 * KC * 2, ls0 * KC * 2 + 4)
            cast("ffn_wout", ls0 * KC * 2 + 4, (ls0 + 1) * KC * 2)
            cast_done.add(("ffn_win", ls0 * FC, (ls0 + 1) * FC))
            cast_done.add(("ffn_wout", ls0 * KC * 2, (ls0 + 1) * KC * 2))
        casts_for(0)
        defer0 = plan[0][1] in ("f0", "f1") and NT >= 4
        if not defer0:
            casts_for(1)
        for pi_, (l, sub) in enumerate(plan):
            tile_hook[0] = None
            if pi_ == 0 and defer0:
                def _hook(t_):
                    if t_ == 0:
                        cast_after[0] = [("X", KC - 1, t_)]
                        casts_for(1)
                        cast_after[0] = ()
                tile_hook[0] = _hook
            if sub == "mix":
                for d_ in ((1, 2, 3, 4) if l % 3 == 1 else (1, 2, 3)):
                    casts_for(pi_ + d_)
            elif not any(s_ == "mix" for (_, s_) in plan):
                casts_for(pi_ + 1)
            with contextlib.ExitStack() as ph:
                if sub == "f0":
                    ffn(l, 0, ph)
                elif sub == "f1":
                    ffn(l, 1, ph)
                else:
                    kind = l % 3
                    if kind == 0:
                        rglru(l, ph)
                    elif kind == 1:
                        (fox2 if USE_FOX2 else fox)(l, ph)
                    else:
                        convmod(l, ph)
                P.barrier()

        fin = []
        if DEBUG:
            dbg_d = nc.dram_tensor("dbg", [len(DEBUG), 128, 512], F32, kind="ExternalOutput").ap()
            for i, (nm, apf, w) in enumerate(DEBUG):
                fin.append(P.dma("pool", "dbg%d" % i, lambda e, i=i, apf=apf, w=w: e.dma_start(out=dbg_d[i][:, 0:w], in_=apf()), writes=[("dbg", i)]))
        for k in range(KC):
            for t in range(NT):
                fin.append(P.dma("sp", "st_x%d_%d" % (k, t % 2),
                                 lambda e, k=k, t=t: e.dma_start(out=out_d[k * 128:(k + 1) * 128, t * TT:(t + 1) * TT],
                                                                  in_=X[:, k, t * TT:(t + 1) * TT]),
                                 reads=[("X", k, t)], writes=[("out", k, t)]))
        P.emit(final_wait_ops=fin)
    return nc, P


FULL_PLAN = [(l, s) for l in range(DEPTH) for s in ("f0", "mix", "f1")]


def prepare_inputs(inp):
    pv = pv_layout(inp)
    w = weight_layouts(inp)
    shared = dict(w)
    shared["pv"] = pv.pack()
    shared.update(const_inputs())
    return shared, pv.idx, pv.n, {k: v.shape for k, v in w.items()}


def kernel(**inputs):
    inp = {k: np.asarray(v) for k, v in inputs.items()}
    x = inp["x"]
    B, S, _ = x.shape
    shared, pvidx, npv, wshapes = prepare_inputs(inp)
    nc, _ = build_program(S, FULL_PLAN, wshapes, npv, pvidx)
    in_maps = []
    for b in range(B):
        m = dict(shared)
        m["xT"] = np.ascontiguousarray(x[b].T)
        in_maps.append(m)
    res = run_bass_kernel_spmd(nc, in_maps, core_ids=list(range(B)))
    out = np.stack([np.ascontiguousarray(res.results[b]["outT"].T) for b in range(B)], axis=0)
    return out.astype(np.float32)
```
